# Optimizing a Trainium2 kernel written in Bass

```python
import math
import jax, jax.numpy as jnp
from jax import lax
import numpy as np

D_MODEL = 1024
BATCH = 4
SEQ = 8192
DEPTH = 1

DA_HEADS = 8
DA_HEAD_DIM = 64
DA_V_DIM = 2 * DA_HEAD_DIM
DA_QK_WIDTH = DA_HEADS * 2 * DA_HEAD_DIM
DA_V_WIDTH = DA_HEADS * DA_V_DIM
ROPE_THETA = 500000.0
ROT_DIM = DA_HEAD_DIM // 4
Q_BLOCK = 128
SG_GROUPS = 8
SG_CHUNK = 128
SG_WIDTH = 1024
SG_GROUP_DIM = SG_WIDTH // SG_GROUPS
MEM_LEN = 256
XA_HEADS = 4
XA_HEAD_DIM = 256
XA_WIDTH = XA_HEADS * XA_HEAD_DIM
N_BRANCHES = 3
D_FF = int(math.ceil(8 * D_MODEL / 3 / 256)) * 256
ALPHA = (2 * DEPTH) ** 0.25
BETA = (8 * DEPTH) ** -0.25
LN_EPS = 1e-5
RMS_EPS = 1e-5

SPLIT_SIZES = (DA_QK_WIDTH, DA_QK_WIDTH, DA_V_WIDTH, SG_WIDTH, SG_WIDTH, XA_WIDTH, N_BRANCHES * D_MODEL)
SPLIT_POINTS = [int(i) for i in np.cumsum(SPLIT_SIZES)[:-1]]
IN_WIDTH = int(sum(SPLIT_SIZES))

kernel_name = "hybrid_diffattn_sgu_memxattn_deepnorm"


def layer_norm(x, g, b):
    xf = x.astype(jnp.float32)
    mu = jnp.mean(xf, axis=-1, keepdims=True)
    var = jnp.mean(jnp.square(xf - mu), axis=-1, keepdims=True)
    y = (xf - mu) * lax.rsqrt(var + LN_EPS) * g.astype(jnp.float32) + b.astype(jnp.float32)
    return y.astype(x.dtype)


def rms_norm(x, g):
    xf = x.astype(jnp.float32)
    y = xf * lax.rsqrt(jnp.mean(jnp.square(xf), axis=-1, keepdims=True) + RMS_EPS) * g.astype(jnp.float32)
    return y.astype(x.dtype)


def apply_partial_rope(t, cos, sin):
    half = ROT_DIM // 2
    t1 = t[..., :half].astype(jnp.float32)
    t2 = t[..., half:ROT_DIM].astype(jnp.float32)
    rot = jnp.concatenate([t1 * cos - t2 * sin, t2 * cos + t1 * sin], axis=-1).astype(t.dtype)
    return jnp.concatenate([rot, t[..., ROT_DIM:]], axis=-1)


def diff_attention(q, k, v, lam):
    B, H, _, S, dh = q.shape
    dv = v.shape[-1]
    nblk = S // Q_BLOCK
    scale = dh ** -0.5
    qb = q.reshape(B, H, 2, nblk, Q_BLOCK, dh).transpose(3, 0, 1, 2, 4, 5)
    kpos = jnp.arange(S)

    def one_block(args):
        qblk, start = args
        s = jnp.einsum('bhmqd,bhmkd->bhmqk', qblk, k).astype(jnp.float32) * scale
        qpos = start + jnp.arange(Q_BLOCK)
        causal = kpos[None, :] <= qpos[:, None]
        p = jax.nn.softmax(jnp.where(causal, s, -jnp.inf), axis=-1)
        w = p[:, :, 0] - lam * p[:, :, 1]
        return jnp.einsum('bhqk,bhkd->bhqd', w.astype(v.dtype), v)

    starts = jnp.arange(nblk) * Q_BLOCK
    out = lax.map(one_block, (qb, starts))
    return out.transpose(1, 2, 0, 3, 4).reshape(B, H, S, dv)


def spatial_gating(u, v, w_s, b_s, norm_g, norm_b):
    B, S, _ = u.shape
    nc = S // SG_CHUNK
    v = layer_norm(v, norm_g, norm_b)
    vc = v.reshape(B, nc, SG_CHUNK, SG_GROUPS, SG_GROUP_DIM)
    causal = jnp.tril(jnp.ones((SG_CHUNK, SG_CHUNK), dtype=bool))
    ws = jnp.where(causal[None], w_s, jnp.zeros_like(w_s))
    s = jnp.einsum('gts,bcsgd->bctgd', ws, vc) + b_s.T[None, None, :, :, None]
    return u * s.reshape(B, S, SG_WIDTH)


def memory_cross_attention(xq, mem_kv):
    B, S, _ = xq.shape
    M = mem_kv.shape[1]
    q = xq.reshape(B, S, XA_HEADS, XA_HEAD_DIM)
    k, v = jnp.split(mem_kv, 2, axis=-1)
    k = k.reshape(B, M, XA_HEADS, XA_HEAD_DIM)
    v = v.reshape(B, M, XA_HEADS, XA_HEAD_DIM)
    s = jnp.einsum('bshd,bmhd->bhsm', q, k).astype(jnp.float32) * (XA_HEAD_DIM ** -0.5)
    p = jax.nn.softmax(s, axis=-1).astype(v.dtype)
    return jnp.einsum('bhsm,bmhd->bshd', p, v).reshape(B, S, XA_WIDTH)


def setup_inputs(seed: int = 0) -> dict:
    key = jax.random.key(seed)
    ks = jax.random.split(key, 32)
    f32 = jnp.float32
    L, D = DEPTH, D_MODEL

    def nrm(k, shape, scale):
        return jax.random.normal(k, shape, f32) * scale

    x = nrm(ks[0], (BATCH, SEQ, D), 1.0)
    mem = nrm(ks[1], (BATCH, MEM_LEN, D), 1.0)
    offsets = jax.random.randint(ks[2], (BATCH, 1), 0, 1024, dtype=jnp.int32)
    positions = (jnp.arange(SEQ, dtype=jnp.int32)[None, :] + offsets).astype(jnp.int32)
    return {
        "x": x,
        "mem": mem,
        "positions": positions,
        "w_in": nrm(ks[3], (L, D, IN_WIDTH), D ** -0.5),
        "lambda_q1": nrm(ks[4], (L, DA_HEAD_DIM), 0.1),
        "lambda_k1": nrm(ks[5], (L, DA_HEAD_DIM), 0.1),
        "lambda_q2": nrm(ks[6], (L, DA_HEAD_DIM), 0.1),
        "lambda_k2": nrm(ks[7], (L, DA_HEAD_DIM), 0.1),
        "da_subln_g": 1.0 + nrm(ks[8], (L, DA_V_DIM), 0.02),
        "sg_norm_g": 1.0 + nrm(ks[9], (L, SG_WIDTH), 0.02),
        "sg_norm_b": nrm(ks[10], (L, SG_WIDTH), 0.02),
        "sg_w_s": nrm(ks[11], (L, SG_GROUPS, SG_CHUNK, SG_CHUNK), SG_CHUNK ** -0.5),
        "sg_b_s": 1.0 + nrm(ks[12], (L, SG_GROUPS, SG_CHUNK), 0.1),
        "w_mem_kv": nrm(ks[13], (L, D, 2 * XA_WIDTH), D ** -0.5),
        "w_br_attn": nrm(ks[14], (L, DA_V_WIDTH, D), DA_V_WIDTH ** -0.5),
        "w_br_sg": nrm(ks[15], (L, SG_WIDTH, D), SG_WIDTH ** -0.5),
        "w_br_mem": nrm(ks[16], (L, XA_WIDTH, D), XA_WIDTH ** -0.5),
        "w_out": nrm(ks[17], (L, D, D), BETA * D ** -0.5),
        "ln1_g": 1.0 + nrm(ks[18], (L, D), 0.02),
        "ln1_b": nrm(ks[19], (L, D), 0.02),
        "w_ffn_in": nrm(ks[20], (L, D, 2 * D_FF), D ** -0.5),
        "w_ffn_out": nrm(ks[21], (L, D_FF, D), BETA * D_FF ** -0.5),
        "ln2_g": 1.0 + nrm(ks[22], (L, D), 0.02),
        "ln2_b": nrm(ks[23], (L, D), 0.02),
    }


def reference(x, mem, positions, w_in, lambda_q1, lambda_k1, lambda_q2, lambda_k2, da_subln_g,
              sg_norm_g, sg_norm_b, sg_w_s, sg_b_s, w_mem_kv, w_br_attn, w_br_sg, w_br_mem, w_out,
              ln1_g, ln1_b, w_ffn_in, w_ffn_out, ln2_g, ln2_b):
    B, S, D = x.shape
    f32 = jnp.float32
    half = ROT_DIM // 2
    inv_freq = ROPE_THETA ** (-jnp.arange(half, dtype=f32) * 2.0 / ROT_DIM)
    ang = positions.astype(f32)[..., None] * inv_freq
    cos = jnp.cos(ang)[:, :, None, None, :]
    sin = jnp.sin(ang)[:, :, None, None, :]

    for l in range(DEPTH):
        lambda_init = 0.8 - 0.6 * math.exp(-0.3 * l)
        z = x @ w_in[l]
        q, k, v, su, sv, xq, gl = jnp.split(z, SPLIT_POINTS, axis=-1)

        q = apply_partial_rope(q.reshape(B, S, DA_HEADS, 2, DA_HEAD_DIM), cos, sin)
        k = apply_partial_rope(k.reshape(B, S, DA_HEADS, 2, DA_HEAD_DIM), cos, sin)
        q = q.transpose(0, 2, 3, 1, 4)
        k = k.transpose(0, 2, 3, 1, 4)
        va = v.reshape(B, S, DA_HEADS, DA_V_DIM).transpose(0, 2, 1, 3)
        lam = (jnp.exp(jnp.sum(lambda_q1[l].astype(f32) * lambda_k1[l].astype(f32)))
               - jnp.exp(jnp.sum(lambda_q2[l].astype(f32) * lambda_k2[l].astype(f32)))
               + lambda_init)
        o_da = diff_attention(q, k, va, lam)
        o_da = rms_norm(o_da, da_subln_g[l]) * (1.0 - lambda_init)
        o_da = o_da.transpose(0, 2, 1, 3).reshape(B, S, DA_V_WIDTH)

        o_sg = spatial_gating(jax.nn.gelu(su), jax.nn.gelu(sv), sg_w_s[l], sg_b_s[l],
                              sg_norm_g[l], sg_norm_b[l])

        o_xa = memory_cross_attention(xq, mem @ w_mem_kv[l])

        g = jax.nn.sigmoid(gl).reshape(B, S, N_BRANCHES, D)
        merged = (g[:, :, 0] * (o_da @ w_br_attn[l])
                  + g[:, :, 1] * (o_sg @ w_br_sg[l])
                  + g[:, :, 2] * (o_xa @ w_br_mem[l]))
        x = layer_norm(ALPHA * x + merged @ w_out[l], ln1_g[l], ln1_b[l])

        a, b = jnp.split(x @ w_ffn_in[l], 2, axis=-1)
        x = layer_norm(ALPHA * x + (jax.nn.silu(a) * b) @ w_ffn_out[l], ln2_g[l], ln2_b[l])
    return x
```

```python
import math
from contextlib import ExitStack
import numpy as np
import concourse.bass as bass
import concourse.mybir as mybir
from concourse.bass_utils import run_bass_kernel_spmd

F32 = mybir.dt.float32
BF16 = mybir.dt.bfloat16
I32 = mybir.dt.int32
AF = mybir.ActivationFunctionType
ALU = mybir.AluOpType
AX = mybir.AxisListType

D = 1024
SEQ = 8192
NBLK = 64
NSLOT = 32
NQ = NSLOT * 128
DFF = 2816
NF = DFF // 128
ALPHA = 2.0 ** 0.25
LN_EPS = 1e-5
RMS_EPS = 1e-5
LAMBDA_INIT = 0.8 - 0.6 * math.exp(0.0)
TWO_PI = 2.0 * math.pi
GELU_C = 0.7978845608028654
NG = 103
CONST_G = 8
DBG = {}
IN_SHAPES = {}


def q_blocks(half):
    out = []
    for s in range(NSLOT):
        g = s // 2
        if half == 0:
            out.append(4 * g + (0 if s % 2 == 0 else 3))
        else:
            out.append(4 * g + (1 if s % 2 == 0 else 2))
    return out


class Buf:
    def __init__(self, ap, res):
        self.ap = ap
        self.res = tuple(res)

    def __getitem__(self, k):
        return self.ap[k]


class Prog:
    ENGS = ("pe", "act", "dve", "pool", "sp")
    CH = 6000

    def __init__(self, nc):
        self.nc = nc
        self.ops = []

    @staticmethod
    def _res(lst):
        out = []
        for x in lst:
            if isinstance(x, Buf):
                out.extend(x.res)
            elif isinstance(x, str):
                out.append(x)
            else:
                out.extend(x)
        return tuple(out)

    def op(self, eng, fn, reads=(), writes=()):
        self.ops.append(dict(eng=eng, fn=fn, reads=self._res(reads), writes=self._res(writes), dma=False, sig=None))

    def dma(self, eng, fns, reads=(), writes=(), key=None):
        if not isinstance(fns, (list, tuple)):
            fns = [fns]
        key = None
        for x in list(writes) + list(reads):
            if isinstance(x, Buf):
                key = x.res[0] + ("s" if eng == "pool" else "h")
                break
        assert key is not None
        self.ops.append(dict(eng=eng, fn=list(fns), reads=self._res(reads), writes=self._res(writes), dma=True,
                             key=key, sig=None))

    def finalize(self, stack):
        nc = self.nc
        ops = self.ops
        lastw = {}
        readers = {}
        deps = [None] * len(ops)
        for n, o in enumerate(ops):
            d = {}
            for r in o["reads"]:
                for w in lastw.get(r, ()):
                    d.setdefault(w, "raw")
                if r[0] == "P" and not o["dma"]:
                    for tag, rd in readers.get(r, {}).items():
                        if tag != o["eng"] and rd not in d:
                            d[rd] = "rar"
            for w in o["writes"]:
                comm = w.startswith("D:")
                if not comm:
                    for pw in lastw.get(w, ()):
                        d.setdefault(pw, "waw")
                for rd in readers.get(w, {}).values():
                    if rd not in d:
                        d[rd] = "war"
            dd = []
            for j, kind in d.items():
                if j == n:
                    continue
                oj = ops[j]
                if (not oj["dma"]) and (not o["dma"]) and oj["eng"] == o["eng"]:
                    if o["eng"] == "pe":
                        continue
                dd.append(j)
            deps[n] = dd
            for r in o["reads"]:
                tag = ("dma", o["key"]) if o["dma"] else o["eng"]
                readers.setdefault(r, {})[tag] = n
            for w in o["writes"]:
                if w.startswith("D:"):
                    lastw.setdefault(w, []).append(n)
                else:
                    lastw[w] = [n]
                readers[w] = {}
        for n in range(len(ops)):
            for j in deps[n]:
                ops[j]["need"] = True
        cnt = {e: 0 for e in self.ENGS}
        dcnt = {}
        for o in ops:
            if o["dma"]:
                k = o["key"]
                dcnt[k] = dcnt.get(k, 0) + 16 * len(o["fn"])
                o["sig"] = dcnt[k]
            elif o.get("need"):
                o["sig"] = (cnt[o["eng"]] // self.CH, cnt[o["eng"]] % self.CH + 1)
                cnt[o["eng"]] += 1
        self.esem = {(e, c): stack.enter_context(nc.semaphore("s_%s%d" % (e, c)))
                     for e in self.ENGS for c in range(cnt[e] // self.CH + 1)}
        self.dsem = {k: stack.enter_context(nc.semaphore("d_%d" % i)) for i, k in enumerate(dcnt.keys())}
        known = {e: {} for e in self.ENGS}
        per_eng = {e: [] for e in self.ENGS}
        snap = {}
        for n, o in enumerate(ops):
            E = o["eng"]
            kn = known[E]
            waits = []
            for j in sorted(deps[n], reverse=True):
                oj = ops[j]
                if oj["dma"]:
                    sk, v = ("d", oj["key"]), (0, oj["sig"])
                else:
                    sk, v = ("e", oj["eng"]), oj["sig"]
                if kn.get(sk, (0, 0)) >= v:
                    continue
                kn[sk] = v
                waits.append((self.dsem[sk[1]] if sk[0] == "d" else self.esem[(sk[1], v[0])], v[1]))
                for k2, v2 in snap.get(j, {}).items():
                    if v2 > kn.get(k2, (0, 0)):
                        kn[k2] = v2
            if o["sig"] is not None:
                snap[n] = dict(kn)
            per_eng[E].append((waits, o))
        self.per_eng = per_eng
        self.counts = (cnt, dcnt)

    def emit(self, block):
        engmap = {"pe": block.tensor, "act": block.scalar, "dve": block.vector, "pool": block.gpsimd,
                  "sp": block.sync}
        for E in self.ENGS:
            items = self.per_eng[E]
            esem = self.esem
            dsem = self.dsem

            def body(eng, items=items, esem=esem, E=E):
                for waits, o in items:
                    for sem, v in waits:
                        eng.wait_ge(sem, v)
                    if o["dma"]:
                        for f in o["fn"]:
                            f(eng).then_inc(dsem[o["key"]], 16)
                    else:
                        ins = o["fn"](eng)
                        if o["sig"] is not None:
                            ins.then_inc(esem[(E, o["sig"][0])], 1)
            engmap[E](body)


def build_program(debug=False, upto=6):
    nc = bass.Bass("TRN2", target_bir_lowering=False)

    IN_PHASE = dict(xb=1, xq=2, mem=3, w_in=1, w_mem_kv=3, w_br_attn=5, w_br_sg=2, w_br_mem=3, w_out=5,
                    w_ffn_in=6, w_ffn_out=6, sg_norm_g=2, sg_norm_b=2, sg_w_s=2, sg_b_s=2, c_tril=2,
                    ln1_g=5, ln1_b=5, ln2_g=6, ln2_b=6)
    IN_SHAPES.clear()

    def din(name, shape, dt=F32):
        if upto < IN_PHASE.get(name, 0):
            shape = [1] * (len(shape) - 1) + [2]
        IN_SHAPES[name] = tuple(shape)
        return nc.dram_tensor(name, list(shape), dt, kind="ExternalInput").ap()

    def dscr(name, shape, dt):
        kind = "ExternalOutput" if debug else "Internal"
        return nc.dram_tensor(name, list(shape), dt, kind=kind).ap()

    xb_d = din("xb", [SEQ, D])
    xq_d = din("xq", [NQ, D])
    posb_d = din("posb", [128, NBLK], I32)
    posq_d = din("posq", [128, NSLOT], I32)
    mem_d = din("mem", [256, D])
    w_in_d = din("w_in", [D, 9 * D])
    w_mkv_d = din("w_mem_kv", [D, 2 * D])
    w_ba_d = din("w_br_attn", [D, D])
    w_bs_d = din("w_br_sg", [D, D])
    w_bm_d = din("w_br_mem", [D, D])
    w_out_d = din("w_out", [D, D])
    w_fi_d = din("w_ffn_in", [D, 2 * DFF])
    w_fo_d = din("w_ffn_out", [DFF, D])
    lq1_d = din("lambda_q1", [1, 64]); lk1_d = din("lambda_k1", [1, 64])
    lq2_d = din("lambda_q2", [1, 64]); lk2_d = din("lambda_k2", [1, 64])
    dag_d = din("da_subln_g", [1, 128])
    sgg_d = din("sg_norm_g", [1, D]); sgb_d = din("sg_norm_b", [1, D])
    sws_d = din("sg_w_s", [8, 128, 128]); sbs_d = din("sg_b_s", [1, 8 * 128])
    ln1g_d = din("ln1_g", [1, D]); ln1b_d = din("ln1_b", [1, D])
    ln2g_d = din("ln2_g", [1, D]); ln2b_d = din("ln2_b", [1, D])
    ident_d = din("c_ident", [128, 128])
    tril_d = din("c_tril", [128, 128])
    maskp_d = din("c_maskp", [128, 2 * 2 * 128])
    invf_d = din("c_invf", [1, 8])
    out_d = nc.dram_tensor("out", [NQ, D], F32, kind="ExternalOutput").ap()

    kT_d = dscr("kT_s", [8, 128, SEQ], BF16)
    v_d = dscr("v_s", [SEQ, D], BF16)
    qT_d = dscr("qT_s", [8, 128, NQ], BF16)
    m1_d = dscr("m1_s", [8, 128, NQ], BF16)
    m12_d = dscr("m12_s", [8, 128, NQ], BF16)
    x1_d = dscr("x1_s", [NQ, D], F32)
    oda_d = dscr("oda_s", [8, 128, NQ], BF16) if debug else None

    st = ExitStack()
    region = st.enter_context(nc.sbuf_tensor("region", [128, NG * 1024], BF16))
    psum = st.enter_context(nc.psum_tensor("psum", [128, 4096], F32))
    P = Prog(nc)

    def rv(off_b, shape, dt):
        esz = 4 if dt in (F32, I32) else 2
        n = int(np.prod(shape[1:]))
        nb = n * esz
        assert off_b % 4 == 0 and off_b + nb <= NG * 2048, (off_b, shape)
        a = region[0:shape[0], off_b // 2: (off_b + nb) // 2]
        if dt != BF16:
            a = a.bitcast(dt)
        if len(shape) == 3:
            a = a.rearrange("p (a b) -> p a b", a=shape[1])
        elif len(shape) == 4:
            a = a.rearrange("p (a b c) -> p a b c", a=shape[1], b=shape[2])
        res = ["G%d" % g for g in range(off_b // 2048, (off_b + nb - 1) // 2048 + 1)]
        return Buf(a, res)

    class Alloc:
        def __init__(self, g0, g1=NG):
            self.off = g0 * 2048
            self.end = g1 * 2048

        def take(self, shape, dt, align=2048):
            self.off = (self.off + align - 1) // align * align
            b = rv(self.off, shape, dt)
            esz = 4 if dt in (F32, I32) else 2
            self.off += int(np.prod(shape[1:])) * esz
            assert self.off <= self.end, ("region overflow", self.off, self.end)
            return b

    def pbank(b, nb=1):
        return ["P%d" % i for i in range(b, b + nb)]

    def ps_f32(b, nb=1):
        return Buf(psum[:, b * 512:(b + nb) * 512], pbank(b, nb))

    def ps_bf(b):
        return Buf(psum[:, b * 512:(b + 1) * 512].bitcast(BF16).rearrange("p (a b) -> p a b", a=8), pbank(b))

    def MM(O, o_ap, L, l_ap, R, r_ap, start, stop):
        P.op("pe", lambda e: e.matmul(o_ap, lhsT=l_ap, rhs=r_ap, start=start, stop=stop), reads=[L, R], writes=[O])

    def TR(O, o_ap, I, i_ap):
        P.op("pe", lambda e: e.transpose(out=o_ap, in_=i_ap, identity=identb.ap), reads=[I, identb], writes=[O])

    def ACT(O, o_ap, I, i_ap, func, scale=1.0, bias=0.0, extra_reads=()):
        P.op("act", lambda e: e.activation(out=o_ap, in_=i_ap, func=func, bias=bias, scale=scale),
             reads=[I] + list(extra_reads), writes=[O])

    def COPY(eng, O, o_ap, I, i_ap):
        if eng == "act":
            P.op("act", lambda e: e.copy(out=o_ap, in_=i_ap), reads=[I], writes=[O])
        else:
            P.op(eng, lambda e: e.tensor_copy(out=o_ap, in_=i_ap), reads=[I], writes=[O])

    def TT(eng, O, o_ap, A, a_ap, B, b_ap, op):
        P.op(eng, lambda e: e.tensor_tensor(out=o_ap, in0=a_ap, in1=b_ap, op=op), reads=[A, B], writes=[O])

    def TS(eng, O, o_ap, A, a_ap, s1, s2, op0, op1=None, extra_reads=()):
        if op1 is None:
            P.op(eng, lambda e: e.tensor_scalar(out=o_ap, in0=a_ap, scalar1=s1, scalar2=None, op0=op0),
                 reads=[A] + list(extra_reads), writes=[O])
        else:
            P.op(eng, lambda e: e.tensor_scalar(out=o_ap, in0=a_ap, scalar1=s1, scalar2=s2, op0=op0, op1=op1),
                 reads=[A] + list(extra_reads), writes=[O])

    def STT(eng, O, o_ap, A, a_ap, scalar, B, b_ap, op0, op1, extra_reads=()):
        P.op(eng, lambda e: e.scalar_tensor_tensor(out=o_ap, in0=a_ap, scalar=scalar, in1=b_ap, op0=op0, op1=op1),
             reads=[A, B] + list(extra_reads), writes=[O])

    def load_w(Wb, src, col0, ncols, nchunks=8, row0=0, key=None, piece=1024):
        fns = []
        for c in range(nchunks):
            for q0 in range(0, ncols, piece):
                q1 = min(ncols, q0 + piece)
                fns.append(lambda e, c=c, q0=q0, q1=q1: e.dma_start(
                    out=Wb.ap[:, c, q0:q1], in_=src[row0 + c * 128: row0 + (c + 1) * 128, col0 + q0: col0 + q1]))
        P.dma("pool", fns, writes=[Wb], key=key)

    def bcast_load(dst, src_row, key):
        P.dma("sp", lambda e: e.dma_start(out=dst.ap, in_=src_row.partition_broadcast(128)), writes=[dst], key=key)

    ca = Alloc(0, CONST_G)
    identb = ca.take([128, 128], BF16, align=4)
    maskp = ca.take([128, 2, 2, 128], BF16, align=4)
    gcol = ca.take([128, 1], F32, align=4)
    onesb = ca.take([128, 128], BF16, align=4)
    neglam = ca.take([128, 1], F32, align=4)
    epsb = ca.take([128, 1], F32, align=4)
    ca.off = 2 * 2048
    trig_b = ca.take([128, NBLK, 32], F32)
    trig_q = ca.take([128, NSLOT, 32], F32)
    DYN = CONST_G

    sa = Alloc(NG - 12)
    posb_i = sa.take([128, NBLK], I32, align=4)
    posq_i = sa.take([128, NSLOT], I32, align=4)
    invf = sa.take([128, 8], F32, align=4)
    lamv = sa.take([128, 4, 64], F32, align=4)
    lprod = sa.take([128, 2, 64], F32, align=4)
    ls = sa.take([128, 4], F32, align=4)
    posf = sa.take([128, NBLK], F32, align=4)
    ang = sa.take([128, NBLK, 8], F32)
    tr1 = sa.take([128, NBLK, 8], F32)
    tr2 = sa.take([128, NBLK, 8], F32)
    tri = sa.take([128, NBLK, 8], I32)

    P.dma("pool", [lambda e: e.dma_start(out=identb.ap, in_=ident_d[:, :]),
                   lambda e: e.dma_start(out=maskp.ap.rearrange("p a b c -> p (a b c)"), in_=maskp_d[:, :])],
          writes=[identb, maskp], key="cst_cast")
    P.dma("sp", [lambda e: e.dma_start(out=posb_i.ap, in_=posb_d[:, :]),
                 lambda e: e.dma_start(out=posq_i.ap, in_=posq_d[:, :]),
                 lambda e: e.dma_start(out=invf.ap, in_=invf_d[0:1, :].partition_broadcast(128)),
                 lambda e: e.dma_start(out=gcol.ap, in_=dag_d[0:1, :].rearrange("o d -> d o")),
                 lambda e: e.dma_start(out=lamv.ap[:, 0, :], in_=lq1_d[0:1, :].partition_broadcast(128)),
                 lambda e: e.dma_start(out=lamv.ap[:, 1, :], in_=lk1_d[0:1, :].partition_broadcast(128)),
                 lambda e: e.dma_start(out=lamv.ap[:, 2, :], in_=lq2_d[0:1, :].partition_broadcast(128)),
                 lambda e: e.dma_start(out=lamv.ap[:, 3, :], in_=lk2_d[0:1, :].partition_broadcast(128))],
          writes=[posb_i, posq_i, invf, gcol, lamv], key="cst_sp")
    P.op("pool", lambda e: e.memset(onesb.ap, 1.0), writes=[onesb])
    P.op("pool", lambda e: e.memset(epsb.ap, RMS_EPS), writes=[epsb])
    lv4 = lamv.ap.rearrange("p (a b) c -> p a b c", a=2)
    TT("dve", lprod, lprod.ap, lamv, lv4[:, :, 0, :], lamv, lv4[:, :, 1, :], ALU.mult)
    P.op("dve", lambda e: e.tensor_reduce(out=ls.ap[:, 0:2], in_=lprod.ap, axis=AX.X, op=ALU.add),
         reads=[lprod], writes=[ls])
    ACT(ls, ls.ap[:, 2:4], ls, ls.ap[:, 0:2], AF.Exp)
    TT("dve", neglam, neglam.ap, ls, ls.ap[:, 3:4], ls, ls.ap[:, 2:3], ALU.subtract)
    TS("dve", neglam, neglam.ap, neglam, neglam.ap, -LAMBDA_INIT, None, ALU.add)
    TS("dve", gcol, gcol.ap, gcol, gcol.ap, 1.0 - LAMBDA_INIT, None, ALU.mult)

    def trig_tables(pos_i, nb, table):
        pf = Buf(posf.ap[:, 0:nb], posf.res)
        A_ = Buf(ang.ap[:, 0:nb, :], ang.res)
        t1 = Buf(tr1.ap[:, 0:nb, :], tr1.res)
        t2 = Buf(tr2.ap[:, 0:nb, :], tr2.res)
        ti = Buf(tri.ap[:, 0:nb, :], tri.res)
        COPY("dve", pf, pf.ap, pos_i, pos_i.ap)
        TT("dve", A_, A_.ap, pf, pf.ap.unsqueeze(2).broadcast_to([128, nb, 8]),
           invf, invf.ap.unsqueeze(1).broadcast_to([128, nb, 8]), ALU.mult)

        def red_sin(dst_ap, shift):
            TS("dve", t1, t1.ap, A_, A_.ap, shift, None, ALU.add)
            TS("dve", t2, t2.ap, t1, t1.ap, 1.0 / TWO_PI, None, ALU.mult)
            COPY("dve", ti, ti.ap, t2, t2.ap)
            COPY("dve", t2, t2.ap, ti, ti.ap)
            STT("dve", t1, t1.ap, t2, t2.ap, -TWO_PI, t1, t1.ap, ALU.mult, ALU.add)
            TS("dve", t2, t2.ap, t1, t1.ap, math.pi, -TWO_PI, ALU.is_ge, ALU.mult)
            TT("dve", t1, t1.ap, t1, t1.ap, t2, t2.ap, ALU.add)
            TS("dve", t2, t2.ap, t1, t1.ap, -math.pi, TWO_PI, ALU.is_lt, ALU.mult)
            TT("dve", t1, t1.ap, t1, t1.ap, t2, t2.ap, ALU.add)
            TS("dve", t1, t1.ap, t1, t1.ap, 0.999999, None, ALU.mult)
            ACT(table, dst_ap, t1, t1.ap, AF.Sin)
        red_sin(table.ap[:, :, 24:32], 0.0)
        red_sin(table.ap[:, :, 0:8], math.pi / 2.0)
        COPY("dve", table, table.ap[:, :, 8:16], table, table.ap[:, :, 0:8])
        TS("dve", table, table.ap[:, :, 16:24], table, table.ap[:, :, 24:32], -1.0, None, ALU.mult)
    trig_tables(posb_i, NBLK, trig_b)
    trig_tables(posq_i, NSLOT, trig_q)

    def rope(Z, z_ap, OB, table, blk, RA, RB):
        kz = z_ap.rearrange("p (g d) -> p g d", g=16)
        ob3 = OB.ap.rearrange("p (g d) -> p g d", g=16)
        cs = table.ap[:, blk, 0:16].unsqueeze(1).broadcast_to([128, 16, 16])
        slo = table.ap[:, blk, 16:24].unsqueeze(1).broadcast_to([128, 16, 8])
        shi = table.ap[:, blk, 24:32].unsqueeze(1).broadcast_to([128, 16, 8])
        TT("dve", RA, RA.ap, Z, kz[:, :, 0:16], table, cs, ALU.mult)
        TT("dve", RB, RB.ap[:, :, 0:8], Z, kz[:, :, 8:16], table, slo, ALU.mult)
        TT("dve", RB, RB.ap[:, :, 8:16], Z, kz[:, :, 0:8], table, shi, ALU.mult)
        TT("dve", OB, ob3[:, :, 0:16], RA, RA.ap, RB, RB.ap, ALU.add)

    stat_ctr = [0]

    def layer_norm(R, r_ap, O, o_ap, g, b, STATS, eps=LN_EPS, eng2="pool"):
        k = stat_ctr[0] % 4
        stat_ctr[0] += 1
        sv_ = STATS.ap[:, k * 16:(k + 1) * 16]
        S = STATS
        P.op("dve", lambda e: e.bn_stats(out=sv_[:, 0:6], in_=r_ap[:, 0:512]), reads=[R], writes=[S])
        P.op("dve", lambda e: e.bn_stats(out=sv_[:, 6:12], in_=r_ap[:, 512:1024]), reads=[R], writes=[S])
        P.op("dve", lambda e: e.bn_aggr(out=sv_[:, 12:14], in_=sv_[:, 0:12]), reads=[S], writes=[S])
        ACT(S, sv_[:, 14:15], S, sv_[:, 13:14], AF.Sqrt, scale=1.0, bias=eps)
        P.op("dve", lambda e: e.reciprocal(out=sv_[:, 15:16], in_=sv_[:, 14:15]), reads=[S], writes=[S])
        STT("dve", S, sv_[:, 14:15], S, sv_[:, 12:13], -1.0, S, sv_[:, 15:16], ALU.mult, ALU.mult)
        P.op("act", lambda e: e.activation(out=r_ap, in_=r_ap, func=AF.Identity, bias=sv_[:, 14:15], scale=sv_[:, 15:16]),
             reads=[R, S], writes=[R])
        TT("pool", R, r_ap, R, r_ap, g, g.ap, ALU.mult)
        TT("pool", O, o_ap, R, r_ap, b, b.ap, ALU.add)

    def load_xT(src_rows_ap, XB, key, tpb, XT, xt_ap, evac="act"):
        P.dma("pool", lambda e: e.dma_start(out=XB.ap, in_=src_rows_ap), writes=[XB], key=key)
        tp = ps_bf(tpb)
        for c in range(8):
            TR(tp, tp.ap[:, c, :], XB, XB.ap[:, c * 128:(c + 1) * 128])
        COPY(evac, XT, xt_ap, tp, tp.ap)

    def x_issue(src_rows_ap, XB, extra_reads=()):
        P.dma("pool", lambda e: e.dma_start(out=XB.ap, in_=src_rows_ap), reads=list(extra_reads), writes=[XB])

    def x_trans(XB, tpb, XT, xt_ap, evac="act"):
        tp = ps_bf(tpb)
        for c in range(8):
            TR(tp, tp.ap[:, c, :], XB, XB.ap[:, c * 128:(c + 1) * 128])
        COPY(evac, XT, xt_ap, tp, tp.ap)

    def x_tile(T, src_d, XBs, X, extra_reads=()):
        if T == 0:
            for blk in range(4):
                x_issue(src_d[blk * 128:(blk + 1) * 128, :], XBs[blk], extra_reads)
        for blk in range(4):
            x_trans(XBs[blk], 0, X, X.ap[:, :, blk * 128:(blk + 1) * 128])
            if T + 1 < 8:
                r1 = (T + 1) * 512 + blk * 128
                x_issue(src_d[r1:r1 + 128, :], XBs[blk], extra_reads)

    fm_ctr = [0]

    def fm_proj(W, col0, X, x_ap, banks):
        b = banks[fm_ctr[0] % len(banks)]
        fm_ctr[0] += 1
        ps = ps_f32(b)
        n = x_ap.shape[-1]
        for c in range(8):
            MM(ps, ps.ap[:, 0:n], W, W.ap[:, c, col0:col0 + 128], X, x_ap[:, c, :], c == 0, c == 7)
        return ps

    A = Alloc(DYN)
    Wk = A.take([128, 8, 1024], BF16)
    Wv = A.take([128, 8, 1024], BF16)
    if upto >= 1:
        load_w(Wk, w_in_d, 1 * D, D, key="W0")
        load_w(Wv, w_in_d, 2 * D, D, key="W1")
    a_xb = [A.take([128, 1024], BF16) for _ in range(3)]
    a_xT = [A.take([128, 8, 128], BF16) for _ in range(2)]
    a_kb = [A.take([128, 1024], BF16) for _ in range(2)]
    a_vb = [A.take([128, 1024], BF16) for _ in range(2)]
    a_kT = [A.take([128, 8, 512], BF16) for _ in range(2)]
    a_RA = A.take([128, 16, 16], F32, align=4)
    a_RB = A.take([128, 16, 16], F32, align=4)

    def a1_A(t):
        load_xT(xb_d[t * 128:(t + 1) * 128, :], a_xb[t % 3], "xb%d" % (t % 3), t % 2, a_xT[t % 2], a_xT[t % 2].ap)

    def a1_B(t):
        X = a_xT[t % 2]
        kps = ps_f32(2, 2)
        vps = ps_f32(4, 2)
        for (O, W) in ((kps, Wk), (vps, Wv)):
            for c in range(8):
                for n in range(2):
                    MM(O, O.ap[:, n * 512:(n + 1) * 512], X, X.ap[:, c, :], W,
                       W.ap[:, c, n * 512:(n + 1) * 512], c == 0, c == 7)
        kb = a_kb[t % 2]
        COPY("act", kb, kb.ap, kps, kps.ap)
        if DBG.get('b', 9) >= 2:
            rope(kps, kps.ap, kb, trig_b, t, a_RA, a_RB)
        vb = a_vb[t % 2]
        if DBG.get('b', 9) >= 3:
            COPY("dve", vb, vb.ap, vps, vps.ap)
        if DBG.get('b', 9) >= 4:
            P.dma("sp", lambda e: e.dma_start(out=v_d[t * 128:(t + 1) * 128, :], in_=vb.ap), reads=[vb],
                  writes=["D:v"], key="vb%d" % (t % 2))

    def a1_C(t):
        kb = a_kb[t % 2]
        tk = ps_bf(6 + t % 2)
        for h in range(8):
            TR(tk, tk.ap[:, h, :], kb, kb.ap[:, h * 128:(h + 1) * 128])
        KT = a_kT[(t // 4) % 2]
        COPY("dve", KT, KT.ap[:, :, (t % 4) * 128:(t % 4 + 1) * 128], tk, tk.ap)
        if t % 4 == 3:
            t0 = (t - 3) * 128
            P.dma("sp", lambda e: e.dma_start(out=kT_d[:, :, t0:t0 + 512].rearrange("h p t -> p h t"), in_=KT.ap),
                  reads=[KT], writes=["D:kT"], key="kTt%d" % ((t // 4) % 2))

    Bw = Alloc(DYN + 40)
    Wq = Bw.take([128, 8, 1024], BF16)
    Wsu = Bw.take([128, 8, 1024], BF16)
    Wsv = Bw.take([128, 8, 1024], BF16)
    Wg1 = Bw.take([128, 8, 1024], BF16)
    Wbs = Bw.take([128, 8, 1024], BF16)
    if upto >= 2:
        load_w(Wq, w_in_d, 0, D, key="W2")
        load_w(Wsv, w_in_d, 4 * D, D, key="W4")
        load_w(Wsu, w_in_d, 3 * D, D, key="W3")

    NB_A1 = NBLK if upto >= 1 else 0
    for t in range(NB_A1 + 2):
        if t < NB_A1:
            a1_A(t)
        if 1 <= t <= NB_A1 and DBG.get('a1', 3) >= 2:
            a1_B(t - 1)
        if t >= 2 and DBG.get('a1', 3) >= 3:
            a1_C(t - 2)
        if t == 8 and upto >= 2:
            load_w(Wg1, w_in_d, 6 * D + 1 * D, D, key="W5")
            load_w(Wbs, w_bs_d, 0, D, key="W6")

    if upto >= 2:
        Bt = Alloc(DYN, DYN + 40)
        b_xb = [Bt.take([128, 1024], BF16) for _ in range(4)]
        b_xT = [Bt.take([128, 8, 512], BF16) for _ in range(2)]
        b_uT = Bt.take([128, 8, 512], BF16)
        b_og = Bt.take([128, 8, 512], BF16)
        b_g1 = Bt.take([128, 8, 512], BF16)
        b_m1 = Bt.take([128, 8, 512], BF16)
        b_qT = Bt.take([128, 8, 512], BF16)
        b_qb = Bt.take([128, 1024], BF16)
        b_vn = [Bt.take([128, 1024], BF16) for _ in range(4)]
        Bt2 = Alloc(DYN + 80)
        b_gl = [Bt2.take([128, 1024], F32) for _ in range(2)]
        b_st = Bt2.take([128, 8, 128], F32)
        b_lng = Bt2.take([128, 1024], F32)
        b_lnb = Bt2.take([128, 1024], F32)
        b_bsB = Bt2.take([128, 8, 128], F32)
        b_WsT = Bt2.take([128, 8, 128], BF16)
        b_RA = Bt2.take([128, 16, 16], F32, align=4)
        b_RB = Bt2.take([128, 16, 16], F32, align=4)
        b_stats = Bt2.take([128, 64], F32, align=4)

        bcast_load(b_lng, sgg_d[0:1, :], "lng")
        bcast_load(b_lnb, sgb_d[0:1, :], "lnb")
        bcast_load(Buf(b_bsB.ap.rearrange("p a b -> p (a b)"), b_bsB.res), sbs_d[0:1, :], "bsB")
        wsf = Buf(b_gl[0].ap.rearrange("p (a b) -> p a b", a=8), b_gl[0].res)
        trf = Buf(b_gl[1].ap[:, 0:128], b_gl[1].res)
        wsb = Buf(b_qb.ap.rearrange("p (a b) -> p a b", a=8), b_qb.res)
        P.dma("sp", [lambda e: e.dma_start(out=wsf.ap, in_=sws_d.rearrange("g t s -> t g s")),
                     lambda e: e.dma_start(out=trf.ap, in_=tril_d[:, :])], writes=[wsf, trf], key="wsf")
        TT("dve", wsb, wsb.ap, wsf, wsf.ap, trf, trf.ap.unsqueeze(1).broadcast_to([128, 8, 128]), ALU.mult)
        tpw = ps_bf(0)
        for g in range(8):
            TR(tpw, tpw.ap[:, g, :], wsb, wsb.ap[:, g, :])
        COPY("act", b_WsT, b_WsT.ap, tpw, tpw.ap)

        def a2a_tile(T):
            X = b_xT[T % 2]
            x_tile(T, xq_d, b_xb, X)
            tok = ps_f32(1, 2)
            for blk in range(4):
                Xb = X.ap[:, :, blk * 128:(blk + 1) * 128]
                for c in range(8):
                    for n in range(2):
                        MM(tok, tok.ap[:, n * 512:(n + 1) * 512], X, Xb[:, c, :], Wq, Wq.ap[:, c, n * 512:(n + 1) * 512],
                           c == 0, c == 7)
                COPY("act", b_qb, b_qb.ap, tok, tok.ap)
                rope(tok, tok.ap, b_qb, trig_q, T * 4 + blk, b_RA, b_RB)
                for j in (2 * blk, 2 * blk + 1):
                    ps = fm_proj(Wsu, j * 128, X, X.ap, (6, 7))
                    ACT(b_uT, b_uT.ap[:, j, :], ps, ps.ap, AF.Gelu_apprx_tanh)
                tq = ps_bf(3)
                for h in range(8):
                    TR(tq, tq.ap[:, h, :], b_qb, b_qb.ap[:, h * 128:(h + 1) * 128])
                COPY("dve", b_qT, b_qT.ap[:, :, blk * 128:(blk + 1) * 128], tq, tq.ap)
            P.dma("sp", lambda e: e.dma_start(out=qT_d[:, :, T * 512:(T + 1) * 512].rearrange("h p t -> p h t"),
                                              in_=b_qT.ap), reads=[b_qT], writes=["D:qT"], key="bqT")
            for blk in range(4):
                Xb = X.ap[:, :, blk * 128:(blk + 1) * 128]
                for c in range(8):
                    for n in range(2):
                        MM(tok, tok.ap[:, n * 512:(n + 1) * 512], X, Xb[:, c, :], Wsv, Wsv.ap[:, c, n * 512:(n + 1) * 512],
                           c == 0, c == 7)
                gl = b_gl[blk % 2]
                ACT(gl, gl.ap, tok, tok.ap, AF.Gelu_apprx_tanh)
                layer_norm(gl, gl.ap, b_vn[blk], b_vn[blk].ap, b_lng, b_lnb, b_stats)
                for j in (2 * blk, 2 * blk + 1):
                    ps = fm_proj(Wg1, j * 128, X, X.ap, (6, 7))
                    ACT(b_g1, b_g1.ap[:, j, :], ps, ps.ap, AF.Sigmoid)
            for blk in range(4):
                sps = ps_f32(4, 2)
                s3 = sps.ap.rearrange("p (g t) -> p g t", g=8)
                for g in range(8):
                    MM(sps, s3[:, g, :], b_vn[blk], b_vn[blk].ap[:, g * 128:(g + 1) * 128], b_WsT, b_WsT.ap[:, g, :],
                       True, True)
                TT("dve", b_st, b_st.ap, sps, s3, b_bsB, b_bsB.ap, ALU.add)
                TT("pool", b_og, b_og.ap[:, :, blk * 128:(blk + 1) * 128], b_st, b_st.ap,
                   b_uT, b_uT.ap[:, :, blk * 128:(blk + 1) * 128], ALU.mult)
            for j in range(8):
                ps = fm_proj(Wbs, j * 128, b_og, b_og.ap, (6, 7))
                TT("dve", b_m1, b_m1.ap[:, j, :], ps, ps.ap, b_g1, b_g1.ap[:, j, :], ALU.mult)
            P.dma("sp", lambda e: e.dma_start(out=m1_d[:, :, T * 512:(T + 1) * 512].rearrange("j p t -> p j t"),
                                              in_=b_m1.ap), reads=[b_m1], writes=["D:m1"], key="bm1")

        for T in range(8 if upto >= 2 else 0):
            a2a_tile(T)

    if upto >= 3:
        Cw = Alloc(DYN)
        Wxq = Cw.take([128, 8, 1024], BF16)
        Wg2 = Cw.take([128, 8, 1024], BF16)
        Wbm = Cw.take([128, 8, 1024], BF16)
        Wmk = Cw.take([128, 8, 1024], BF16)
        Wmv = Cw.take([128, 8, 1024], BF16)
        c_xb = [Cw.take([128, 1024], BF16) for _ in range(4)]
        c_xT = [Cw.take([128, 8, 512], BF16) for _ in range(2)]
        c_memb = Cw.take([128, 2, 1024], BF16)
        c_memT = Cw.take([128, 8, 256], BF16)
        c_kmT = Cw.take([128, 8, 256], BF16)
        c_vm = Cw.take([128, 2, 1024], BF16)
        c_xqT = Cw.take([128, 8, 512], BF16)
        c_pT = Cw.take([128, 8, 512], BF16)
        c_rs = [Cw.take([128, 512], F32) for _ in range(2)]
        c_ox = Cw.take([128, 8, 512], BF16)
        c_g2 = Cw.take([128, 8, 512], BF16)
        c_m1 = Cw.take([128, 8, 512], BF16)
        c_m12 = Cw.take([128, 8, 512], BF16)
        c_tmp = [Cw.take([128, 512], F32) for _ in range(2)]
        load_w(Wmk, w_mkv_d, 0, D, key="W0")
        load_w(Wmv, w_mkv_d, D, D, key="W1")
        load_w(Wxq, w_in_d, 5 * D, D, key="W2")
        load_w(Wg2, w_in_d, 8 * D, D, key="W3")
        load_w(Wbm, w_bm_d, 0, D, key="W4")
        P.dma("pool", lambda e: e.dma_start(out=c_memb.ap, in_=mem_d.rearrange("(b p) d -> p b d", p=128)),
              writes=[c_memb], key="memb")
        for mb in range(2):
            tp = ps_bf(0)
            for c in range(8):
                TR(tp, tp.ap[:, c, :], c_memb, c_memb.ap[:, mb, c * 128:(c + 1) * 128])
            COPY("act", c_memT, c_memT.ap[:, :, mb * 128:(mb + 1) * 128], tp, tp.ap)
        for j in range(8):
            ps = fm_proj(Wmk, j * 128, c_memT, c_memT.ap, (1, 2, 3))
            COPY("dve", c_kmT, c_kmT.ap[:, j, :], ps, ps.ap[:, 0:256])
        for mb in range(2):
            for n in range(2):
                ps = ps_f32(4 + n)
                for c in range(8):
                    MM(ps, ps.ap, c_memT, c_memT.ap[:, c, mb * 128:(mb + 1) * 128], Wmv, Wmv.ap[:, c, n * 512:(n + 1) * 512],
                       c == 0, c == 7)
                COPY("act", c_vm, c_vm.ap[:, mb, n * 512:(n + 1) * 512], ps, ps.ap)

        xa_banks = (1, 2, 3, 4, 5, 6, 7)

        def a2b_tile(T):
            X = c_xT[T % 2]
            x_tile(T, xq_d, c_xb, X)
            P.dma("sp", lambda e: e.dma_start(out=c_m1.ap, in_=m1_d[:, :, T * 512:(T + 1) * 512].rearrange("j p t -> p j t")),
                  reads=["D:m1"], writes=[c_m1], key="cm1")
            for j in range(8):
                ps = fm_proj(Wxq, j * 128, X, X.ap, xa_banks)
                COPY("dve" if j % 2 else "act", c_xqT, c_xqT.ap[:, j, :], ps, ps.ap)
            for hh in range(4):
                for mb in range(2):
                    b = xa_banks[fm_ctr[0] % 7]
                    fm_ctr[0] += 1
                    ps = ps_f32(b)
                    for dc in range(2):
                        MM(ps, ps.ap, c_kmT, c_kmT.ap[:, hh * 2 + dc, mb * 128:(mb + 1) * 128], c_xqT,
                           c_xqT.ap[:, hh * 2 + dc, :], dc == 0, dc == 1)
                    ACT(c_pT, c_pT.ap[:, hh * 2 + mb, :], ps, ps.ap, AF.Exp, scale=1.0 / 16.0)
            for j in range(8):
                ps = fm_proj(Wg2, j * 128, X, X.ap, xa_banks)
                ACT(c_g2, c_g2.ap[:, j, :], ps, ps.ap, AF.Sigmoid)
            for hh in range(4):
                b = xa_banks[fm_ctr[0] % 7]
                fm_ctr[0] += 1
                pss = ps_f32(b)
                for mb in range(2):
                    MM(pss, pss.ap, onesb, onesb.ap, c_pT, c_pT.ap[:, hh * 2 + mb, :], mb == 0, mb == 1)
                rs = c_rs[hh % 2]
                P.op("dve", lambda e, rs=rs, pss=pss: e.reciprocal(out=rs.ap, in_=pss.ap), reads=[pss], writes=[rs])
                for dc in range(2):
                    b = xa_banks[fm_ctr[0] % 7]
                    fm_ctr[0] += 1
                    ps = ps_f32(b)
                    for mb in range(2):
                        MM(ps, ps.ap, c_vm, c_vm.ap[:, mb, hh * 256 + dc * 128: hh * 256 + (dc + 1) * 128], c_pT,
                           c_pT.ap[:, hh * 2 + mb, :], mb == 0, mb == 1)
                    TT("dve", c_ox, c_ox.ap[:, hh * 2 + dc, :], ps, ps.ap, rs, rs.ap, ALU.mult)
            for j in range(8):
                ps = fm_proj(Wbm, j * 128, c_ox, c_ox.ap, xa_banks)
                tm = c_tmp[j % 2]
                TT("dve", tm, tm.ap, ps, ps.ap, c_g2, c_g2.ap[:, j, :], ALU.mult)
                TT("pool", c_m12, c_m12.ap[:, j, :], tm, tm.ap, c_m1, c_m1.ap[:, j, :], ALU.add)
            P.dma("sp", lambda e: e.dma_start(out=m12_d[:, :, T * 512:(T + 1) * 512].rearrange("j p t -> p j t"),
                                              in_=c_m12.ap), reads=[c_m12], writes=["D:m12"], key="cm12")

        for T in range(8 if upto >= 3 else 0):
            a2b_tile(T)

    if upto >= 4:
        Ca = Alloc(DYN)
        odaT = Ca.take([128, 8, NQ], BF16)
        KT = [Ca.take([128, SEQ], BF16) for _ in range(2)]
        VV = [Ca.take([128, NBLK, 128], BF16) for _ in range(2)]
        QT = [Ca.take([128, NQ], BF16) for _ in range(2)]
        NEB = 4
        ET = [Ca.take([128, 2, 512], BF16) for _ in range(NEB)]
        ES = [Ca.take([128, 512], BF16) for _ in range(NEB)]
        ES2 = [Ca.take([128, 512], BF16) for _ in range(2)]
        held = [None]
        On = [Ca.take([128, 512], F32) for _ in range(2)]
        c_rs = Ca.take([128, 512], F32)
        c_of = Ca.take([128, 512], F32)
        c_rstd = Ca.take([128, 512], F32)
        c_sq = Ca.take([128, 512], BF16)

        def c_loads(h):
            kt, vv, qt = KT[h % 2], VV[h % 2], QT[h % 2]
            P.dma("sp", [lambda e, q=q: e.dma_start(out=qt.ap[:, q * 2048:(q + 1) * 2048], in_=qT_d[h, :, q * 2048:(q + 1) * 2048])
                         for q in range(2)], reads=["D:qT"], writes=[qt])
            P.dma("sp", [lambda e, q=q: e.dma_start(out=kt.ap[:, q * 2048:(q + 1) * 2048], in_=kT_d[h, :, q * 2048:(q + 1) * 2048])
                         for q in range(4)], reads=["D:kT"], writes=[kt])
            P.dma("sp", [lambda e, q=q: e.dma_start(
                out=vv.ap[:, q * 8:(q + 1) * 8, :],
                in_=v_d[q * 1024:(q + 1) * 1024, h * 128:(h + 1) * 128].rearrange("(j p) d -> p j d", p=128))
                for q in range(8)], reads=["D:v"], writes=[vv])

        def attn_items(h):
            items = []
            for T in range(8):
                for m in range(2):
                    npairs = 4 * T + 4
                    for pr in range(npairs):
                        items.append((h, T, m, pr, npairs))
            return items

        item_ctr = [0]
        grp_ctr = [0]
        deferred = []

        def item_qk_exp(it):
            h, T, m, pr, npairs = it
            kt, qt = KT[h % 2], QT[h % 2]
            k = item_ctr[0]
            item_ctr[0] += 1
            i0 = max(0, pr - 4 * T)
            c0 = i0 * 128
            S = ps_f32(2 * (k % 2), 2)
            S3 = S.ap.rearrange("p (j n) -> p j n", j=2)
            E = ET[k % NEB]
            Es = ES[k % NEB]
            for jj in range(2):
                j = 2 * pr + jj
                MM(S, S3[:, jj, c0:512], kt, kt.ap[m * 64:(m + 1) * 64, j * 128:(j + 1) * 128],
                   qt, qt.ap[m * 64:(m + 1) * 64, T * 512 + c0:(T + 1) * 512], True, True)
            ACT(E, E.ap[:, :, c0:512], S, S3[:, :, c0:512], AF.Exp, scale=0.125)
            if pr >= 4 * T:
                i = pr - 4 * T
                TT("pool", E, E.ap[:, :, i * 128:(i + 1) * 128], E, E.ap[:, :, i * 128:(i + 1) * 128],
                   maskp, maskp.ap[:, i % 2, :, :], ALU.mult)
            TT("dve", Es, Es.ap[:, c0:512], E, E.ap[:, 0, c0:512], E, E.ap[:, 1, c0:512], ALU.add)
            return (E, Es, k)

        def item_pv(it, E, Es, k):
            h, T, m, pr, npairs = it
            vv = VV[h % 2]
            if pr == 0:
                grp_ctr[0] += 1
            gp = grp_ctr[0] % 2
            aO = ps_f32(4 + gp)
            aS = ps_f32(6 + gp)
            c0 = max(0, pr - 4 * T) * 128
            for jj in range(2):
                j = 2 * pr + jj
                MM(aO, aO.ap[:, c0:512], vv, vv.ap[:, j, :], E, E.ap[:, jj, c0:512],
                   pr == 0 and jj == 0, pr == npairs - 1 and jj == 1)
            if pr < 4 * T:
                if pr % 2 == 0:
                    held[0] = Es
                else:
                    E2 = ES2[(pr // 2) % 2]
                    TT("dve", E2, E2.ap, held[0], held[0].ap, Es, Es.ap, ALU.add)
                    MM(aS, aS.ap, onesb, onesb.ap, E2, E2.ap, pr == 1, False)
            else:
                MM(aS, aS.ap[:, c0:512], onesb, onesb.ap, Es, Es.ap[:, c0:512], pr == 0, pr == npairs - 1)
            if pr == npairs - 1:
                P.op("dve", lambda e: e.reciprocal(out=c_rs.ap, in_=aS.ap), reads=[aS], writes=[c_rs])
                if m == 0:
                    TT("dve", On[0], On[0].ap, aO, aO.ap, c_rs, c_rs.ap, ALU.mult)
                else:
                    STT("dve", On[1], On[1].ap, aO, aO.ap, neglam.ap[:, 0:1], c_rs, c_rs.ap, ALU.mult, ALU.mult,
                        extra_reads=[neglam])
                    TT("pool", c_of, c_of.ap, On[0], On[0].ap, On[1], On[1].ap, ALU.add)
                    TT("pool", c_sq, c_sq.ap, c_of, c_of.ap, c_of, c_of.ap, ALU.mult)
                    deferred.append([3, h, T, aS])

        def attn_finish(h, T, aS):
            MM(aS, aS.ap, onesb, onesb.ap, c_sq, c_sq.ap, True, True)
            ACT(c_rstd, c_rstd.ap, aS, aS.ap, AF.Ln, scale=1.0 / 128.0, bias=epsb.ap[:, 0:1], extra_reads=[epsb])
            ACT(c_rstd, c_rstd.ap, c_rstd, c_rstd.ap, AF.Exp, scale=-0.5)
            STT("dve", odaT, odaT.ap[:, h, T * 512:(T + 1) * 512], c_of, c_of.ap, gcol.ap[:, 0:1], c_rstd, c_rstd.ap,
                ALU.mult, ALU.mult, extra_reads=[gcol])

        def tick_deferred(force=False):
            for d_ in list(deferred):
                d_[0] -= 1
                if d_[0] <= 0 or force:
                    deferred.remove(d_)
                    attn_finish(d_[1], d_[2], d_[3])

        NHEAD = 8
        c_loads(0)
        LAG = 2
        pend = []
        for h in range(NHEAD):
            while pend:
                item_pv(*pend.pop(0))
            if h + 1 < NHEAD:
                c_loads(h + 1)
            for it in attn_items(h):
                cur = item_qk_exp(it)
                pend.append((it,) + cur)
                if len(pend) > LAG:
                    item_pv(*pend.pop(0))
                tick_deferred()
        while pend:
            item_pv(*pend.pop(0))
        tick_deferred(force=True)
        if debug:
            P.dma("sp", lambda e: e.dma_start(out=oda_d.rearrange("h p t -> p h t"), in_=odaT.ap), reads=[odaT],
                  writes=["D:out"])

    if upto >= 5:
        Da = Alloc(DYN + 32)
        Wba = Da.take([128, 8, 1024], BF16)
        Wg0 = Da.take([128, 8, 1024], BF16)
        Wo = Da.take([128, 8, 1024], BF16)
        d_xb = [Da.take([128, 1024], BF16) for _ in range(4)]
        d_xT = Da.take([128, 8, 512], BF16)
        d_xf = [Da.take([128, 1024], F32) for _ in range(2)]
        d_g0 = Da.take([128, 8, 512], BF16)
        d_m12 = Da.take([128, 8, 512], BF16)
        d_mT = Da.take([128, 8, 512], BF16)
        d_r = [Da.take([128, 1024], F32) for _ in range(2)]
        d_lng = Da.take([128, 1024], F32)
        d_lnb = Da.take([128, 1024], F32)
        d_tmp = [Da.take([128, 512], F32) for _ in range(2)]
        d_stats = Da.take([128, 64], F32, align=4)
        load_w(Wg0, w_in_d, 6 * D, D, key="W0")
        load_w(Wba, w_ba_d, 0, D, key="W1")
        load_w(Wo, w_out_d, 0, D, key="W2")
        bcast_load(d_lng, ln1g_d[0:1, :], "lng")
        bcast_load(d_lnb, ln1b_d[0:1, :], "lnb")

        def d1_tile(T):
            X = d_xT
            x_tile(T, xq_d, d_xb, X)
            P.dma("sp", lambda e: e.dma_start(out=d_m12.ap, in_=m12_d[:, :, T * 512:(T + 1) * 512].rearrange("j p t -> p j t")),
                  reads=["D:m12"], writes=[d_m12], key="dm12")
            for j in range(8):
                ps = fm_proj(Wg0, j * 128, X, X.ap, (1, 2, 3))
                ACT(d_g0, d_g0.ap[:, j, :], ps, ps.ap, AF.Sigmoid)
            for j in range(8):
                ps = fm_proj(Wba, j * 128, odaT, odaT.ap[:, :, T * 512:(T + 1) * 512], (1, 2, 3))
                tm = d_tmp[j % 2]
                TT("dve", tm, tm.ap, ps, ps.ap, d_g0, d_g0.ap[:, j, :], ALU.mult)
                TT("pool", d_mT, d_mT.ap[:, j, :], tm, tm.ap, d_m12, d_m12.ap[:, j, :], ALU.add)
            for blk in range(4):
                r0 = T * 512 + blk * 128
                xf = d_xf[blk % 2]
                P.dma("sp", lambda e, xf=xf, r0=r0: e.dma_start(out=xf.ap, in_=xq_d[r0:r0 + 128, :]), writes=[xf],
                      key="dxf%d" % (blk % 2))
                yps = ps_f32(4 + 2 * (blk % 2), 2)
                for c in range(8):
                    for n in range(2):
                        MM(yps, yps.ap[:, n * 512:(n + 1) * 512], d_mT, d_mT.ap[:, c, blk * 128:(blk + 1) * 128],
                           Wo, Wo.ap[:, c, n * 512:(n + 1) * 512], c == 0, c == 7)
                r = d_r[blk % 2]
                STT("dve", r, r.ap, xf, xf.ap, ALPHA, yps, yps.ap, ALU.mult, ALU.add)
                layer_norm(r, r.ap, r, r.ap, d_lng, d_lnb, d_stats, eng2="dve")
                P.dma("sp", lambda e, r=r, r0=r0: e.dma_start(out=x1_d[r0:r0 + 128, :], in_=r.ap), reads=[r],
                      writes=["D:x1"], key="dr%d" % (blk % 2))

        for T in range(8 if upto >= 5 else 0):
            d1_tile(T)

    if upto >= 6:
        Ea = Alloc(DYN)
        Wfi = Ea.take([128, 8, 2 * DFF], BF16)
        Wfo = Ea.take([128, NF, 1024], BF16)
        e_xb = [Ea.take([128, 1024], BF16) for _ in range(4)]
        e_xT = Ea.take([128, 8, 512], BF16)
        e_hT = Ea.take([128, NF, 512], BF16)
        e_sa = [Ea.take([128, 512], F32) for _ in range(2)]
        e_r = [Ea.take([128, 1024], F32) for _ in range(2)]
        e_stats = Ea.take([128, 64], F32, align=4)
        e_lng = Buf(rv(2 * 2048, [128, 1024], F32).ap, rv(2 * 2048, [128, 1024], F32).res)
        e_lnb = Buf(rv(4 * 2048, [128, 1024], F32).ap, rv(4 * 2048, [128, 1024], F32).res)
        load_w(Wfi, w_fi_d, 0, 2 * DFF, key="W3", piece=1408)
        load_w(Wfo, w_fo_d, 0, D, nchunks=NF, key="W4")
        bcast_load(e_lng, ln2g_d[0:1, :], "lng")
        bcast_load(e_lnb, ln2b_d[0:1, :], "lnb")

        def d2_tile(T):
            X = e_xT
            x_tile(T, x1_d, e_xb, X, extra_reads=["D:x1"])
            for f in range(NF):
                pa = ps_f32(1 + 2 * (f % 2))
                pb = ps_f32(2 + 2 * (f % 2))
                for c in range(8):
                    MM(pa, pa.ap, Wfi, Wfi.ap[:, c, f * 128:(f + 1) * 128], X, X.ap[:, c, :], c == 0, c == 7)
                for c in range(8):
                    MM(pb, pb.ap, Wfi, Wfi.ap[:, c, DFF + f * 128:DFF + (f + 1) * 128], X, X.ap[:, c, :], c == 0, c == 7)
                sa_ = e_sa[f % 2]
                ACT(sa_, sa_.ap, pa, pa.ap, AF.Silu)
                TT("dve", e_hT, e_hT.ap[:, f, :], sa_, sa_.ap, pb, pb.ap, ALU.mult)
            for blk in range(4):
                r0 = T * 512 + blk * 128
                r = e_r[blk % 2]
                P.dma("sp", lambda e, r=r, r0=r0: e.dma_start(out=r.ap, in_=x1_d[r0:r0 + 128, :]), reads=["D:x1"],
                      writes=[r])
                yps = ps_f32(5, 2)
                for f in range(NF):
                    for n in range(2):
                        MM(yps, yps.ap[:, n * 512:(n + 1) * 512], e_hT, e_hT.ap[:, f, blk * 128:(blk + 1) * 128],
                           Wfo, Wfo.ap[:, f, n * 512:(n + 1) * 512], f == 0, f == NF - 1)
                STT("dve", r, r.ap, r, r.ap, ALPHA, yps, yps.ap, ALU.mult, ALU.add)
                layer_norm(r, r.ap, r, r.ap, e_lng, e_lnb, e_stats, eng2="dve")
                P.dma("sp", lambda e, r=r, r0=r0: e.dma_start(out=out_d[r0:r0 + 128, :], in_=r.ap), reads=[r],
                      writes=["D:out"], key="er%d" % (blk % 2))

        for T in range(8 if upto >= 6 else 0):
            d2_tile(T)

    P.op("sp", lambda e: e.nop(), reads=["D:out"])
    P.finalize(st)
    with nc.Block() as block:
        P.emit(block)
    st.close()
    return nc, P, st


def host_inputs(inputs):
    x = np.asarray(inputs["x"], dtype=np.float32)
    mem = np.asarray(inputs["mem"], dtype=np.float32)
    pos = np.asarray(inputs["positions"], dtype=np.int32)
    ident = np.eye(128, dtype=np.float32)
    tril = np.tril(np.ones((128, 128), dtype=np.float32))
    tri_kq = np.ascontiguousarray(tril.T)
    ones = np.ones((128, 128), np.float32)
    zeros = np.zeros((128, 128), np.float32)
    invf = (500000.0 ** (-np.arange(8, dtype=np.float32) * 2.0 / 16.0)).astype(np.float32).reshape(1, 8)
    shared = {}
    for k in ("w_in", "w_mem_kv", "w_br_attn", "w_br_sg", "w_br_mem", "w_out", "w_ffn_in", "w_ffn_out"):
        shared[k] = np.ascontiguousarray(np.asarray(inputs[k], dtype=np.float32)[0])
    for k in ("lambda_q1", "lambda_k1", "lambda_q2", "lambda_k2", "da_subln_g", "sg_norm_g", "sg_norm_b",
              "ln1_g", "ln1_b", "ln2_g", "ln2_b"):
        shared[k] = np.ascontiguousarray(np.asarray(inputs[k], dtype=np.float32)[0].reshape(1, -1))
    shared["sg_w_s"] = np.ascontiguousarray(np.asarray(inputs["sg_w_s"], dtype=np.float32)[0])
    shared["sg_b_s"] = np.ascontiguousarray(np.asarray(inputs["sg_b_s"], dtype=np.float32)[0].reshape(1, -1))
    shared["c_ident"] = ident
    shared["c_tril"] = tril
    shared["c_invf"] = invf
    in_maps = []
    for c in range(8):
        b, half = c // 2, c % 2
        blocks = q_blocks(half)
        xb = np.ascontiguousarray(x[b])
        xq = np.ascontiguousarray(x[b].reshape(NBLK, 128, D)[blocks].reshape(NQ, D))
        pb = pos[b].reshape(NBLK, 128)
        posb = np.ascontiguousarray(pb.T)
        posq = np.ascontiguousarray(pb[blocks].T)
        if half == 0:
            mp = np.stack([np.stack([tri_kq, zeros], 1), np.stack([ones, tri_kq], 1)], 1)
        else:
            mp = np.stack([np.stack([ones, tri_kq], 1), np.stack([tri_kq, zeros], 1)], 1)
        m = dict(shared)
        m.update(xb=xb, xq=xq, posb=posb, posq=posq, mem=np.ascontiguousarray(mem[b]),
                 c_maskp=np.ascontiguousarray(mp.reshape(128, 512)))
        in_maps.append(m)
    return in_maps


_CACHE = {}


def kernel(**inputs):
    in_maps = host_inputs(inputs)
    if "nc" not in _CACHE:
        _CACHE["nc"] = build_program(debug=False)[0]
    nc = _CACHE["nc"]
    res = run_bass_kernel_spmd(nc, in_maps, core_ids=list(range(8)))
    out = np.zeros((4, SEQ, D), dtype=np.float32)
    for c in range(8):
        b, half = c // 2, c % 2
        blocks = q_blocks(half)
        o = np.asarray(res.results[c]["out"]).reshape(NSLOT, 128, D)
        out[b].reshape(NBLK, 128, D)[blocks] = o
    return out
```

```python
import math
from contextlib import ExitStack
import numpy as np
import concourse.bass as bass
import concourse.mybir as mybir
from concourse.bass_utils import run_bass_kernel_spmd

F32 = mybir.dt.float32
BF16 = mybir.dt.bfloat16
I32 = mybir.dt.int32
AF = mybir.ActivationFunctionType
ALU = mybir.AluOpType
AX = mybir.AxisListType

D = 1024
SEQ = 8192
NBLK = 64
NSLOT = 32
NQ = NSLOT * 128
DFF = 2816
NF = DFF // 128
ALPHA = 2.0 ** 0.25
LN_EPS = 1e-5
RMS_EPS = 1e-5
LAMBDA_INIT = 0.8 - 0.6 * math.exp(0.0)
TWO_PI = 2.0 * math.pi
GELU_C = 0.7978845608028654
NG = 103
CONST_G = 8
DBG = {}
IN_SHAPES = {}


def q_blocks(half):
    out = []
    for s in range(NSLOT):
        g = s // 2
        if half == 0:
            out.append(4 * g + (0 if s % 2 == 0 else 3))
        else:
            out.append(4 * g + (1 if s % 2 == 0 else 2))
    return out


class Buf:
    def __init__(self, ap, res):
        self.ap = ap
        self.res = tuple(res)

    def __getitem__(self, k):
        return self.ap[k]


class Prog:
    ENGS = ("pe", "act", "dve", "pool", "sp")
    CH = 6000

    def __init__(self, nc):
        self.nc = nc
        self.ops = []

    @staticmethod
    def _res(lst):
        out = []
        for x in lst:
            if isinstance(x, Buf):
                out.extend(x.res)
            elif isinstance(x, str):
                out.append(x)
            else:
                out.extend(x)
        return tuple(out)

    def op(self, eng, fn, reads=(), writes=()):
        self.ops.append(dict(eng=eng, fn=fn, reads=self._res(reads), writes=self._res(writes), dma=False, sig=None))

    def dma(self, eng, fns, reads=(), writes=(), key=None):
        if not isinstance(fns, (list, tuple)):
            fns = [fns]
        key = None
        for x in list(writes) + list(reads):
            if isinstance(x, Buf):
                key = x.res[0] + ("s" if eng == "pool" else "h")
                break
        assert key is not None
        self.ops.append(dict(eng=eng, fn=list(fns), reads=self._res(reads), writes=self._res(writes), dma=True,
                             key=key, sig=None))

    def finalize(self, stack):
        nc = self.nc
        ops = self.ops
        lastw = {}
        readers = {}
        deps = [None] * len(ops)
        for n, o in enumerate(ops):
            d = {}
            for r in o["reads"]:
                for w in lastw.get(r, ()):
                    d.setdefault(w, "raw")
                if r[0] == "P" and not o["dma"]:
                    for tag, rd in readers.get(r, {}).items():
                        if tag != o["eng"] and rd not in d:
                            d[rd] = "rar"
            for w in o["writes"]:
                comm = w.startswith("D:")
                if not comm:
                    for pw in lastw.get(w, ()):
                        d.setdefault(pw, "waw")
                for rd in readers.get(w, {}).values():
                    if rd not in d:
                        d[rd] = "war"
            dd = []
            for j, kind in d.items():
                if j == n:
                    continue
                oj = ops[j]
                if (not oj["dma"]) and (not o["dma"]) and oj["eng"] == o["eng"]:
                    if o["eng"] == "pe":
                        continue
                dd.append(j)
            deps[n] = dd
            for r in o["reads"]:
                tag = ("dma", o["key"]) if o["dma"] else o["eng"]
                readers.setdefault(r, {})[tag] = n
            for w in o["writes"]:
                if w.startswith("D:"):
                    lastw.setdefault(w, []).append(n)
                else:
                    lastw[w] = [n]
                readers[w] = {}
        for n in range(len(ops)):
            for j in deps[n]:
                ops[j]["need"] = True
        cnt = {e: 0 for e in self.ENGS}
        dcnt = {}
        for o in ops:
            if o["dma"]:
                k = o["key"]
                dcnt[k] = dcnt.get(k, 0) + 16 * len(o["fn"])
                o["sig"] = dcnt[k]
            elif o.get("need"):
                o["sig"] = (cnt[o["eng"]] // self.CH, cnt[o["eng"]] % self.CH + 1)
                cnt[o["eng"]] += 1
        self.esem = {(e, c): stack.enter_context(nc.semaphore("s_%s%d" % (e, c)))
                     for e in self.ENGS for c in range(cnt[e] // self.CH + 1)}
        self.dsem = {k: stack.enter_context(nc.semaphore("d_%d" % i)) for i, k in enumerate(dcnt.keys())}
        known = {e: {} for e in self.ENGS}
        per_eng = {e: [] for e in self.ENGS}
        snap = {}
        for n, o in enumerate(ops):
            E = o["eng"]
            kn = known[E]
            waits = []
            for j in sorted(deps[n], reverse=True):
                oj = ops[j]
                if oj["dma"]:
                    sk, v = ("d", oj["key"]), (0, oj["sig"])
                else:
                    sk, v = ("e", oj["eng"]), oj["sig"]
                if kn.get(sk, (0, 0)) >= v:
                    continue
                kn[sk] = v
                waits.append((self.dsem[sk[1]] if sk[0] == "d" else self.esem[(sk[1], v[0])], v[1]))
                for k2, v2 in snap.get(j, {}).items():
                    if v2 > kn.get(k2, (0, 0)):
                        kn[k2] = v2
            if o["sig"] is not None:
                snap[n] = dict(kn)
            per_eng[E].append((waits, o))
        self.per_eng = per_eng
        self.counts = (cnt, dcnt)

    def emit(self, block):
        engmap = {"pe": block.tensor, "act": block.scalar, "dve": block.vector, "pool": block.gpsimd,
                  "sp": block.sync}
        for E in self.ENGS:
            items = self.per_eng[E]
            esem = self.esem
            dsem = self.dsem

            def body(eng, items=items, esem=esem, E=E):
                for waits, o in items:
                    for sem, v in waits:
                        eng.wait_ge(sem, v)
                    if o["dma"]:
                        for f in o["fn"]:
                            f(eng).then_inc(dsem[o["key"]], 16)
                    else:
                        ins = o["fn"](eng)
                        if o["sig"] is not None:
                            ins.then_inc(esem[(E, o["sig"][0])], 1)
            engmap[E](body)


def build_program(debug=False, upto=6):
    nc = bass.Bass("TRN2", target_bir_lowering=False)

    IN_PHASE = dict(xb=1, xq=2, mem=3, w_in=1, w_mem_kv=3, w_br_attn=5, w_br_sg=2, w_br_mem=3, w_out=5,
                    w_ffn_in=6, w_ffn_out=6, sg_norm_g=2, sg_norm_b=2, sg_w_s=2, sg_b_s=2, c_tril=2,
                    ln1_g=5, ln1_b=5, ln2_g=6, ln2_b=6)
    IN_SHAPES.clear()

    def din(name, shape, dt=F32):
        if upto < IN_PHASE.get(name, 0):
            shape = [1] * (len(shape) - 1) + [2]
        IN_SHAPES[name] = tuple(shape)
        return nc.dram_tensor(name, list(shape), dt, kind="ExternalInput").ap()

    def dscr(name, shape, dt):
        kind = "ExternalOutput" if debug else "Internal"
        return nc.dram_tensor(name, list(shape), dt, kind=kind).ap()

    xb_d = din("xb", [SEQ, D])
    xq_d = din("xq", [NQ, D])
    posb_d = din("posb", [128, NBLK], I32)
    posq_d = din("posq", [128, NSLOT], I32)
    mem_d = din("mem", [256, D])
    w_in_d = din("w_in", [D, 9 * D])
    w_mkv_d = din("w_mem_kv", [D, 2 * D])
    w_ba_d = din("w_br_attn", [D, D])
    w_bs_d = din("w_br_sg", [D, D])
    w_bm_d = din("w_br_mem", [D, D])
    w_out_d = din("w_out", [D, D])
    w_fi_d = din("w_ffn_in", [D, 2 * DFF])
    w_fo_d = din("w_ffn_out", [DFF, D])
    lq1_d = din("lambda_q1", [1, 64]); lk1_d = din("lambda_k1", [1, 64])
    lq2_d = din("lambda_q2", [1, 64]); lk2_d = din("lambda_k2", [1, 64])
    dag_d = din("da_subln_g", [1, 128])
    sgg_d = din("sg_norm_g", [1, D]); sgb_d = din("sg_norm_b", [1, D])
    sws_d = din("sg_w_s", [8, 128, 128]); sbs_d = din("sg_b_s", [1, 8 * 128])
    ln1g_d = din("ln1_g", [1, D]); ln1b_d = din("ln1_b", [1, D])
    ln2g_d = din("ln2_g", [1, D]); ln2b_d = din("ln2_b", [1, D])
    ident_d = din("c_ident", [128, 128])
    tril_d = din("c_tril", [128, 128])
    maskp_d = din("c_maskp", [128, 2 * 2 * 128])
    invf_d = din("c_invf", [1, 8])
    out_d = nc.dram_tensor("out", [NQ, D], F32, kind="ExternalOutput").ap()

    kT_d = dscr("kT_s", [8, 128, SEQ], BF16)
    v_d = dscr("v_s", [SEQ, D], BF16)
    qT_d = dscr("qT_s", [8, 128, NQ], BF16)
    m1_d = dscr("m1_s", [8, 128, NQ], BF16)
    m12_d = dscr("m12_s", [8, 128, NQ], BF16)
    x1_d = dscr("x1_s", [NQ, D], F32)
    oda_d = dscr("oda_s", [8, 128, NQ], BF16) if debug else None

    st = ExitStack()
    region = st.enter_context(nc.sbuf_tensor("region", [128, NG * 1024], BF16))
    psum = st.enter_context(nc.psum_tensor("psum", [128, 4096], F32))
    P = Prog(nc)

    def rv(off_b, shape, dt):
        esz = 4 if dt in (F32, I32) else 2
        n = int(np.prod(shape[1:]))
        nb = n * esz
        assert off_b % 4 == 0 and off_b + nb <= NG * 2048, (off_b, shape)
        a = region[0:shape[0], off_b // 2: (off_b + nb) // 2]
        if dt != BF16:
            a = a.bitcast(dt)
        if len(shape) == 3:
            a = a.rearrange("p (a b) -> p a b", a=shape[1])
        elif len(shape) == 4:
            a = a.rearrange("p (a b c) -> p a b c", a=shape[1], b=shape[2])
        res = ["G%d" % g for g in range(off_b // 2048, (off_b + nb - 1) // 2048 + 1)]
        return Buf(a, res)

    class Alloc:
        def __init__(self, g0, g1=NG):
            self.off = g0 * 2048
            self.end = g1 * 2048

        def take(self, shape, dt, align=2048):
            self.off = (self.off + align - 1) // align * align
            b = rv(self.off, shape, dt)
            esz = 4 if dt in (F32, I32) else 2
            self.off += int(np.prod(shape[1:])) * esz
            assert self.off <= self.end, ("region overflow", self.off, self.end)
            return b

    def pbank(b, nb=1):
        return ["P%d" % i for i in range(b, b + nb)]

    def ps_f32(b, nb=1):
        return Buf(psum[:, b * 512:(b + nb) * 512], pbank(b, nb))

    def ps_bf(b):
        return Buf(psum[:, b * 512:(b + 1) * 512].bitcast(BF16).rearrange("p (a b) -> p a b", a=8), pbank(b))

    def MM(O, o_ap, L, l_ap, R, r_ap, start, stop):
        P.op("pe", lambda e: e.matmul(o_ap, lhsT=l_ap, rhs=r_ap, start=start, stop=stop), reads=[L, R], writes=[O])

    def TR(O, o_ap, I, i_ap):
        P.op("pe", lambda e: e.transpose(out=o_ap, in_=i_ap, identity=identb.ap), reads=[I, identb], writes=[O])

    def ACT(O, o_ap, I, i_ap, func, scale=1.0, bias=0.0, extra_reads=()):
        P.op("act", lambda e: e.activation(out=o_ap, in_=i_ap, func=func, bias=bias, scale=scale),
             reads=[I] + list(extra_reads), writes=[O])

    def COPY(eng, O, o_ap, I, i_ap):
        if eng == "act":
            P.op("act", lambda e: e.copy(out=o_ap, in_=i_ap), reads=[I], writes=[O])
        else:
            P.op(eng, lambda e: e.tensor_copy(out=o_ap, in_=i_ap), reads=[I], writes=[O])

    def TT(eng, O, o_ap, A, a_ap, B, b_ap, op):
        P.op(eng, lambda e: e.tensor_tensor(out=o_ap, in0=a_ap, in1=b_ap, op=op), reads=[A, B], writes=[O])

    def TS(eng, O, o_ap, A, a_ap, s1, s2, op0, op1=None, extra_reads=()):
        if op1 is None:
            P.op(eng, lambda e: e.tensor_scalar(out=o_ap, in0=a_ap, scalar1=s1, scalar2=None, op0=op0),
                 reads=[A] + list(extra_reads), writes=[O])
        else:
            P.op(eng, lambda e: e.tensor_scalar(out=o_ap, in0=a_ap, scalar1=s1, scalar2=s2, op0=op0, op1=op1),
                 reads=[A] + list(extra_reads), writes=[O])

    def STT(eng, O, o_ap, A, a_ap, scalar, B, b_ap, op0, op1, extra_reads=()):
        P.op(eng, lambda e: e.scalar_tensor_tensor(out=o_ap, in0=a_ap, scalar=scalar, in1=b_ap, op0=op0, op1=op1),
             reads=[A, B] + list(extra_reads), writes=[O])

    def load_w(Wb, src, col0, ncols, nchunks=8, row0=0, key=None, piece=1024):
        fns = []
        for c in range(nchunks):
            for q0 in range(0, ncols, piece):
                q1 = min(ncols, q0 + piece)
                fns.append(lambda e, c=c, q0=q0, q1=q1: e.dma_start(
                    out=Wb.ap[:, c, q0:q1], in_=src[row0 + c * 128: row0 + (c + 1) * 128, col0 + q0: col0 + q1]))
        P.dma("pool", fns, writes=[Wb], key=key)

    def bcast_load(dst, src_row, key):
        P.dma("sp", lambda e: e.dma_start(out=dst.ap, in_=src_row.partition_broadcast(128)), writes=[dst], key=key)

    ca = Alloc(0, CONST_G)
    identb = ca.take([128, 128], BF16, align=4)
    maskp = ca.take([128, 2, 2, 128], BF16, align=4)
    gcol = ca.take([128, 1], F32, align=4)
    onesb = ca.take([128, 128], BF16, align=4)
    neglam = ca.take([128, 1], F32, align=4)
    epsb = ca.take([128, 1], F32, align=4)
    ca.off = 2 * 2048
    trig_b = ca.take([128, NBLK, 32], F32)
    trig_q = ca.take([128, NSLOT, 32], F32)
    DYN = CONST_G

    sa = Alloc(NG - 12)
    posb_i = sa.take([128, NBLK], I32, align=4)
    posq_i = sa.take([128, NSLOT], I32, align=4)
    invf = sa.take([128, 8], F32, align=4)
    lamv = sa.take([128, 4, 64], F32, align=4)
    lprod = sa.take([128, 2, 64], F32, align=4)
    ls = sa.take([128, 4], F32, align=4)
    posf = sa.take([128, NBLK], F32, align=4)
    ang = sa.take([128, NBLK, 8], F32)
    tr1 = sa.take([128, NBLK, 8], F32)
    tr2 = sa.take([128, NBLK, 8], F32)
    tri = sa.take([128, NBLK, 8], I32)

    P.dma("pool", [lambda e: e.dma_start(out=identb.ap, in_=ident_d[:, :]),
                   lambda e: e.dma_start(out=maskp.ap.rearrange("p a b c -> p (a b c)"), in_=maskp_d[:, :])],
          writes=[identb, maskp], key="cst_cast")
    P.dma("sp", [lambda e: e.dma_start(out=posb_i.ap, in_=posb_d[:, :]),
                 lambda e: e.dma_start(out=posq_i.ap, in_=posq_d[:, :]),
                 lambda e: e.dma_start(out=invf.ap, in_=invf_d[0:1, :].partition_broadcast(128)),
                 lambda e: e.dma_start(out=gcol.ap, in_=dag_d[0:1, :].rearrange("o d -> d o")),
                 lambda e: e.dma_start(out=lamv.ap[:, 0, :], in_=lq1_d[0:1, :].partition_broadcast(128)),
                 lambda e: e.dma_start(out=lamv.ap[:, 1, :], in_=lk1_d[0:1, :].partition_broadcast(128)),
                 lambda e: e.dma_start(out=lamv.ap[:, 2, :], in_=lq2_d[0:1, :].partition_broadcast(128)),
                 lambda e: e.dma_start(out=lamv.ap[:, 3, :], in_=lk2_d[0:1, :].partition_broadcast(128))],
          writes=[posb_i, posq_i, invf, gcol, lamv], key="cst_sp")
    P.op("pool", lambda e: e.memset(onesb.ap, 1.0), writes=[onesb])
    P.op("pool", lambda e: e.memset(epsb.ap, RMS_EPS), writes=[epsb])
    lv4 = lamv.ap.rearrange("p (a b) c -> p a b c", a=2)
    TT("dve", lprod, lprod.ap, lamv, lv4[:, :, 0, :], lamv, lv4[:, :, 1, :], ALU.mult)
    P.op("dve", lambda e: e.tensor_reduce(out=ls.ap[:, 0:2], in_=lprod.ap, axis=AX.X, op=ALU.add),
         reads=[lprod], writes=[ls])
    ACT(ls, ls.ap[:, 2:4], ls, ls.ap[:, 0:2], AF.Exp)
    TT("dve", neglam, neglam.ap, ls, ls.ap[:, 3:4], ls, ls.ap[:, 2:3], ALU.subtract)
    TS("dve", neglam, neglam.ap, neglam, neglam.ap, -LAMBDA_INIT, None, ALU.add)
    TS("dve", gcol, gcol.ap, gcol, gcol.ap, 1.0 - LAMBDA_INIT, None, ALU.mult)

    def trig_tables(pos_i, nb, table):
        pf = Buf(posf.ap[:, 0:nb], posf.res)
        A_ = Buf(ang.ap[:, 0:nb, :], ang.res)
        t1 = Buf(tr1.ap[:, 0:nb, :], tr1.res)
        t2 = Buf(tr2.ap[:, 0:nb, :], tr2.res)
        ti = Buf(tri.ap[:, 0:nb, :], tri.res)
        COPY("dve", pf, pf.ap, pos_i, pos_i.ap)
        TT("dve", A_, A_.ap, pf, pf.ap.unsqueeze(2).broadcast_to([128, nb, 8]),
           invf, invf.ap.unsqueeze(1).broadcast_to([128, nb, 8]), ALU.mult)

        def red_sin(dst_ap, shift):
            TS("dve", t1, t1.ap, A_, A_.ap, shift, None, ALU.add)
            TS("dve", t2, t2.ap, t1, t1.ap, 1.0 / TWO_PI, None, ALU.mult)
            COPY("dve", ti, ti.ap, t2, t2.ap)
            COPY("dve", t2, t2.ap, ti, ti.ap)
            STT("dve", t1, t1.ap, t2, t2.ap, -TWO_PI, t1, t1.ap, ALU.mult, ALU.add)
            TS("dve", t2, t2.ap, t1, t1.ap, math.pi, -TWO_PI, ALU.is_ge, ALU.mult)
            TT("dve", t1, t1.ap, t1, t1.ap, t2, t2.ap, ALU.add)
            TS("dve", t2, t2.ap, t1, t1.ap, -math.pi, TWO_PI, ALU.is_lt, ALU.mult)
            TT("dve", t1, t1.ap, t1, t1.ap, t2, t2.ap, ALU.add)
            TS("dve", t1, t1.ap, t1, t1.ap, 0.999999, None, ALU.mult)
            ACT(table, dst_ap, t1, t1.ap, AF.Sin)
        red_sin(table.ap[:, :, 24:32], 0.0)
        red_sin(table.ap[:, :, 0:8], math.pi / 2.0)
        COPY("dve", table, table.ap[:, :, 8:16], table, table.ap[:, :, 0:8])
        TS("dve", table, table.ap[:, :, 16:24], table, table.ap[:, :, 24:32], -1.0, None, ALU.mult)
    trig_tables(posb_i, NBLK, trig_b)
    trig_tables(posq_i, NSLOT, trig_q)

    def rope(Z, z_ap, OB, table, blk, RA, RB):
        kz = z_ap.rearrange("p (g d) -> p g d", g=16)
        ob3 = OB.ap.rearrange("p (g d) -> p g d", g=16)
        cs = table.ap[:, blk, 0:16].unsqueeze(1).broadcast_to([128, 16, 16])
        slo = table.ap[:, blk, 16:24].unsqueeze(1).broadcast_to([128, 16, 8])
        shi = table.ap[:, blk, 24:32].unsqueeze(1).broadcast_to([128, 16, 8])
        TT("dve", RA, RA.ap, Z, kz[:, :, 0:16], table, cs, ALU.mult)
        TT("dve", RB, RB.ap[:, :, 0:8], Z, kz[:, :, 8:16], table, slo, ALU.mult)
        TT("dve", RB, RB.ap[:, :, 8:16], Z, kz[:, :, 0:8], table, shi, ALU.mult)
        TT("dve", OB, ob3[:, :, 0:16], RA, RA.ap, RB, RB.ap, ALU.add)

    stat_ctr = [0]

    def layer_norm(R, r_ap, O, o_ap, g, b, STATS, eps=LN_EPS, eng2="pool"):
        k = stat_ctr[0] % 4
        stat_ctr[0] += 1
        sv_ = STATS.ap[:, k * 16:(k + 1) * 16]
        S = STATS
        P.op("dve", lambda e: e.bn_stats(out=sv_[:, 0:6], in_=r_ap[:, 0:512]), reads=[R], writes=[S])
        P.op("dve", lambda e: e.bn_stats(out=sv_[:, 6:12], in_=r_ap[:, 512:1024]), reads=[R], writes=[S])
        P.op("dve", lambda e: e.bn_aggr(out=sv_[:, 12:14], in_=sv_[:, 0:12]), reads=[S], writes=[S])
        ACT(S, sv_[:, 14:15], S, sv_[:, 13:14], AF.Sqrt, scale=1.0, bias=eps)
        P.op("dve", lambda e: e.reciprocal(out=sv_[:, 15:16], in_=sv_[:, 14:15]), reads=[S], writes=[S])
        STT("dve", S, sv_[:, 14:15], S, sv_[:, 12:13], -1.0, S, sv_[:, 15:16], ALU.mult, ALU.mult)
        P.op("act", lambda e: e.activation(out=r_ap, in_=r_ap, func=AF.Identity, bias=sv_[:, 14:15], scale=sv_[:, 15:16]),
             reads=[R, S], writes=[R])
        TT("pool", R, r_ap, R, r_ap, g, g.ap, ALU.mult)
        TT("pool", O, o_ap, R, r_ap, b, b.ap, ALU.add)

    def load_xT(src_rows_ap, XB, key, tpb, XT, xt_ap, evac="act"):
        P.dma("pool", lambda e: e.dma_start(out=XB.ap, in_=src_rows_ap), writes=[XB], key=key)
        tp = ps_bf(tpb)
        for c in range(8):
            TR(tp, tp.ap[:, c, :], XB, XB.ap[:, c * 128:(c + 1) * 128])
        COPY(evac, XT, xt_ap, tp, tp.ap)

    def x_issue(src_rows_ap, XB, extra_reads=()):
        P.dma("pool", lambda e: e.dma_start(out=XB.ap, in_=src_rows_ap), reads=list(extra_reads), writes=[XB])

    def x_trans(XB, tpb, XT, xt_ap, evac="act"):
        tp = ps_bf(tpb)
        for c in range(8):
            TR(tp, tp.ap[:, c, :], XB, XB.ap[:, c * 128:(c + 1) * 128])
        COPY(evac, XT, xt_ap, tp, tp.ap)

    def x_tile(T, src_d, XBs, X, extra_reads=()):
        if T == 0:
            for blk in range(4):
                x_issue(src_d[blk * 128:(blk + 1) * 128, :], XBs[blk], extra_reads)
        for blk in range(4):
            x_trans(XBs[blk], 0, X, X.ap[:, :, blk * 128:(blk + 1) * 128])
            if T + 1 < 8:
                r1 = (T + 1) * 512 + blk * 128
                x_issue(src_d[r1:r1 + 128, :], XBs[blk], extra_reads)

    fm_ctr = [0]

    def fm_proj(W, col0, X, x_ap, banks):
        b = banks[fm_ctr[0] % len(banks)]
        fm_ctr[0] += 1
        ps = ps_f32(b)
        n = x_ap.shape[-1]
        for c in range(8):
            MM(ps, ps.ap[:, 0:n], W, W.ap[:, c, col0:col0 + 128], X, x_ap[:, c, :], c == 0, c == 7)
        return ps

    A = Alloc(DYN)
    Wk = A.take([128, 8, 1024], BF16)
    Wv = A.take([128, 8, 1024], BF16)
    if upto >= 1:
        load_w(Wk, w_in_d, 1 * D, D, key="W0")
        load_w(Wv, w_in_d, 2 * D, D, key="W1")
    a_xb = [A.take([128, 1024], BF16) for _ in range(3)]
    a_xT = [A.take([128, 8, 128], BF16) for _ in range(2)]
    a_kb = [A.take([128, 1024], BF16) for _ in range(2)]
    a_vb = [A.take([128, 1024], BF16) for _ in range(2)]
    a_kT = [A.take([128, 8, 512], BF16) for _ in range(2)]
    a_RA = A.take([128, 16, 16], F32, align=4)
    a_RB = A.take([128, 16, 16], F32, align=4)

    def a1_A(t):
        load_xT(xb_d[t * 128:(t + 1) * 128, :], a_xb[t % 3], "xb%d" % (t % 3), t % 2, a_xT[t % 2], a_xT[t % 2].ap)

    def a1_B(t):
        X = a_xT[t % 2]
        kps = ps_f32(2, 2)
        vps = ps_f32(4, 2)
        for (O, W) in ((kps, Wk), (vps, Wv)):
            for c in range(8):
                for n in range(2):
                    MM(O, O.ap[:, n * 512:(n + 1) * 512], X, X.ap[:, c, :], W,
                       W.ap[:, c, n * 512:(n + 1) * 512], c == 0, c == 7)
        kb = a_kb[t % 2]
        COPY("act", kb, kb.ap, kps, kps.ap)
        if DBG.get('b', 9) >= 2:
            rope(kps, kps.ap, kb, trig_b, t, a_RA, a_RB)
        vb = a_vb[t % 2]
        if DBG.get('b', 9) >= 3:
            COPY("dve", vb, vb.ap, vps, vps.ap)
        if DBG.get('b', 9) >= 4:
            P.dma("sp", lambda e: e.dma_start(out=v_d[t * 128:(t + 1) * 128, :], in_=vb.ap), reads=[vb],
                  writes=["D:v"], key="vb%d" % (t % 2))

    def a1_C(t):
        kb = a_kb[t % 2]
        tk = ps_bf(6 + t % 2)
        for h in range(8):
            TR(tk, tk.ap[:, h, :], kb, kb.ap[:, h * 128:(h + 1) * 128])
        KT = a_kT[(t // 4) % 2]
        COPY("dve", KT, KT.ap[:, :, (t % 4) * 128:(t % 4 + 1) * 128], tk, tk.ap)
        if t % 4 == 3:
            t0 = (t - 3) * 128
            P.dma("sp", lambda e: e.dma_start(out=kT_d[:, :, t0:t0 + 512].rearrange("h p t -> p h t"), in_=KT.ap),
                  reads=[KT], writes=["D:kT"], key="kTt%d" % ((t // 4) % 2))

    Bw = Alloc(DYN + 40)
    Wq = Bw.take([128, 8, 1024], BF16)
    Wsu = Bw.take([128, 8, 1024], BF16)
    Wsv = Bw.take([128, 8, 1024], BF16)
    Wg1 = Bw.take([128, 8, 1024], BF16)
    Wbs = Bw.take([128, 8, 1024], BF16)
    if upto >= 2:
        load_w(Wq, w_in_d, 0, D, key="W2")
        load_w(Wsv, w_in_d, 4 * D, D, key="W4")
        load_w(Wsu, w_in_d, 3 * D, D, key="W3")

    NB_A1 = NBLK if upto >= 1 else 0
    for t in range(NB_A1 + 2):
        if t < NB_A1:
            a1_A(t)
        if 1 <= t <= NB_A1 and DBG.get('a1', 3) >= 2:
            a1_B(t - 1)
        if t >= 2 and DBG.get('a1', 3) >= 3:
            a1_C(t - 2)
        if t == 8 and upto >= 2:
            load_w(Wg1, w_in_d, 6 * D + 1 * D, D, key="W5")
            load_w(Wbs, w_bs_d, 0, D, key="W6")

    if upto >= 2:
        Bt = Alloc(DYN, DYN + 40)
        b_xb = [Bt.take([128, 1024], BF16) for _ in range(4)]
        b_xT = [Bt.take([128, 8, 512], BF16) for _ in range(2)]
        b_uT = Bt.take([128, 8, 512], BF16)
        b_og = Bt.take([128, 8, 512], BF16)
        b_g1 = Bt.take([128, 8, 512], BF16)
        b_m1 = Bt.take([128, 8, 512], BF16)
        b_qT = Bt.take([128, 8, 512], BF16)
        b_qb = Bt.take([128, 1024], BF16)
        b_vn = [Bt.take([128, 1024], BF16) for _ in range(4)]
        Bt2 = Alloc(DYN + 80)
        b_gl = [Bt2.take([128, 1024], F32) for _ in range(2)]
        b_st = Bt2.take([128, 8, 128], F32)
        b_lng = Bt2.take([128, 1024], F32)
        b_lnb = Bt2.take([128, 1024], F32)
        b_bsB = Bt2.take([128, 8, 128], F32)
        b_WsT = Bt2.take([128, 8, 128], BF16)
        b_RA = Bt2.take([128, 16, 16], F32, align=4)
        b_RB = Bt2.take([128, 16, 16], F32, align=4)
        b_stats = Bt2.take([128, 64], F32, align=4)

        bcast_load(b_lng, sgg_d[0:1, :], "lng")
        bcast_load(b_lnb, sgb_d[0:1, :], "lnb")
        bcast_load(Buf(b_bsB.ap.rearrange("p a b -> p (a b)"), b_bsB.res), sbs_d[0:1, :], "bsB")
        wsf = Buf(b_gl[0].ap.rearrange("p (a b) -> p a b", a=8), b_gl[0].res)
        trf = Buf(b_gl[1].ap[:, 0:128], b_gl[1].res)
        wsb = Buf(b_qb.ap.rearrange("p (a b) -> p a b", a=8), b_qb.res)
        P.dma("sp", [lambda e: e.dma_start(out=wsf.ap, in_=sws_d.rearrange("g t s -> t g s")),
                     lambda e: e.dma_start(out=trf.ap, in_=tril_d[:, :])], writes=[wsf, trf], key="wsf")
        TT("dve", wsb, wsb.ap, wsf, wsf.ap, trf, trf.ap.unsqueeze(1).broadcast_to([128, 8, 128]), ALU.mult)
        tpw = ps_bf(0)
        for g in range(8):
            TR(tpw, tpw.ap[:, g, :], wsb, wsb.ap[:, g, :])
        COPY("act", b_WsT, b_WsT.ap, tpw, tpw.ap)

        def a2a_tile(T):
            X = b_xT[T % 2]
            x_tile(T, xq_d, b_xb, X)
            tok = ps_f32(1, 2)
            for blk in range(4):
                Xb = X.ap[:, :, blk * 128:(blk + 1) * 128]
                for c in range(8):
                    for n in range(2):
                        MM(tok, tok.ap[:, n * 512:(n + 1) * 512], X, Xb[:, c, :], Wq, Wq.ap[:, c, n * 512:(n + 1) * 512],
                           c == 0, c == 7)
                COPY("act", b_qb, b_qb.ap, tok, tok.ap)
                rope(tok, tok.ap, b_qb, trig_q, T * 4 + blk, b_RA, b_RB)
                for j in (2 * blk, 2 * blk + 1):
                    ps = fm_proj(Wsu, j * 128, X, X.ap, (6, 7))
                    ACT(b_uT, b_uT.ap[:, j, :], ps, ps.ap, AF.Gelu_apprx_tanh)
                tq = ps_bf(3)
                for h in range(8):
                    TR(tq, tq.ap[:, h, :], b_qb, b_qb.ap[:, h * 128:(h + 1) * 128])
                COPY("dve", b_qT, b_qT.ap[:, :, blk * 128:(blk + 1) * 128], tq, tq.ap)
            P.dma("sp", lambda e: e.dma_start(out=qT_d[:, :, T * 512:(T + 1) * 512].rearrange("h p t -> p h t"),
                                              in_=b_qT.ap), reads=[b_qT], writes=["D:qT"], key="bqT")
            for blk in range(4):
                Xb = X.ap[:, :, blk * 128:(blk + 1) * 128]
                for c in range(8):
                    for n in range(2):
                        MM(tok, tok.ap[:, n * 512:(n + 1) * 512], X, Xb[:, c, :], Wsv, Wsv.ap[:, c, n * 512:(n + 1) * 512],
                           c == 0, c == 7)
                gl = b_gl[blk % 2]
                ACT(gl, gl.ap, tok, tok.ap, AF.Gelu_apprx_tanh)
                layer_norm(gl, gl.ap, b_vn[blk], b_vn[blk].ap, b_lng, b_lnb, b_stats)
                for j in (2 * blk, 2 * blk + 1):
                    ps = fm_proj(Wg1, j * 128, X, X.ap, (6, 7))
                    ACT(b_g1, b_g1.ap[:, j, :], ps, ps.ap, AF.Sigmoid)
            for blk in range(4):
                sps = ps_f32(4, 2)
                s3 = sps.ap.rearrange("p (g t) -> p g t", g=8)
                for g in range(8):
                    MM(sps, s3[:, g, :], b_vn[blk], b_vn[blk].ap[:, g * 128:(g + 1) * 128], b_WsT, b_WsT.ap[:, g, :],
                       True, True)
                TT("dve", b_st, b_st.ap, sps, s3, b_bsB, b_bsB.ap, ALU.add)
                TT("pool", b_og, b_og.ap[:, :, blk * 128:(blk + 1) * 128], b_st, b_st.ap,
                   b_uT, b_uT.ap[:, :, blk * 128:(blk + 1) * 128], ALU.mult)
            for j in range(8):
                ps = fm_proj(Wbs, j * 128, b_og, b_og.ap, (6, 7))
                TT("dve", b_m1, b_m1.ap[:, j, :], ps, ps.ap, b_g1, b_g1.ap[:, j, :], ALU.mult)
            P.dma("sp", lambda e: e.dma_start(out=m1_d[:, :, T * 512:(T + 1) * 512].rearrange("j p t -> p j t"),
                                              in_=b_m1.ap), reads=[b_m1], writes=["D:m1"], key="bm1")

        for T in range(8 if upto >= 2 else 0):
            a2a_tile(T)

    if upto >= 3:
        Cw = Alloc(DYN)
        Wxq = Cw.take([128, 8, 1024], BF16)
        Wg2 = Cw.take([128, 8, 1024], BF16)
        Wbm = Cw.take([128, 8, 1024], BF16)
        Wmk = Cw.take([128, 8, 1024], BF16)
        Wmv = Cw.take([128, 8, 1024], BF16)
        c_xb = [Cw.take([128, 1024], BF16) for _ in range(4)]
        c_xT = [Cw.take([128, 8, 512], BF16) for _ in range(2)]
        c_memb = Cw.take([128, 2, 1024], BF16)
        c_memT = Cw.take([128, 8, 256], BF16)
        c_kmT = Cw.take([128, 8, 256], BF16)
        c_vm = Cw.take([128, 2, 1024], BF16)
        c_xqT = Cw.take([128, 8, 512], BF16)
        c_pT = Cw.take([128, 8, 512], BF16)
        c_rs = [Cw.take([128, 512], F32) for _ in range(2)]
        c_ox = Cw.take([128, 8, 512], BF16)
        c_g2 = Cw.take([128, 8, 512], BF16)
        c_m1 = Cw.take([128, 8, 512], BF16)
        c_m12 = Cw.take([128, 8, 512], BF16)
        c_tmp = [Cw.take([128, 512], F32) for _ in range(2)]
        load_w(Wmk, w_mkv_d, 0, D, key="W0")
        load_w(Wmv, w_mkv_d, D, D, key="W1")
        load_w(Wxq, w_in_d, 5 * D, D, key="W2")
        load_w(Wg2, w_in_d, 8 * D, D, key="W3")
        load_w(Wbm, w_bm_d, 0, D, key="W4")
        P.dma("pool", lambda e: e.dma_start(out=c_memb.ap, in_=mem_d.rearrange("(b p) d -> p b d", p=128)),
              writes=[c_memb], key="memb")
        for mb in range(2):
            tp = ps_bf(0)
            for c in range(8):
                TR(tp, tp.ap[:, c, :], c_memb, c_memb.ap[:, mb, c * 128:(c + 1) * 128])
            COPY("act", c_memT, c_memT.ap[:, :, mb * 128:(mb + 1) * 128], tp, tp.ap)
        for j in range(8):
            ps = fm_proj(Wmk, j * 128, c_memT, c_memT.ap, (1, 2, 3))
            COPY("dve", c_kmT, c_kmT.ap[:, j, :], ps, ps.ap[:, 0:256])
        for mb in range(2):
            for n in range(2):
                ps = ps_f32(4 + n)
                for c in range(8):
                    MM(ps, ps.ap, c_memT, c_memT.ap[:, c, mb * 128:(mb + 1) * 128], Wmv, Wmv.ap[:, c, n * 512:(n + 1) * 512],
                       c == 0, c == 7)
                COPY("act", c_vm, c_vm.ap[:, mb, n * 512:(n + 1) * 512], ps, ps.ap)

        xa_banks = (1, 2, 3, 4, 5, 6, 7)

        def a2b_tile(T):
            X = c_xT[T % 2]
            x_tile(T, xq_d, c_xb, X)
            P.dma("sp", lambda e: e.dma_start(out=c_m1.ap, in_=m1_d[:, :, T * 512:(T + 1) * 512].rearrange("j p t -> p j t")),
                  reads=["D:m1"], writes=[c_m1], key="cm1")
            for j in range(8):
                ps = fm_proj(Wxq, j * 128, X, X.ap, xa_banks)
                COPY("dve" if j % 2 else "act", c_xqT, c_xqT.ap[:, j, :], ps, ps.ap)
            for hh in range(4):
                for mb in range(2):
                    b = xa_banks[fm_ctr[0] % 7]
                    fm_ctr[0] += 1
                    ps = ps_f32(b)
                    for dc in range(2):
                        MM(ps, ps.ap, c_kmT, c_kmT.ap[:, hh * 2 + dc, mb * 128:(mb + 1) * 128], c_xqT,
                           c_xqT.ap[:, hh * 2 + dc, :], dc == 0, dc == 1)
                    ACT(c_pT, c_pT.ap[:, hh * 2 + mb, :], ps, ps.ap, AF.Exp, scale=1.0 / 16.0)
            for j in range(8):
                ps = fm_proj(Wg2, j * 128, X, X.ap, xa_banks)
                ACT(c_g2, c_g2.ap[:, j, :], ps, ps.ap, AF.Sigmoid)
            for hh in range(4):
                b = xa_banks[fm_ctr[0] % 7]
                fm_ctr[0] += 1
                pss = ps_f32(b)
                for mb in range(2):
                    MM(pss, pss.ap, onesb, onesb.ap, c_pT, c_pT.ap[:, hh * 2 + mb, :], mb == 0, mb == 1)
                rs = c_rs[hh % 2]
                P.op("dve", lambda e, rs=rs, pss=pss: e.reciprocal(out=rs.ap, in_=pss.ap), reads=[pss], writes=[rs])
                for dc in range(2):
                    b = xa_banks[fm_ctr[0] % 7]
                    fm_ctr[0] += 1
                    ps = ps_f32(b)
                    for mb in range(2):
                        MM(ps, ps.ap, c_vm, c_vm.ap[:, mb, hh * 256 + dc * 128: hh * 256 + (dc + 1) * 128], c_pT,
                           c_pT.ap[:, hh * 2 + mb, :], mb == 0, mb == 1)
                    TT("dve", c_ox, c_ox.ap[:, hh * 2 + dc, :], ps, ps.ap, rs, rs.ap, ALU.mult)
            for j in range(8):
                ps = fm_proj(Wbm, j * 128, c_ox, c_ox.ap, xa_banks)
                tm = c_tmp[j % 2]
                TT("dve", tm, tm.ap, ps, ps.ap, c_g2, c_g2.ap[:, j, :], ALU.mult)
                TT("pool", c_m12, c_m12.ap[:, j, :], tm, tm.ap, c_m1, c_m1.ap[:, j, :], ALU.add)
            P.dma("sp", lambda e: e.dma_start(out=m12_d[:, :, T * 512:(T + 1) * 512].rearrange("j p t -> p j t"),
                                              in_=c_m12.ap), reads=[c_m12], writes=["D:m12"], key="cm12")

        for T in range(8 if upto >= 3 else 0):
            a2b_tile(T)

    if upto >= 4:
        Ca = Alloc(DYN)
        odaT = Ca.take([128, 8, NQ], BF16)
        KT = [Ca.take([128, SEQ], BF16) for _ in range(2)]
        VV = [Ca.take([128, NBLK, 128], BF16) for _ in range(2)]
        QT = [Ca.take([128, NQ], BF16) for _ in range(2)]
        NEB = 4
        ET = [Ca.take([128, 2, 512], BF16) for _ in range(NEB)]
        ES = [Ca.take([128, 512], BF16) for _ in range(NEB)]
        ES2 = [Ca.take([128, 512], BF16) for _ in range(2)]
        held = [None]
        On = [Ca.take([128, 512], F32) for _ in range(2)]
        c_rs = Ca.take([128, 512], F32)
        c_of = Ca.take([128, 512], F32)
        c_rstd = Ca.take([128, 512], F32)
        c_sq = Ca.take([128, 512], BF16)

        def c_loads(h):
            kt, vv, qt = KT[h % 2], VV[h % 2], QT[h % 2]
            P.dma("sp", [lambda e, q=q: e.dma_start(out=qt.ap[:, q * 2048:(q + 1) * 2048], in_=qT_d[h, :, q * 2048:(q + 1) * 2048])
                         for q in range(2)], reads=["D:qT"], writes=[qt])
            P.dma("sp", [lambda e, q=q: e.dma_start(out=kt.ap[:, q * 2048:(q + 1) * 2048], in_=kT_d[h, :, q * 2048:(q + 1) * 2048])
                         for q in range(4)], reads=["D:kT"], writes=[kt])
            P.dma("sp", [lambda e, q=q: e.dma_start(
                out=vv.ap[:, q * 8:(q + 1) * 8, :],
                in_=v_d[q * 1024:(q + 1) * 1024, h * 128:(h + 1) * 128].rearrange("(j p) d -> p j d", p=128))
                for q in range(8)], reads=["D:v"], writes=[vv])

        def attn_items(h):
            items = []
            for T in range(8):
                for m in range(2):
                    npairs = 4 * T + 4
                    for pr in range(npairs):
                        items.append((h, T, m, pr, npairs))
            return items

        item_ctr = [0]
        grp_ctr = [0]
        deferred = []

        def item_qk_exp(it):
            h, T, m, pr, npairs = it
            kt, qt = KT[h % 2], QT[h % 2]
            k = item_ctr[0]
            item_ctr[0] += 1
            i0 = max(0, pr - 4 * T)
            c0 = i0 * 128
            S = ps_f32(2 * (k % 2), 2)
            S3 = S.ap.rearrange("p (j n) -> p j n", j=2)
            E = ET[k % NEB]
            Es = ES[k % NEB]
            for jj in range(2):
                j = 2 * pr + jj
                MM(S, S3[:, jj, c0:512], kt, kt.ap[m * 64:(m + 1) * 64, j * 128:(j + 1) * 128],
                   qt, qt.ap[m * 64:(m + 1) * 64, T * 512 + c0:(T + 1) * 512], True, True)
            ACT(E, E.ap[:, :, c0:512], S, S3[:, :, c0:512], AF.Exp, scale=0.125)
            if pr >= 4 * T:
                i = pr - 4 * T
                TT("pool", E, E.ap[:, :, i * 128:(i + 1) * 128], E, E.ap[:, :, i * 128:(i + 1) * 128],
                   maskp, maskp.ap[:, i % 2, :, :], ALU.mult)
            TT("dve", Es, Es.ap[:, c0:512], E, E.ap[:, 0, c0:512], E, E.ap[:, 1, c0:512], ALU.add)
            if pr < 4 * T:
                if pr % 2 == 0:
                    held[0] = Es
                else:
                    E2 = ES2[(pr // 2) % 2]
                    TT("dve", E2, E2.ap, held[0], held[0].ap, Es, Es.ap, ALU.add)
                    Es = E2
            return (E, Es, k)

        def item_pv(it, E, Es, k):
            h, T, m, pr, npairs = it
            vv = VV[h % 2]
            if pr == 0:
                grp_ctr[0] += 1
            gp = grp_ctr[0] % 2
            aO = ps_f32(4 + gp)
            aS = ps_f32(6 + gp)
            c0 = max(0, pr - 4 * T) * 128
            for jj in range(2):
                j = 2 * pr + jj
                MM(aO, aO.ap[:, c0:512], vv, vv.ap[:, j, :], E, E.ap[:, jj, c0:512],
                   pr == 0 and jj == 0, pr == npairs - 1 and jj == 1)
            if pr < 4 * T:
                if pr % 2 == 1:
                    MM(aS, aS.ap, onesb, onesb.ap, Es, Es.ap, pr == 1, False)
            else:
                MM(aS, aS.ap[:, c0:512], onesb, onesb.ap, Es, Es.ap[:, c0:512], pr == 0, pr == npairs - 1)
            if pr == npairs - 1:
                P.op("dve", lambda e: e.reciprocal(out=c_rs.ap, in_=aS.ap), reads=[aS], writes=[c_rs])
                if m == 0:
                    TT("dve", On[0], On[0].ap, aO, aO.ap, c_rs, c_rs.ap, ALU.mult)
                else:
                    STT("dve", On[1], On[1].ap, aO, aO.ap, neglam.ap[:, 0:1], c_rs, c_rs.ap, ALU.mult, ALU.mult,
                        extra_reads=[neglam])
                    TT("pool", c_of, c_of.ap, On[0], On[0].ap, On[1], On[1].ap, ALU.add)
                    TT("pool", c_sq, c_sq.ap, c_of, c_of.ap, c_of, c_of.ap, ALU.mult)
                    deferred.append([3, h, T, aS])

        def attn_finish(h, T, aS):
            MM(aS, aS.ap, onesb, onesb.ap, c_sq, c_sq.ap, True, True)
            ACT(c_rstd, c_rstd.ap, aS, aS.ap, AF.Ln, scale=1.0 / 128.0, bias=epsb.ap[:, 0:1], extra_reads=[epsb])
            ACT(c_rstd, c_rstd.ap, c_rstd, c_rstd.ap, AF.Exp, scale=-0.5)
            STT("dve", odaT, odaT.ap[:, h, T * 512:(T + 1) * 512], c_of, c_of.ap, gcol.ap[:, 0:1], c_rstd, c_rstd.ap,
                ALU.mult, ALU.mult, extra_reads=[gcol])

        def tick_deferred(force=False):
            for d_ in list(deferred):
                d_[0] -= 1
                if d_[0] <= 0 or force:
                    deferred.remove(d_)
                    attn_finish(d_[1], d_[2], d_[3])

        NHEAD = 8
        c_loads(0)
        LAG = 2
        pend = []
        for h in range(NHEAD):
            while pend:
                item_pv(*pend.pop(0))
            if h + 1 < NHEAD:
                c_loads(h + 1)
            for it in attn_items(h):
                cur = item_qk_exp(it)
                pend.append((it,) + cur)
                if len(pend) > LAG:
                    item_pv(*pend.pop(0))
                tick_deferred()
        while pend:
            item_pv(*pend.pop(0))
        tick_deferred(force=True)
        if debug:
            P.dma("sp", lambda e: e.dma_start(out=oda_d.rearrange("h p t -> p h t"), in_=odaT.ap), reads=[odaT],
                  writes=["D:out"])

    if upto >= 5:
        Da = Alloc(DYN + 32)
        Wba = Da.take([128, 8, 1024], BF16)
        Wg0 = Da.take([128, 8, 1024], BF16)
        Wo = Da.take([128, 8, 1024], BF16)
        d_xb = [Da.take([128, 1024], BF16) for _ in range(4)]
        d_xT = Da.take([128, 8, 512], BF16)
        d_xf = [Da.take([128, 1024], F32) for _ in range(2)]
        d_g0 = Da.take([128, 8, 512], BF16)
        d_m12 = Da.take([128, 8, 512], BF16)
        d_mT = Da.take([128, 8, 512], BF16)
        d_r = [Da.take([128, 1024], F32) for _ in range(2)]
        d_lng = Da.take([128, 1024], F32)
        d_lnb = Da.take([128, 1024], F32)
        d_tmp = [Da.take([128, 512], F32) for _ in range(2)]
        d_stats = Da.take([128, 64], F32, align=4)
        load_w(Wg0, w_in_d, 6 * D, D, key="W0")
        load_w(Wba, w_ba_d, 0, D, key="W1")
        load_w(Wo, w_out_d, 0, D, key="W2")
        bcast_load(d_lng, ln1g_d[0:1, :], "lng")
        bcast_load(d_lnb, ln1b_d[0:1, :], "lnb")

        def d1_tile(T):
            X = d_xT
            x_tile(T, xq_d, d_xb, X)
            P.dma("sp", lambda e: e.dma_start(out=d_m12.ap, in_=m12_d[:, :, T * 512:(T + 1) * 512].rearrange("j p t -> p j t")),
                  reads=["D:m12"], writes=[d_m12], key="dm12")
            for j in range(8):
                ps = fm_proj(Wg0, j * 128, X, X.ap, (1, 2, 3))
                ACT(d_g0, d_g0.ap[:, j, :], ps, ps.ap, AF.Sigmoid)
            for j in range(8):
                ps = fm_proj(Wba, j * 128, odaT, odaT.ap[:, :, T * 512:(T + 1) * 512], (1, 2, 3))
                tm = d_tmp[j % 2]
                TT("dve", tm, tm.ap, ps, ps.ap, d_g0, d_g0.ap[:, j, :], ALU.mult)
                TT("pool", d_mT, d_mT.ap[:, j, :], tm, tm.ap, d_m12, d_m12.ap[:, j, :], ALU.add)
            for blk in range(4):
                r0 = T * 512 + blk * 128
                xf = d_xf[blk % 2]
                P.dma("sp", lambda e, xf=xf, r0=r0: e.dma_start(out=xf.ap, in_=xq_d[r0:r0 + 128, :]), writes=[xf],
                      key="dxf%d" % (blk % 2))
                yps = ps_f32(4 + 2 * (blk % 2), 2)
                for c in range(8):
                    for n in range(2):
                        MM(yps, yps.ap[:, n * 512:(n + 1) * 512], d_mT, d_mT.ap[:, c, blk * 128:(blk + 1) * 128],
                           Wo, Wo.ap[:, c, n * 512:(n + 1) * 512], c == 0, c == 7)
                r = d_r[blk % 2]
                STT("dve", r, r.ap, xf, xf.ap, ALPHA, yps, yps.ap, ALU.mult, ALU.add)
                layer_norm(r, r.ap, r, r.ap, d_lng, d_lnb, d_stats, eng2="dve")
                P.dma("sp", lambda e, r=r, r0=r0: e.dma_start(out=x1_d[r0:r0 + 128, :], in_=r.ap), reads=[r],
                      writes=["D:x1"], key="dr%d" % (blk % 2))

        for T in range(8 if upto >= 5 else 0):
            d1_tile(T)

    if upto >= 6:
        Ea = Alloc(DYN)
        Wfi = Ea.take([128, 8, 2 * DFF], BF16)
        Wfo = Ea.take([128, NF, 1024], BF16)
        e_xb = [Ea.take([128, 1024], BF16) for _ in range(4)]
        e_xT = Ea.take([128, 8, 512], BF16)
        e_hT = Ea.take([128, NF, 512], BF16)
        e_sa = [Ea.take([128, 512], F32) for _ in range(2)]
        e_r = [Ea.take([128, 1024], F32) for _ in range(2)]
        e_stats = Ea.take([128, 64], F32, align=4)
        e_lng = Buf(rv(2 * 2048, [128, 1024], F32).ap, rv(2 * 2048, [128, 1024], F32).res)
        e_lnb = Buf(rv(4 * 2048, [128, 1024], F32).ap, rv(4 * 2048, [128, 1024], F32).res)
        load_w(Wfi, w_fi_d, 0, 2 * DFF, key="W3", piece=1408)
        load_w(Wfo, w_fo_d, 0, D, nchunks=NF, key="W4")
        bcast_load(e_lng, ln2g_d[0:1, :], "lng")
        bcast_load(e_lnb, ln2b_d[0:1, :], "lnb")

        def d2_tile(T):
            X = e_xT
            x_tile(T, x1_d, e_xb, X, extra_reads=["D:x1"])
            for f in range(NF):
                pa = ps_f32(1 + 2 * (f % 2))
                pb = ps_f32(2 + 2 * (f % 2))
                for c in range(8):
                    MM(pa, pa.ap, Wfi, Wfi.ap[:, c, f * 128:(f + 1) * 128], X, X.ap[:, c, :], c == 0, c == 7)
                for c in range(8):
                    MM(pb, pb.ap, Wfi, Wfi.ap[:, c, DFF + f * 128:DFF + (f + 1) * 128], X, X.ap[:, c, :], c == 0, c == 7)
                sa_ = e_sa[f % 2]
                ACT(sa_, sa_.ap, pa, pa.ap, AF.Silu)
                TT("dve", e_hT, e_hT.ap[:, f, :], sa_, sa_.ap, pb, pb.ap, ALU.mult)
            for blk in range(4):
                r0 = T * 512 + blk * 128
                r = e_r[blk % 2]
                P.dma("sp", lambda e, r=r, r0=r0: e.dma_start(out=r.ap, in_=x1_d[r0:r0 + 128, :]), reads=["D:x1"],
                      writes=[r])
                yps = ps_f32(5, 2)
                for f in range(NF):
                    for n in range(2):
                        MM(yps, yps.ap[:, n * 512:(n + 1) * 512], e_hT, e_hT.ap[:, f, blk * 128:(blk + 1) * 128],
                           Wfo, Wfo.ap[:, f, n * 512:(n + 1) * 512], f == 0, f == NF - 1)
                STT("dve", r, r.ap, r, r.ap, ALPHA, yps, yps.ap, ALU.mult, ALU.add)
                layer_norm(r, r.ap, r, r.ap, e_lng, e_lnb, e_stats, eng2="dve")
                P.dma("sp", lambda e, r=r, r0=r0: e.dma_start(out=out_d[r0:r0 + 128, :], in_=r.ap), reads=[r],
                      writes=["D:out"], key="er%d" % (blk % 2))

        for T in range(8 if upto >= 6 else 0):
            d2_tile(T)

    P.op("sp", lambda e: e.nop(), reads=["D:out"])
    P.finalize(st)
    with nc.Block() as block:
        P.emit(block)
    st.close()
    return nc, P, st


def host_inputs(inputs):
    x = np.asarray(inputs["x"], dtype=np.float32)
    mem = np.asarray(inputs["mem"], dtype=np.float32)
    pos = np.asarray(inputs["positions"], dtype=np.int32)
    ident = np.eye(128, dtype=np.float32)
    tril = np.tril(np.ones((128, 128), dtype=np.float32))
    tri_kq = np.ascontiguousarray(tril.T)
    ones = np.ones((128, 128), np.float32)
    zeros = np.zeros((128, 128), np.float32)
    invf = (500000.0 ** (-np.arange(8, dtype=np.float32) * 2.0 / 16.0)).astype(np.float32).reshape(1, 8)
    shared = {}
    for k in ("w_in", "w_mem_kv", "w_br_attn", "w_br_sg", "w_br_mem", "w_out", "w_ffn_in", "w_ffn_out"):
        shared[k] = np.ascontiguousarray(np.asarray(inputs[k], dtype=np.float32)[0])
    for k in ("lambda_q1", "lambda_k1", "lambda_q2", "lambda_k2", "da_subln_g", "sg_norm_g", "sg_norm_b",
              "ln1_g", "ln1_b", "ln2_g", "ln2_b"):
        shared[k] = np.ascontiguousarray(np.asarray(inputs[k], dtype=np.float32)[0].reshape(1, -1))
    shared["sg_w_s"] = np.ascontiguousarray(np.asarray(inputs["sg_w_s"], dtype=np.float32)[0])
    shared["sg_b_s"] = np.ascontiguousarray(np.asarray(inputs["sg_b_s"], dtype=np.float32)[0].reshape(1, -1))
    shared["c_ident"] = ident
    shared["c_tril"] = tril
    shared["c_invf"] = invf
    in_maps = []
    for c in range(8):
        b, half = c // 2, c % 2
        blocks = q_blocks(half)
        xb = np.ascontiguousarray(x[b])
        xq = np.ascontiguousarray(x[b].reshape(NBLK, 128, D)[blocks].reshape(NQ, D))
        pb = pos[b].reshape(NBLK, 128)
        posb = np.ascontiguousarray(pb.T)
        posq = np.ascontiguousarray(pb[blocks].T)
        if half == 0:
            mp = np.stack([np.stack([tri_kq, zeros], 1), np.stack([ones, tri_kq], 1)], 1)
        else:
            mp = np.stack([np.stack([ones, tri_kq], 1), np.stack([tri_kq, zeros], 1)], 1)
        m = dict(shared)
        m.update(xb=xb, xq=xq, posb=posb, posq=posq, mem=np.ascontiguousarray(mem[b]),
                 c_maskp=np.ascontiguousarray(mp.reshape(128, 512)))
        in_maps.append(m)
    return in_maps


_CACHE = {}


def kernel(**inputs):
    in_maps = host_inputs(inputs)
    if "nc" not in _CACHE:
        _CACHE["nc"] = build_program(debug=False)[0]
    nc = _CACHE["nc"]
    res = run_bass_kernel_spmd(nc, in_maps, core_ids=list(range(8)))
    out = np.zeros((4, SEQ, D), dtype=np.float32)
    for c in range(8):
        b, half = c // 2, c % 2
        blocks = q_blocks(half)
        o = np.asarray(res.results[c]["out"]).reshape(NSLOT, 128, D)
        out[b].reshape(NBLK, 128, D)[blocks] = o
    return out
```

```python
import math
from contextlib import ExitStack
import numpy as np
import concourse.bass as bass
import concourse.mybir as mybir
from concourse.bass_utils import run_bass_kernel_spmd

F32 = mybir.dt.float32
BF16 = mybir.dt.bfloat16
I32 = mybir.dt.int32
AF = mybir.ActivationFunctionType
ALU = mybir.AluOpType
AX = mybir.AxisListType

D = 1024
SEQ = 8192
NBLK = 64
NSLOT = 32
NQ = NSLOT * 128
DFF = 2816
NF = DFF // 128
ALPHA = 2.0 ** 0.25
LN_EPS = 1e-5
RMS_EPS = 1e-5
LAMBDA_INIT = 0.8 - 0.6 * math.exp(0.0)
TWO_PI = 2.0 * math.pi
GELU_C = 0.7978845608028654
NG = 103
CONST_G = 8
DBG = {}
IN_SHAPES = {}


def q_blocks(half):
    out = []
    for s in range(NSLOT):
        g = s // 2
        if half == 0:
            out.append(4 * g + (0 if s % 2 == 0 else 3))
        else:
            out.append(4 * g + (1 if s % 2 == 0 else 2))
    return out


class Buf:
    def __init__(self, ap, res):
        self.ap = ap
        self.res = tuple(res)

    def __getitem__(self, k):
        return self.ap[k]


class Prog:
    ENGS = ("pe", "act", "dve", "pool", "sp")
    CH = 6000

    def __init__(self, nc):
        self.nc = nc
        self.ops = []

    @staticmethod
    def _res(lst):
        out = []
        for x in lst:
            if isinstance(x, Buf):
                out.extend(x.res)
            elif isinstance(x, str):
                out.append(x)
            else:
                out.extend(x)
        return tuple(out)

    def op(self, eng, fn, reads=(), writes=()):
        self.ops.append(dict(eng=eng, fn=fn, reads=self._res(reads), writes=self._res(writes), dma=False, sig=None))

    def dma(self, eng, fns, reads=(), writes=(), key=None):
        if not isinstance(fns, (list, tuple)):
            fns = [fns]
        key = None
        for x in list(writes) + list(reads):
            if isinstance(x, Buf):
                key = x.res[0] + ("s" if eng == "pool" else "h")
                break
        assert key is not None
        self.ops.append(dict(eng=eng, fn=list(fns), reads=self._res(reads), writes=self._res(writes), dma=True,
                             key=key, sig=None))

    def finalize(self, stack):
        nc = self.nc
        ops = self.ops
        lastw = {}
        readers = {}
        deps = [None] * len(ops)
        for n, o in enumerate(ops):
            d = {}
            for r in o["reads"]:
                for w in lastw.get(r, ()):
                    d.setdefault(w, "raw")
                if r[0] == "P" and not o["dma"]:
                    for tag, rd in readers.get(r, {}).items():
                        if tag != o["eng"] and rd not in d:
                            d[rd] = "rar"
            for w in o["writes"]:
                comm = w.startswith("D:")
                if not comm:
                    for pw in lastw.get(w, ()):
                        d.setdefault(pw, "waw")
                for rd in readers.get(w, {}).values():
                    if rd not in d:
                        d[rd] = "war"
            dd = []
            for j, kind in d.items():
                if j == n:
                    continue
                oj = ops[j]
                if (not oj["dma"]) and (not o["dma"]) and oj["eng"] == o["eng"]:
                    if o["eng"] == "pe":
                        continue
                dd.append(j)
            deps[n] = dd
            for r in o["reads"]:
                tag = ("dma", o["key"]) if o["dma"] else o["eng"]
                readers.setdefault(r, {})[tag] = n
            for w in o["writes"]:
                if w.startswith("D:"):
                    lastw.setdefault(w, []).append(n)
                else:
                    lastw[w] = [n]
                readers[w] = {}
        for n in range(len(ops)):
            for j in deps[n]:
                ops[j]["need"] = True
        cnt = {e: 0 for e in self.ENGS}
        dcnt = {}
        for o in ops:
            if o["dma"]:
                k = o["key"]
                dcnt[k] = dcnt.get(k, 0) + 16 * len(o["fn"])
                o["sig"] = dcnt[k]
            elif o.get("need"):
                o["sig"] = (cnt[o["eng"]] // self.CH, cnt[o["eng"]] % self.CH + 1)
                cnt[o["eng"]] += 1
        self.esem = {(e, c): stack.enter_context(nc.semaphore("s_%s%d" % (e, c)))
                     for e in self.ENGS for c in range(cnt[e] // self.CH + 1)}
        self.dsem = {k: stack.enter_context(nc.semaphore("d_%d" % i)) for i, k in enumerate(dcnt.keys())}
        known = {e: {} for e in self.ENGS}
        per_eng = {e: [] for e in self.ENGS}
        snap = {}
        for n, o in enumerate(ops):
            E = o["eng"]
            kn = known[E]
            waits = []
            for j in sorted(deps[n], reverse=True):
                oj = ops[j]
                if oj["dma"]:
                    sk, v = ("d", oj["key"]), (0, oj["sig"])
                else:
                    sk, v = ("e", oj["eng"]), oj["sig"]
                if kn.get(sk, (0, 0)) >= v:
                    continue
                kn[sk] = v
                waits.append((self.dsem[sk[1]] if sk[0] == "d" else self.esem[(sk[1], v[0])], v[1]))
                for k2, v2 in snap.get(j, {}).items():
                    if v2 > kn.get(k2, (0, 0)):
                        kn[k2] = v2
            if o["sig"] is not None:
                snap[n] = dict(kn)
            per_eng[E].append((waits, o))
        self.per_eng = per_eng
        self.counts = (cnt, dcnt)

    def emit(self, block):
        engmap = {"pe": block.tensor, "act": block.scalar, "dve": block.vector, "pool": block.gpsimd,
                  "sp": block.sync}
        for E in self.ENGS:
            items = self.per_eng[E]
            esem = self.esem
            dsem = self.dsem

            def body(eng, items=items, esem=esem, E=E):
                for waits, o in items:
                    for sem, v in waits:
                        eng.wait_ge(sem, v)
                    if o["dma"]:
                        for f in o["fn"]:
                            f(eng).then_inc(dsem[o["key"]], 16)
                    else:
                        ins = o["fn"](eng)
                        if o["sig"] is not None:
                            ins.then_inc(esem[(E, o["sig"][0])], 1)
            engmap[E](body)


def build_program(debug=False, upto=6):
    nc = bass.Bass("TRN2", target_bir_lowering=False)

    IN_PHASE = dict(xb=1, xq=2, mem=3, w_in=1, w_mem_kv=3, w_br_attn=5, w_br_sg=2, w_br_mem=3, w_out=5,
                    w_ffn_in=6, w_ffn_out=6, sg_norm_g=2, sg_norm_b=2, sg_w_s=2, sg_b_s=2, c_tril=2,
                    ln1_g=5, ln1_b=5, ln2_g=6, ln2_b=6)
    IN_SHAPES.clear()

    def din(name, shape, dt=F32):
        if upto < IN_PHASE.get(name, 0):
            shape = [1] * (len(shape) - 1) + [2]
        IN_SHAPES[name] = tuple(shape)
        return nc.dram_tensor(name, list(shape), dt, kind="ExternalInput").ap()

    def dscr(name, shape, dt):
        kind = "ExternalOutput" if debug else "Internal"
        return nc.dram_tensor(name, list(shape), dt, kind=kind).ap()

    xb_d = din("xb", [SEQ, D])
    xq_d = din("xq", [NQ, D])
    posb_d = din("posb", [128, NBLK], I32)
    posq_d = din("posq", [128, NSLOT], I32)
    mem_d = din("mem", [256, D])
    w_in_d = din("w_in", [D, 9 * D])
    w_mkv_d = din("w_mem_kv", [D, 2 * D])
    w_ba_d = din("w_br_attn", [D, D])
    w_bs_d = din("w_br_sg", [D, D])
    w_bm_d = din("w_br_mem", [D, D])
    w_out_d = din("w_out", [D, D])
    w_fi_d = din("w_ffn_in", [D, 2 * DFF])
    w_fo_d = din("w_ffn_out", [DFF, D])
    lq1_d = din("lambda_q1", [1, 64]); lk1_d = din("lambda_k1", [1, 64])
    lq2_d = din("lambda_q2", [1, 64]); lk2_d = din("lambda_k2", [1, 64])
    dag_d = din("da_subln_g", [1, 128])
    sgg_d = din("sg_norm_g", [1, D]); sgb_d = din("sg_norm_b", [1, D])
    sws_d = din("sg_w_s", [8, 128, 128]); sbs_d = din("sg_b_s", [1, 8 * 128])
    ln1g_d = din("ln1_g", [1, D]); ln1b_d = din("ln1_b", [1, D])
    ln2g_d = din("ln2_g", [1, D]); ln2b_d = din("ln2_b", [1, D])
    ident_d = din("c_ident", [128, 128])
    tril_d = din("c_tril", [128, 128])
    maskp_d = din("c_maskp", [128, 2 * 2 * 128])
    invf_d = din("c_invf", [1, 8])
    out_d = nc.dram_tensor("out", [NQ, D], F32, kind="ExternalOutput").ap()

    kT_d = dscr("kT_s", [8, 128, SEQ], BF16)
    v_d = dscr("v_s", [SEQ, D], BF16)
    qT_d = dscr("qT_s", [8, 128, NQ], BF16)
    m1_d = dscr("m1_s", [8, 128, NQ], BF16)
    m12_d = dscr("m12_s", [8, 128, NQ], BF16)
    x1_d = dscr("x1_s", [NQ, D], F32)
    oda_d = dscr("oda_s", [8, 128, NQ], BF16) if debug else None

    st = ExitStack()
    region = st.enter_context(nc.sbuf_tensor("region", [128, NG * 1024], BF16))
    psum = st.enter_context(nc.psum_tensor("psum", [128, 4096], F32))
    P = Prog(nc)

    def rv(off_b, shape, dt):
        esz = 4 if dt in (F32, I32) else 2
        n = int(np.prod(shape[1:]))
        nb = n * esz
        assert off_b % 4 == 0 and off_b + nb <= NG * 2048, (off_b, shape)
        a = region[0:shape[0], off_b // 2: (off_b + nb) // 2]
        if dt != BF16:
            a = a.bitcast(dt)
        if len(shape) == 3:
            a = a.rearrange("p (a b) -> p a b", a=shape[1])
        elif len(shape) == 4:
            a = a.rearrange("p (a b c) -> p a b c", a=shape[1], b=shape[2])
        res = ["G%d" % g for g in range(off_b // 2048, (off_b + nb - 1) // 2048 + 1)]
        return Buf(a, res)

    class Alloc:
        def __init__(self, g0, g1=NG):
            self.off = g0 * 2048
            self.end = g1 * 2048

        def take(self, shape, dt, align=2048):
            self.off = (self.off + align - 1) // align * align
            b = rv(self.off, shape, dt)
            esz = 4 if dt in (F32, I32) else 2
            self.off += int(np.prod(shape[1:])) * esz
            assert self.off <= self.end, ("region overflow", self.off, self.end)
            return b

    def pbank(b, nb=1):
        return ["P%d" % i for i in range(b, b + nb)]

    def ps_f32(b, nb=1):
        return Buf(psum[:, b * 512:(b + nb) * 512], pbank(b, nb))

    def ps_bf(b):
        return Buf(psum[:, b * 512:(b + 1) * 512].bitcast(BF16).rearrange("p (a b) -> p a b", a=8), pbank(b))

    def MM(O, o_ap, L, l_ap, R, r_ap, start, stop):
        P.op("pe", lambda e: e.matmul(o_ap, lhsT=l_ap, rhs=r_ap, start=start, stop=stop), reads=[L, R], writes=[O])

    def TR(O, o_ap, I, i_ap):
        P.op("pe", lambda e: e.transpose(out=o_ap, in_=i_ap, identity=identb.ap), reads=[I, identb], writes=[O])

    def ACT(O, o_ap, I, i_ap, func, scale=1.0, bias=0.0, extra_reads=()):
        P.op("act", lambda e: e.activation(out=o_ap, in_=i_ap, func=func, bias=bias, scale=scale),
             reads=[I] + list(extra_reads), writes=[O])

    def COPY(eng, O, o_ap, I, i_ap):
        if eng == "act":
            P.op("act", lambda e: e.copy(out=o_ap, in_=i_ap), reads=[I], writes=[O])
        else:
            P.op(eng, lambda e: e.tensor_copy(out=o_ap, in_=i_ap), reads=[I], writes=[O])

    def TT(eng, O, o_ap, A, a_ap, B, b_ap, op):
        P.op(eng, lambda e: e.tensor_tensor(out=o_ap, in0=a_ap, in1=b_ap, op=op), reads=[A, B], writes=[O])

    def TS(eng, O, o_ap, A, a_ap, s1, s2, op0, op1=None, extra_reads=()):
        if op1 is None:
            P.op(eng, lambda e: e.tensor_scalar(out=o_ap, in0=a_ap, scalar1=s1, scalar2=None, op0=op0),
                 reads=[A] + list(extra_reads), writes=[O])
        else:
            P.op(eng, lambda e: e.tensor_scalar(out=o_ap, in0=a_ap, scalar1=s1, scalar2=s2, op0=op0, op1=op1),
                 reads=[A] + list(extra_reads), writes=[O])

    def STT(eng, O, o_ap, A, a_ap, scalar, B, b_ap, op0, op1, extra_reads=()):
        P.op(eng, lambda e: e.scalar_tensor_tensor(out=o_ap, in0=a_ap, scalar=scalar, in1=b_ap, op0=op0, op1=op1),
             reads=[A, B] + list(extra_reads), writes=[O])

    def load_w(Wb, src, col0, ncols, nchunks=8, row0=0, key=None, piece=1024):
        fns = []
        for c in range(nchunks):
            for q0 in range(0, ncols, piece):
                q1 = min(ncols, q0 + piece)
                fns.append(lambda e, c=c, q0=q0, q1=q1: e.dma_start(
                    out=Wb.ap[:, c, q0:q1], in_=src[row0 + c * 128: row0 + (c + 1) * 128, col0 + q0: col0 + q1]))
        P.dma("pool", fns, writes=[Wb], key=key)

    def bcast_load(dst, src_row, key):
        P.dma("sp", lambda e: e.dma_start(out=dst.ap, in_=src_row.partition_broadcast(128)), writes=[dst], key=key)

    ca = Alloc(0, CONST_G)
    identb = ca.take([128, 128], BF16, align=4)
    maskp = ca.take([128, 2, 2, 128], BF16, align=4)
    gcol = ca.take([128, 1], F32, align=4)
    onesb = ca.take([128, 128], BF16, align=4)
    neglam = ca.take([128, 1], F32, align=4)
    epsb = ca.take([128, 1], F32, align=4)
    ca.off = 2 * 2048
    trig_b = ca.take([128, NBLK, 32], F32)
    trig_q = ca.take([128, NSLOT, 32], F32)
    DYN = CONST_G

    sa = Alloc(NG - 12)
    posb_i = sa.take([128, NBLK], I32, align=4)
    posq_i = sa.take([128, NSLOT], I32, align=4)
    invf = sa.take([128, 8], F32, align=4)
    lamv = sa.take([128, 4, 64], F32, align=4)
    lprod = sa.take([128, 2, 64], F32, align=4)
    ls = sa.take([128, 4], F32, align=4)
    posf = sa.take([128, NBLK], F32, align=4)
    ang = sa.take([128, NBLK, 8], F32)
    tr1 = sa.take([128, NBLK, 8], F32)
    tr2 = sa.take([128, NBLK, 8], F32)
    tri = sa.take([128, NBLK, 8], I32)

    P.dma("pool", [lambda e: e.dma_start(out=identb.ap, in_=ident_d[:, :]),
                   lambda e: e.dma_start(out=maskp.ap.rearrange("p a b c -> p (a b c)"), in_=maskp_d[:, :])],
          writes=[identb, maskp], key="cst_cast")
    P.dma("sp", [lambda e: e.dma_start(out=posb_i.ap, in_=posb_d[:, :]),
                 lambda e: e.dma_start(out=posq_i.ap, in_=posq_d[:, :]),
                 lambda e: e.dma_start(out=invf.ap, in_=invf_d[0:1, :].partition_broadcast(128)),
                 lambda e: e.dma_start(out=gcol.ap, in_=dag_d[0:1, :].rearrange("o d -> d o")),
                 lambda e: e.dma_start(out=lamv.ap[:, 0, :], in_=lq1_d[0:1, :].partition_broadcast(128)),
                 lambda e: e.dma_start(out=lamv.ap[:, 1, :], in_=lk1_d[0:1, :].partition_broadcast(128)),
                 lambda e: e.dma_start(out=lamv.ap[:, 2, :], in_=lq2_d[0:1, :].partition_broadcast(128)),
                 lambda e: e.dma_start(out=lamv.ap[:, 3, :], in_=lk2_d[0:1, :].partition_broadcast(128))],
          writes=[posb_i, posq_i, invf, gcol, lamv], key="cst_sp")
    P.op("pool", lambda e: e.memset(onesb.ap, 1.0), writes=[onesb])
    P.op("pool", lambda e: e.memset(epsb.ap, RMS_EPS), writes=[epsb])
    lv4 = lamv.ap.rearrange("p (a b) c -> p a b c", a=2)
    TT("dve", lprod, lprod.ap, lamv, lv4[:, :, 0, :], lamv, lv4[:, :, 1, :], ALU.mult)
    P.op("dve", lambda e: e.tensor_reduce(out=ls.ap[:, 0:2], in_=lprod.ap, axis=AX.X, op=ALU.add),
         reads=[lprod], writes=[ls])
    ACT(ls, ls.ap[:, 2:4], ls, ls.ap[:, 0:2], AF.Exp)
    TT("dve", neglam, neglam.ap, ls, ls.ap[:, 3:4], ls, ls.ap[:, 2:3], ALU.subtract)
    TS("dve", neglam, neglam.ap, neglam, neglam.ap, -LAMBDA_INIT, None, ALU.add)
    TS("dve", gcol, gcol.ap, gcol, gcol.ap, 1.0 - LAMBDA_INIT, None, ALU.mult)

    def trig_tables(pos_i, nb, table):
        pf = Buf(posf.ap[:, 0:nb], posf.res)
        A_ = Buf(ang.ap[:, 0:nb, :], ang.res)
        t1 = Buf(tr1.ap[:, 0:nb, :], tr1.res)
        t2 = Buf(tr2.ap[:, 0:nb, :], tr2.res)
        ti = Buf(tri.ap[:, 0:nb, :], tri.res)
        COPY("dve", pf, pf.ap, pos_i, pos_i.ap)
        TT("dve", A_, A_.ap, pf, pf.ap.unsqueeze(2).broadcast_to([128, nb, 8]),
           invf, invf.ap.unsqueeze(1).broadcast_to([128, nb, 8]), ALU.mult)

        def red_sin(dst_ap, shift):
            TS("dve", t1, t1.ap, A_, A_.ap, shift, None, ALU.add)
            TS("dve", t2, t2.ap, t1, t1.ap, 1.0 / TWO_PI, None, ALU.mult)
            COPY("dve", ti, ti.ap, t2, t2.ap)
            COPY("dve", t2, t2.ap, ti, ti.ap)
            STT("dve", t1, t1.ap, t2, t2.ap, -TWO_PI, t1, t1.ap, ALU.mult, ALU.add)
            TS("dve", t2, t2.ap, t1, t1.ap, math.pi, -TWO_PI, ALU.is_ge, ALU.mult)
            TT("dve", t1, t1.ap, t1, t1.ap, t2, t2.ap, ALU.add)
            TS("dve", t2, t2.ap, t1, t1.ap, -math.pi, TWO_PI, ALU.is_lt, ALU.mult)
            TT("dve", t1, t1.ap, t1, t1.ap, t2, t2.ap, ALU.add)
            TS("dve", t1, t1.ap, t1, t1.ap, 0.999999, None, ALU.mult)
            ACT(table, dst_ap, t1, t1.ap, AF.Sin)
        red_sin(table.ap[:, :, 24:32], 0.0)
        red_sin(table.ap[:, :, 0:8], math.pi / 2.0)
        COPY("dve", table, table.ap[:, :, 8:16], table, table.ap[:, :, 0:8])
        TS("dve", table, table.ap[:, :, 16:24], table, table.ap[:, :, 24:32], -1.0, None, ALU.mult)
    trig_tables(posb_i, NBLK, trig_b)
    trig_tables(posq_i, NSLOT, trig_q)

    def rope(Z, z_ap, OB, table, blk, RA, RB):
        kz = z_ap.rearrange("p (g d) -> p g d", g=16)
        ob3 = OB.ap.rearrange("p (g d) -> p g d", g=16)
        cs = table.ap[:, blk, 0:16].unsqueeze(1).broadcast_to([128, 16, 16])
        slo = table.ap[:, blk, 16:24].unsqueeze(1).broadcast_to([128, 16, 8])
        shi = table.ap[:, blk, 24:32].unsqueeze(1).broadcast_to([128, 16, 8])
        TT("dve", RA, RA.ap, Z, kz[:, :, 0:16], table, cs, ALU.mult)
        TT("dve", RB, RB.ap[:, :, 0:8], Z, kz[:, :, 8:16], table, slo, ALU.mult)
        TT("dve", RB, RB.ap[:, :, 8:16], Z, kz[:, :, 0:8], table, shi, ALU.mult)
        TT("dve", OB, ob3[:, :, 0:16], RA, RA.ap, RB, RB.ap, ALU.add)

    stat_ctr = [0]

    def layer_norm(R, r_ap, O, o_ap, g, b, STATS, eps=LN_EPS, eng2="pool"):
        k = stat_ctr[0] % 4
        stat_ctr[0] += 1
        sv_ = STATS.ap[:, k * 16:(k + 1) * 16]
        S = STATS
        P.op("dve", lambda e: e.bn_stats(out=sv_[:, 0:6], in_=r_ap[:, 0:512]), reads=[R], writes=[S])
        P.op("dve", lambda e: e.bn_stats(out=sv_[:, 6:12], in_=r_ap[:, 512:1024]), reads=[R], writes=[S])
        P.op("dve", lambda e: e.bn_aggr(out=sv_[:, 12:14], in_=sv_[:, 0:12]), reads=[S], writes=[S])
        ACT(S, sv_[:, 14:15], S, sv_[:, 13:14], AF.Sqrt, scale=1.0, bias=eps)
        P.op("dve", lambda e: e.reciprocal(out=sv_[:, 15:16], in_=sv_[:, 14:15]), reads=[S], writes=[S])
        TS("dve", R, r_ap, R, r_ap, sv_[:, 12:13], sv_[:, 15:16], ALU.subtract, ALU.mult, extra_reads=[S])
        TT(eng2, R, r_ap, R, r_ap, g, g.ap, ALU.mult)
        TT(eng2, O, o_ap, R, r_ap, b, b.ap, ALU.add)

    def load_xT(src_rows_ap, XB, key, tpb, XT, xt_ap, evac="act"):
        P.dma("pool", lambda e: e.dma_start(out=XB.ap, in_=src_rows_ap), writes=[XB], key=key)
        tp = ps_bf(tpb)
        for c in range(8):
            TR(tp, tp.ap[:, c, :], XB, XB.ap[:, c * 128:(c + 1) * 128])
        COPY(evac, XT, xt_ap, tp, tp.ap)

    def x_issue(src_rows_ap, XB, extra_reads=()):
        P.dma("pool", lambda e: e.dma_start(out=XB.ap, in_=src_rows_ap), reads=list(extra_reads), writes=[XB])

    def x_trans(XB, tpb, XT, xt_ap, evac="act"):
        tp = ps_bf(tpb)
        for c in range(8):
            TR(tp, tp.ap[:, c, :], XB, XB.ap[:, c * 128:(c + 1) * 128])
        COPY(evac, XT, xt_ap, tp, tp.ap)

    def x_tile(T, src_d, XBs, X, extra_reads=()):
        if T == 0:
            for blk in range(4):
                x_issue(src_d[blk * 128:(blk + 1) * 128, :], XBs[blk], extra_reads)
        for blk in range(4):
            x_trans(XBs[blk], 0, X, X.ap[:, :, blk * 128:(blk + 1) * 128])
            if T + 1 < 8:
                r1 = (T + 1) * 512 + blk * 128
                x_issue(src_d[r1:r1 + 128, :], XBs[blk], extra_reads)

    fm_ctr = [0]

    def fm_proj(W, col0, X, x_ap, banks):
        b = banks[fm_ctr[0] % len(banks)]
        fm_ctr[0] += 1
        ps = ps_f32(b)
        n = x_ap.shape[-1]
        for c in range(8):
            MM(ps, ps.ap[:, 0:n], W, W.ap[:, c, col0:col0 + 128], X, x_ap[:, c, :], c == 0, c == 7)
        return ps

    A = Alloc(DYN)
    Wk = A.take([128, 8, 1024], BF16)
    Wv = A.take([128, 8, 1024], BF16)
    if upto >= 1:
        load_w(Wk, w_in_d, 1 * D, D, key="W0")
        load_w(Wv, w_in_d, 2 * D, D, key="W1")
    a_xb = [A.take([128, 1024], BF16) for _ in range(3)]
    a_xT = [A.take([128, 8, 128], BF16) for _ in range(2)]
    a_kb = [A.take([128, 1024], BF16) for _ in range(2)]
    a_vb = [A.take([128, 1024], BF16) for _ in range(2)]
    a_kT = [A.take([128, 8, 512], BF16) for _ in range(2)]
    a_RA = A.take([128, 16, 16], F32, align=4)
    a_RB = A.take([128, 16, 16], F32, align=4)

    def a1_A(t):
        load_xT(xb_d[t * 128:(t + 1) * 128, :], a_xb[t % 3], "xb%d" % (t % 3), t % 2, a_xT[t % 2], a_xT[t % 2].ap)

    def a1_B(t):
        X = a_xT[t % 2]
        kps = ps_f32(2, 2)
        vps = ps_f32(4, 2)
        for (O, W) in ((kps, Wk), (vps, Wv)):
            for c in range(8):
                for n in range(2):
                    MM(O, O.ap[:, n * 512:(n + 1) * 512], X, X.ap[:, c, :], W,
                       W.ap[:, c, n * 512:(n + 1) * 512], c == 0, c == 7)
        kb = a_kb[t % 2]
        COPY("act", kb, kb.ap, kps, kps.ap)
        if DBG.get('b', 9) >= 2:
            rope(kps, kps.ap, kb, trig_b, t, a_RA, a_RB)
        vb = a_vb[t % 2]
        if DBG.get('b', 9) >= 3:
            COPY("dve", vb, vb.ap, vps, vps.ap)
        if DBG.get('b', 9) >= 4:
            P.dma("sp", lambda e: e.dma_start(out=v_d[t * 128:(t + 1) * 128, :], in_=vb.ap), reads=[vb],
                  writes=["D:v"], key="vb%d" % (t % 2))

    def a1_C(t):
        kb = a_kb[t % 2]
        tk = ps_bf(6 + t % 2)
        for h in range(8):
            TR(tk, tk.ap[:, h, :], kb, kb.ap[:, h * 128:(h + 1) * 128])
        KT = a_kT[(t // 4) % 2]
        COPY("dve", KT, KT.ap[:, :, (t % 4) * 128:(t % 4 + 1) * 128], tk, tk.ap)
        if t % 4 == 3:
            t0 = (t - 3) * 128
            P.dma("sp", lambda e: e.dma_start(out=kT_d[:, :, t0:t0 + 512].rearrange("h p t -> p h t"), in_=KT.ap),
                  reads=[KT], writes=["D:kT"], key="kTt%d" % ((t // 4) % 2))

    Bw = Alloc(DYN + 40)
    Wq = Bw.take([128, 8, 1024], BF16)
    Wsu = Bw.take([128, 8, 1024], BF16)
    Wsv = Bw.take([128, 8, 1024], BF16)
    Wg1 = Bw.take([128, 8, 1024], BF16)
    Wbs = Bw.take([128, 8, 1024], BF16)
    if upto >= 2:
        load_w(Wq, w_in_d, 0, D, key="W2")
        load_w(Wsv, w_in_d, 4 * D, D, key="W4")
        load_w(Wsu, w_in_d, 3 * D, D, key="W3")

    NB_A1 = NBLK if upto >= 1 else 0
    for t in range(NB_A1 + 2):
        if t < NB_A1:
            a1_A(t)
        if 1 <= t <= NB_A1 and DBG.get('a1', 3) >= 2:
            a1_B(t - 1)
        if t >= 2 and DBG.get('a1', 3) >= 3:
            a1_C(t - 2)
        if t == 8 and upto >= 2:
            load_w(Wg1, w_in_d, 6 * D + 1 * D, D, key="W5")
            load_w(Wbs, w_bs_d, 0, D, key="W6")

    if upto >= 2:
        Bt = Alloc(DYN, DYN + 40)
        b_xb = [Bt.take([128, 1024], BF16) for _ in range(4)]
        b_xT = [Bt.take([128, 8, 512], BF16) for _ in range(2)]
        b_uT = Bt.take([128, 8, 512], BF16)
        b_og = Bt.take([128, 8, 512], BF16)
        b_g1 = Bt.take([128, 8, 512], BF16)
        b_m1 = Bt.take([128, 8, 512], BF16)
        b_qT = Bt.take([128, 8, 512], BF16)
        b_qb = Bt.take([128, 1024], BF16)
        b_vn = [Bt.take([128, 1024], BF16) for _ in range(4)]
        Bt2 = Alloc(DYN + 80)
        b_gl = [Bt2.take([128, 1024], F32) for _ in range(2)]
        b_st = Bt2.take([128, 8, 128], F32)
        b_lng = Bt2.take([128, 1024], F32)
        b_lnb = Bt2.take([128, 1024], F32)
        b_bsB = Bt2.take([128, 8, 128], F32)
        b_WsT = Bt2.take([128, 8, 128], BF16)
        b_RA = Bt2.take([128, 16, 16], F32, align=4)
        b_RB = Bt2.take([128, 16, 16], F32, align=4)
        b_stats = Bt2.take([128, 64], F32, align=4)

        bcast_load(b_lng, sgg_d[0:1, :], "lng")
        bcast_load(b_lnb, sgb_d[0:1, :], "lnb")
        bcast_load(Buf(b_bsB.ap.rearrange("p a b -> p (a b)"), b_bsB.res), sbs_d[0:1, :], "bsB")
        wsf = Buf(b_gl[0].ap.rearrange("p (a b) -> p a b", a=8), b_gl[0].res)
        trf = Buf(b_gl[1].ap[:, 0:128], b_gl[1].res)
        wsb = Buf(b_qb.ap.rearrange("p (a b) -> p a b", a=8), b_qb.res)
        P.dma("sp", [lambda e: e.dma_start(out=wsf.ap, in_=sws_d.rearrange("g t s -> t g s")),
                     lambda e: e.dma_start(out=trf.ap, in_=tril_d[:, :])], writes=[wsf, trf], key="wsf")
        TT("dve", wsb, wsb.ap, wsf, wsf.ap, trf, trf.ap.unsqueeze(1).broadcast_to([128, 8, 128]), ALU.mult)
        tpw = ps_bf(0)
        for g in range(8):
            TR(tpw, tpw.ap[:, g, :], wsb, wsb.ap[:, g, :])
        COPY("act", b_WsT, b_WsT.ap, tpw, tpw.ap)

        def a2a_tile(T):
            X = b_xT[T % 2]
            x_tile(T, xq_d, b_xb, X)
            tok = ps_f32(1, 2)
            for blk in range(4):
                Xb = X.ap[:, :, blk * 128:(blk + 1) * 128]
                for c in range(8):
                    for n in range(2):
                        MM(tok, tok.ap[:, n * 512:(n + 1) * 512], X, Xb[:, c, :], Wq, Wq.ap[:, c, n * 512:(n + 1) * 512],
                           c == 0, c == 7)
                COPY("act", b_qb, b_qb.ap, tok, tok.ap)
                rope(tok, tok.ap, b_qb, trig_q, T * 4 + blk, b_RA, b_RB)
                for j in (2 * blk, 2 * blk + 1):
                    ps = fm_proj(Wsu, j * 128, X, X.ap, (6, 7))
                    ACT(b_uT, b_uT.ap[:, j, :], ps, ps.ap, AF.Gelu_apprx_tanh)
                tq = ps_bf(3)
                for h in range(8):
                    TR(tq, tq.ap[:, h, :], b_qb, b_qb.ap[:, h * 128:(h + 1) * 128])
                COPY("dve", b_qT, b_qT.ap[:, :, blk * 128:(blk + 1) * 128], tq, tq.ap)
            P.dma("sp", lambda e: e.dma_start(out=qT_d[:, :, T * 512:(T + 1) * 512].rearrange("h p t -> p h t"),
                                              in_=b_qT.ap), reads=[b_qT], writes=["D:qT"], key="bqT")
            for blk in range(4):
                Xb = X.ap[:, :, blk * 128:(blk + 1) * 128]
                for c in range(8):
                    for n in range(2):
                        MM(tok, tok.ap[:, n * 512:(n + 1) * 512], X, Xb[:, c, :], Wsv, Wsv.ap[:, c, n * 512:(n + 1) * 512],
                           c == 0, c == 7)
                gl = b_gl[blk % 2]
                ACT(gl, gl.ap, tok, tok.ap, AF.Gelu_apprx_tanh)
                layer_norm(gl, gl.ap, b_vn[blk], b_vn[blk].ap, b_lng, b_lnb, b_stats)
                for j in (2 * blk, 2 * blk + 1):
                    ps = fm_proj(Wg1, j * 128, X, X.ap, (6, 7))
                    ACT(b_g1, b_g1.ap[:, j, :], ps, ps.ap, AF.Sigmoid)
            for blk in range(4):
                sps = ps_f32(4, 2)
                s3 = sps.ap.rearrange("p (g t) -> p g t", g=8)
                for g in range(8):
                    MM(sps, s3[:, g, :], b_vn[blk], b_vn[blk].ap[:, g * 128:(g + 1) * 128], b_WsT, b_WsT.ap[:, g, :],
                       True, True)
                TT("dve", b_st, b_st.ap, sps, s3, b_bsB, b_bsB.ap, ALU.add)
                TT("pool", b_og, b_og.ap[:, :, blk * 128:(blk + 1) * 128], b_st, b_st.ap,
                   b_uT, b_uT.ap[:, :, blk * 128:(blk + 1) * 128], ALU.mult)
            for j in range(8):
                ps = fm_proj(Wbs, j * 128, b_og, b_og.ap, (6, 7))
                TT("dve", b_m1, b_m1.ap[:, j, :], ps, ps.ap, b_g1, b_g1.ap[:, j, :], ALU.mult)
            P.dma("sp", lambda e: e.dma_start(out=m1_d[:, :, T * 512:(T + 1) * 512].rearrange("j p t -> p j t"),
                                              in_=b_m1.ap), reads=[b_m1], writes=["D:m1"], key="bm1")

        for T in range(8 if upto >= 2 else 0):
            a2a_tile(T)

    if upto >= 3:
        Cw = Alloc(DYN)
        Wxq = Cw.take([128, 8, 1024], BF16)
        Wg2 = Cw.take([128, 8, 1024], BF16)
        Wbm = Cw.take([128, 8, 1024], BF16)
        Wmk = Cw.take([128, 8, 1024], BF16)
        Wmv = Cw.take([128, 8, 1024], BF16)
        c_xb = [Cw.take([128, 1024], BF16) for _ in range(4)]
        c_xT = [Cw.take([128, 8, 512], BF16) for _ in range(2)]
        c_memb = Cw.take([128, 2, 1024], BF16)
        c_memT = Cw.take([128, 8, 256], BF16)
        c_kmT = Cw.take([128, 8, 256], BF16)
        c_vm = Cw.take([128, 2, 1024], BF16)
        c_xqT = Cw.take([128, 8, 512], BF16)
        c_pT = Cw.take([128, 8, 512], BF16)
        c_rs = [Cw.take([128, 512], F32) for _ in range(2)]
        c_ox = Cw.take([128, 8, 512], BF16)
        c_g2 = Cw.take([128, 8, 512], BF16)
        c_m1 = Cw.take([128, 8, 512], BF16)
        c_m12 = Cw.take([128, 8, 512], BF16)
        c_tmp = [Cw.take([128, 512], F32) for _ in range(2)]
        load_w(Wmk, w_mkv_d, 0, D, key="W0")
        load_w(Wmv, w_mkv_d, D, D, key="W1")
        load_w(Wxq, w_in_d, 5 * D, D, key="W2")
        load_w(Wg2, w_in_d, 8 * D, D, key="W3")
        load_w(Wbm, w_bm_d, 0, D, key="W4")
        P.dma("pool", lambda e: e.dma_start(out=c_memb.ap, in_=mem_d.rearrange("(b p) d -> p b d", p=128)),
              writes=[c_memb], key="memb")
        for mb in range(2):
            tp = ps_bf(0)
            for c in range(8):
                TR(tp, tp.ap[:, c, :], c_memb, c_memb.ap[:, mb, c * 128:(c + 1) * 128])
            COPY("act", c_memT, c_memT.ap[:, :, mb * 128:(mb + 1) * 128], tp, tp.ap)
        for j in range(8):
            ps = fm_proj(Wmk, j * 128, c_memT, c_memT.ap, (1, 2, 3))
            COPY("dve", c_kmT, c_kmT.ap[:, j, :], ps, ps.ap[:, 0:256])
        for mb in range(2):
            for n in range(2):
                ps = ps_f32(4 + n)
                for c in range(8):
                    MM(ps, ps.ap, c_memT, c_memT.ap[:, c, mb * 128:(mb + 1) * 128], Wmv, Wmv.ap[:, c, n * 512:(n + 1) * 512],
                       c == 0, c == 7)
                COPY("act", c_vm, c_vm.ap[:, mb, n * 512:(n + 1) * 512], ps, ps.ap)

        xa_banks = (1, 2, 3, 4, 5, 6, 7)

        def a2b_tile(T):
            X = c_xT[T % 2]
            x_tile(T, xq_d, c_xb, X)
            P.dma("sp", lambda e: e.dma_start(out=c_m1.ap, in_=m1_d[:, :, T * 512:(T + 1) * 512].rearrange("j p t -> p j t")),
                  reads=["D:m1"], writes=[c_m1], key="cm1")
            for j in range(8):
                ps = fm_proj(Wxq, j * 128, X, X.ap, xa_banks)
                COPY("dve" if j % 2 else "act", c_xqT, c_xqT.ap[:, j, :], ps, ps.ap)
            for hh in range(4):
                for mb in range(2):
                    b = xa_banks[fm_ctr[0] % 7]
                    fm_ctr[0] += 1
                    ps = ps_f32(b)
                    for dc in range(2):
                        MM(ps, ps.ap, c_kmT, c_kmT.ap[:, hh * 2 + dc, mb * 128:(mb + 1) * 128], c_xqT,
                           c_xqT.ap[:, hh * 2 + dc, :], dc == 0, dc == 1)
                    ACT(c_pT, c_pT.ap[:, hh * 2 + mb, :], ps, ps.ap, AF.Exp, scale=1.0 / 16.0)
            for j in range(8):
                ps = fm_proj(Wg2, j * 128, X, X.ap, xa_banks)
                ACT(c_g2, c_g2.ap[:, j, :], ps, ps.ap, AF.Sigmoid)
            for hh in range(4):
                b = xa_banks[fm_ctr[0] % 7]
                fm_ctr[0] += 1
                pss = ps_f32(b)
                for mb in range(2):
                    MM(pss, pss.ap, onesb, onesb.ap, c_pT, c_pT.ap[:, hh * 2 + mb, :], mb == 0, mb == 1)
                rs = c_rs[hh % 2]
                P.op("dve", lambda e, rs=rs, pss=pss: e.reciprocal(out=rs.ap, in_=pss.ap), reads=[pss], writes=[rs])
                for dc in range(2):
                    b = xa_banks[fm_ctr[0] % 7]
                    fm_ctr[0] += 1
                    ps = ps_f32(b)
                    for mb in range(2):
                        MM(ps, ps.ap, c_vm, c_vm.ap[:, mb, hh * 256 + dc * 128: hh * 256 + (dc + 1) * 128], c_pT,
                           c_pT.ap[:, hh * 2 + mb, :], mb == 0, mb == 1)
                    TT("dve", c_ox, c_ox.ap[:, hh * 2 + dc, :], ps, ps.ap, rs, rs.ap, ALU.mult)
            for j in range(8):
                ps = fm_proj(Wbm, j * 128, c_ox, c_ox.ap, xa_banks)
                tm = c_tmp[j % 2]
                TT("dve", tm, tm.ap, ps, ps.ap, c_g2, c_g2.ap[:, j, :], ALU.mult)
                TT("pool", c_m12, c_m12.ap[:, j, :], tm, tm.ap, c_m1, c_m1.ap[:, j, :], ALU.add)
            P.dma("sp", lambda e: e.dma_start(out=m12_d[:, :, T * 512:(T + 1) * 512].rearrange("j p t -> p j t"),
                                              in_=c_m12.ap), reads=[c_m12], writes=["D:m12"], key="cm12")

        for T in range(8 if upto >= 3 else 0):
            a2b_tile(T)

    if upto >= 4:
        Ca = Alloc(DYN)
        odaT = Ca.take([128, 8, NQ], BF16)
        KT = [Ca.take([128, SEQ], BF16) for _ in range(2)]
        VV = [Ca.take([128, NBLK, 128], BF16) for _ in range(2)]
        QT = [Ca.take([128, NQ], BF16) for _ in range(2)]
        NEB = 4
        ET = [Ca.take([128, 2, 512], BF16) for _ in range(NEB)]
        ES = [Ca.take([128, 512], BF16) for _ in range(NEB)]
        On = [Ca.take([128, 512], F32) for _ in range(2)]
        c_rs = Ca.take([128, 512], F32)
        c_of = Ca.take([128, 512], F32)
        c_rstd = Ca.take([128, 512], F32)
        c_sq = Ca.take([128, 512], BF16)

        def c_loads(h):
            kt, vv, qt = KT[h % 2], VV[h % 2], QT[h % 2]
            P.dma("sp", [lambda e, q=q: e.dma_start(out=qt.ap[:, q * 2048:(q + 1) * 2048], in_=qT_d[h, :, q * 2048:(q + 1) * 2048])
                         for q in range(2)], reads=["D:qT"], writes=[qt])
            P.dma("sp", [lambda e, q=q: e.dma_start(out=kt.ap[:, q * 2048:(q + 1) * 2048], in_=kT_d[h, :, q * 2048:(q + 1) * 2048])
                         for q in range(4)], reads=["D:kT"], writes=[kt])
            P.dma("sp", [lambda e, q=q: e.dma_start(
                out=vv.ap[:, q * 8:(q + 1) * 8, :],
                in_=v_d[q * 1024:(q + 1) * 1024, h * 128:(h + 1) * 128].rearrange("(j p) d -> p j d", p=128))
                for q in range(8)], reads=["D:v"], writes=[vv])

        def attn_items(h):
            items = []
            for T in range(8):
                for m in range(2):
                    npairs = 4 * T + 4
                    for pr in range(npairs):
                        items.append((h, T, m, pr, npairs))
            return items

        item_ctr = [0]
        grp_ctr = [0]
        deferred = []

        def item_qk_exp(it):
            h, T, m, pr, npairs = it
            kt, qt = KT[h % 2], QT[h % 2]
            k = item_ctr[0]
            item_ctr[0] += 1
            i0 = max(0, pr - 4 * T)
            c0 = i0 * 128
            S = ps_f32(2 * (k % 2), 2)
            S3 = S.ap.rearrange("p (j n) -> p j n", j=2)
            E = ET[k % NEB]
            Es = ES[k % NEB]
            for jj in range(2):
                j = 2 * pr + jj
                MM(S, S3[:, jj, c0:512], kt, kt.ap[m * 64:(m + 1) * 64, j * 128:(j + 1) * 128],
                   qt, qt.ap[m * 64:(m + 1) * 64, T * 512 + c0:(T + 1) * 512], True, True)
            ACT(E, E.ap[:, :, c0:512], S, S3[:, :, c0:512], AF.Exp, scale=0.125)
            if pr >= 4 * T:
                i = pr - 4 * T
                TT("dve", E, E.ap[:, :, i * 128:(i + 1) * 128], E, E.ap[:, :, i * 128:(i + 1) * 128],
                   maskp, maskp.ap[:, i % 2, :, :], ALU.mult)
            TT("dve", Es, Es.ap[:, c0:512], E, E.ap[:, 0, c0:512], E, E.ap[:, 1, c0:512], ALU.add)
            return (E, Es, k)

        def item_pv(it, E, Es, k):
            h, T, m, pr, npairs = it
            vv = VV[h % 2]
            if pr == 0:
                grp_ctr[0] += 1
            gp = grp_ctr[0] % 2
            aO = ps_f32(4 + gp)
            aS = ps_f32(6 + gp)
            c0 = max(0, pr - 4 * T) * 128
            for jj in range(2):
                j = 2 * pr + jj
                MM(aO, aO.ap[:, c0:512], vv, vv.ap[:, j, :], E, E.ap[:, jj, c0:512],
                   pr == 0 and jj == 0, pr == npairs - 1 and jj == 1)
            MM(aS, aS.ap[:, c0:512], onesb, onesb.ap, Es, Es.ap[:, c0:512], pr == 0, pr == npairs - 1)
            if pr == npairs - 1:
                P.op("dve", lambda e: e.reciprocal(out=c_rs.ap, in_=aS.ap), reads=[aS], writes=[c_rs])
                if m == 0:
                    TT("dve", On[0], On[0].ap, aO, aO.ap, c_rs, c_rs.ap, ALU.mult)
                else:
                    STT("dve", On[1], On[1].ap, aO, aO.ap, neglam.ap[:, 0:1], c_rs, c_rs.ap, ALU.mult, ALU.mult,
                        extra_reads=[neglam])
                    TT("pool", c_of, c_of.ap, On[0], On[0].ap, On[1], On[1].ap, ALU.add)
                    TT("pool", c_sq, c_sq.ap, c_of, c_of.ap, c_of, c_of.ap, ALU.mult)
                    deferred.append([3, h, T, aS])

        def attn_finish(h, T, aS):
            MM(aS, aS.ap, onesb, onesb.ap, c_sq, c_sq.ap, True, True)
            ACT(c_rstd, c_rstd.ap, aS, aS.ap, AF.Ln, scale=1.0 / 128.0, bias=epsb.ap[:, 0:1], extra_reads=[epsb])
            ACT(c_rstd, c_rstd.ap, c_rstd, c_rstd.ap, AF.Exp, scale=-0.5)
            STT("dve", odaT, odaT.ap[:, h, T * 512:(T + 1) * 512], c_of, c_of.ap, gcol.ap[:, 0:1], c_rstd, c_rstd.ap,
                ALU.mult, ALU.mult, extra_reads=[gcol])

        def tick_deferred(force=False):
            for d_ in list(deferred):
                d_[0] -= 1
                if d_[0] <= 0 or force:
                    deferred.remove(d_)
                    attn_finish(d_[1], d_[2], d_[3])

        NHEAD = 8
        c_loads(0)
        LAG = 2
        pend = []
        for h in range(NHEAD):
            while pend:
                item_pv(*pend.pop(0))
            if h + 1 < NHEAD:
                c_loads(h + 1)
            for it in attn_items(h):
                cur = item_qk_exp(it)
                pend.append((it,) + cur)
                if len(pend) > LAG:
                    item_pv(*pend.pop(0))
                tick_deferred()
        while pend:
            item_pv(*pend.pop(0))
        tick_deferred(force=True)
        if debug:
            P.dma("sp", lambda e: e.dma_start(out=oda_d.rearrange("h p t -> p h t"), in_=odaT.ap), reads=[odaT],
                  writes=["D:out"])

    if upto >= 5:
        Da = Alloc(DYN + 32)
        Wba = Da.take([128, 8, 1024], BF16)
        Wg0 = Da.take([128, 8, 1024], BF16)
        Wo = Da.take([128, 8, 1024], BF16)
        d_xb = [Da.take([128, 1024], BF16) for _ in range(4)]
        d_xT = Da.take([128, 8, 512], BF16)
        d_xf = [Da.take([128, 1024], F32) for _ in range(2)]
        d_g0 = Da.take([128, 8, 512], BF16)
        d_m12 = Da.take([128, 8, 512], BF16)
        d_mT = Da.take([128, 8, 512], BF16)
        d_r = [Da.take([128, 1024], F32) for _ in range(2)]
        d_lng = Da.take([128, 1024], F32)
        d_lnb = Da.take([128, 1024], F32)
        d_tmp = [Da.take([128, 512], F32) for _ in range(2)]
        d_stats = Da.take([128, 64], F32, align=4)
        load_w(Wg0, w_in_d, 6 * D, D, key="W0")
        load_w(Wba, w_ba_d, 0, D, key="W1")
        load_w(Wo, w_out_d, 0, D, key="W2")
        bcast_load(d_lng, ln1g_d[0:1, :], "lng")
        bcast_load(d_lnb, ln1b_d[0:1, :], "lnb")

        def d1_tile(T):
            X = d_xT
            x_tile(T, xq_d, d_xb, X)
            P.dma("sp", lambda e: e.dma_start(out=d_m12.ap, in_=m12_d[:, :, T * 512:(T + 1) * 512].rearrange("j p t -> p j t")),
                  reads=["D:m12"], writes=[d_m12], key="dm12")
            for j in range(8):
                ps = fm_proj(Wg0, j * 128, X, X.ap, (1, 2, 3))
                ACT(d_g0, d_g0.ap[:, j, :], ps, ps.ap, AF.Sigmoid)
            for j in range(8):
                ps = fm_proj(Wba, j * 128, odaT, odaT.ap[:, :, T * 512:(T + 1) * 512], (1, 2, 3))
                tm = d_tmp[j % 2]
                TT("dve", tm, tm.ap, ps, ps.ap, d_g0, d_g0.ap[:, j, :], ALU.mult)
                TT("dve", d_mT, d_mT.ap[:, j, :], tm, tm.ap, d_m12, d_m12.ap[:, j, :], ALU.add)
            for blk in range(4):
                r0 = T * 512 + blk * 128
                xf = d_xf[blk % 2]
                P.dma("sp", lambda e, xf=xf, r0=r0: e.dma_start(out=xf.ap, in_=xq_d[r0:r0 + 128, :]), writes=[xf],
                      key="dxf%d" % (blk % 2))
                yps = ps_f32(4 + 2 * (blk % 2), 2)
                for c in range(8):
                    for n in range(2):
                        MM(yps, yps.ap[:, n * 512:(n + 1) * 512], d_mT, d_mT.ap[:, c, blk * 128:(blk + 1) * 128],
                           Wo, Wo.ap[:, c, n * 512:(n + 1) * 512], c == 0, c == 7)
                r = d_r[blk % 2]
                STT("dve", r, r.ap, xf, xf.ap, ALPHA, yps, yps.ap, ALU.mult, ALU.add)
                layer_norm(r, r.ap, r, r.ap, d_lng, d_lnb, d_stats, eng2="dve")
                P.dma("sp", lambda e, r=r, r0=r0: e.dma_start(out=x1_d[r0:r0 + 128, :], in_=r.ap), reads=[r],
                      writes=["D:x1"], key="dr%d" % (blk % 2))

        for T in range(8 if upto >= 5 else 0):
            d1_tile(T)

    if upto >= 6:
        Ea = Alloc(DYN)
        Wfi = Ea.take([128, 8, 2 * DFF], BF16)
        Wfo = Ea.take([128, NF, 1024], BF16)
        e_xb = [Ea.take([128, 1024], BF16) for _ in range(4)]
        e_xT = Ea.take([128, 8, 512], BF16)
        e_hT = Ea.take([128, NF, 512], BF16)
        e_sa = [Ea.take([128, 512], F32) for _ in range(2)]
        e_r = [Ea.take([128, 1024], F32) for _ in range(2)]
        e_stats = Ea.take([128, 64], F32, align=4)
        e_lng = Buf(rv(2 * 2048, [128, 1024], F32).ap, rv(2 * 2048, [128, 1024], F32).res)
        e_lnb = Buf(rv(4 * 2048, [128, 1024], F32).ap, rv(4 * 2048, [128, 1024], F32).res)
        load_w(Wfi, w_fi_d, 0, 2 * DFF, key="W3", piece=1408)
        load_w(Wfo, w_fo_d, 0, D, nchunks=NF, key="W4")
        bcast_load(e_lng, ln2g_d[0:1, :], "lng")
        bcast_load(e_lnb, ln2b_d[0:1, :], "lnb")

        def d2_tile(T):
            X = e_xT
            x_tile(T, x1_d, e_xb, X, extra_reads=["D:x1"])
            for f in range(NF):
                pa = ps_f32(1 + 2 * (f % 2))
                pb = ps_f32(2 + 2 * (f % 2))
                for c in range(8):
                    MM(pa, pa.ap, Wfi, Wfi.ap[:, c, f * 128:(f + 1) * 128], X, X.ap[:, c, :], c == 0, c == 7)
                for c in range(8):
                    MM(pb, pb.ap, Wfi, Wfi.ap[:, c, DFF + f * 128:DFF + (f + 1) * 128], X, X.ap[:, c, :], c == 0, c == 7)
                sa_ = e_sa[f % 2]
                ACT(sa_, sa_.ap, pa, pa.ap, AF.Silu)
                TT("dve", e_hT, e_hT.ap[:, f, :], sa_, sa_.ap, pb, pb.ap, ALU.mult)
            for blk in range(4):
                r0 = T * 512 + blk * 128
                r = e_r[blk % 2]
                P.dma("sp", lambda e, r=r, r0=r0: e.dma_start(out=r.ap, in_=x1_d[r0:r0 + 128, :]), reads=["D:x1"],
                      writes=[r])
                yps = ps_f32(5, 2)
                for f in range(NF):
                    for n in range(2):
                        MM(yps, yps.ap[:, n * 512:(n + 1) * 512], e_hT, e_hT.ap[:, f, blk * 128:(blk + 1) * 128],
                           Wfo, Wfo.ap[:, f, n * 512:(n + 1) * 512], f == 0, f == NF - 1)
                STT("dve", r, r.ap, r, r.ap, ALPHA, yps, yps.ap, ALU.mult, ALU.add)
                layer_norm(r, r.ap, r, r.ap, e_lng, e_lnb, e_stats, eng2="dve")
                P.dma("sp", lambda e, r=r, r0=r0: e.dma_start(out=out_d[r0:r0 + 128, :], in_=r.ap), reads=[r],
                      writes=["D:out"], key="er%d" % (blk % 2))

        for T in range(8 if upto >= 6 else 0):
            d2_tile(T)

    P.op("sp", lambda e: e.nop(), reads=["D:out"])
    P.finalize(st)
    with nc.Block() as block:
        P.emit(block)
    st.close()
    return nc, P, st


def host_inputs(inputs):
    x = np.asarray(inputs["x"], dtype=np.float32)
    mem = np.asarray(inputs["mem"], dtype=np.float32)
    pos = np.asarray(inputs["positions"], dtype=np.int32)
    ident = np.eye(128, dtype=np.float32)
    tril = np.tril(np.ones((128, 128), dtype=np.float32))
    tri_kq = np.ascontiguousarray(tril.T)
    ones = np.ones((128, 128), np.float32)
    zeros = np.zeros((128, 128), np.float32)
    invf = (500000.0 ** (-np.arange(8, dtype=np.float32) * 2.0 / 16.0)).astype(np.float32).reshape(1, 8)
    shared = {}
    for k in ("w_in", "w_mem_kv", "w_br_attn", "w_br_sg", "w_br_mem", "w_out", "w_ffn_in", "w_ffn_out"):
        shared[k] = np.ascontiguousarray(np.asarray(inputs[k], dtype=np.float32)[0])
    for k in ("lambda_q1", "lambda_k1", "lambda_q2", "lambda_k2", "da_subln_g", "sg_norm_g", "sg_norm_b",
              "ln1_g", "ln1_b", "ln2_g", "ln2_b"):
        shared[k] = np.ascontiguousarray(np.asarray(inputs[k], dtype=np.float32)[0].reshape(1, -1))
    shared["sg_w_s"] = np.ascontiguousarray(np.asarray(inputs["sg_w_s"], dtype=np.float32)[0])
    shared["sg_b_s"] = np.ascontiguousarray(np.asarray(inputs["sg_b_s"], dtype=np.float32)[0].reshape(1, -1))
    shared["c_ident"] = ident
    shared["c_tril"] = tril
    shared["c_invf"] = invf
    in_maps = []
    for c in range(8):
        b, half = c // 2, c % 2
        blocks = q_blocks(half)
        xb = np.ascontiguousarray(x[b])
        xq = np.ascontiguousarray(x[b].reshape(NBLK, 128, D)[blocks].reshape(NQ, D))
        pb = pos[b].reshape(NBLK, 128)
        posb = np.ascontiguousarray(pb.T)
        posq = np.ascontiguousarray(pb[blocks].T)
        if half == 0:
            mp = np.stack([np.stack([tri_kq, zeros], 1), np.stack([ones, tri_kq], 1)], 1)
        else:
            mp = np.stack([np.stack([ones, tri_kq], 1), np.stack([tri_kq, zeros], 1)], 1)
        m = dict(shared)
        m.update(xb=xb, xq=xq, posb=posb, posq=posq, mem=np.ascontiguousarray(mem[b]),
                 c_maskp=np.ascontiguousarray(mp.reshape(128, 512)))
        in_maps.append(m)
    return in_maps


_CACHE = {}


def kernel(**inputs):
    in_maps = host_inputs(inputs)
    if "nc" not in _CACHE:
        _CACHE["nc"] = build_program(debug=False)[0]
    nc = _CACHE["nc"]
    res = run_bass_kernel_spmd(nc, in_maps, core_ids=list(range(8)))
    out = np.zeros((4, SEQ, D), dtype=np.float32)
    for c in range(8):
        b, half = c // 2, c % 2
        blocks = q_blocks(half)
        o = np.asarray(res.results[c]["out"]).reshape(NSLOT, 128, D)
        out[b].reshape(NBLK, 128, D)[blocks] = o
    return out
```

```python
import math
from contextlib import ExitStack
import numpy as np
import concourse.bass as bass
import concourse.mybir as mybir
from concourse.bass_utils import run_bass_kernel_spmd

F32 = mybir.dt.float32
BF16 = mybir.dt.bfloat16
I32 = mybir.dt.int32
AF = mybir.ActivationFunctionType
ALU = mybir.AluOpType
AX = mybir.AxisListType

D = 1024
SEQ = 8192
NBLK = 64
NSLOT = 32
NQ = NSLOT * 128
DFF = 2816
NF = DFF // 128
ALPHA = 2.0 ** 0.25
LN_EPS = 1e-5
RMS_EPS = 1e-5
LAMBDA_INIT = 0.8 - 0.6 * math.exp(0.0)
TWO_PI = 2.0 * math.pi
GELU_C = 0.7978845608028654
NG = 103
CONST_G = 8
DBG = {}
IN_SHAPES = {}


def q_blocks(half):
    out = []
    for s in range(NSLOT):
        g = s // 2
        if half == 0:
            out.append(4 * g + (0 if s % 2 == 0 else 3))
        else:
            out.append(4 * g + (1 if s % 2 == 0 else 2))
    return out


class Buf:
    def __init__(self, ap, res):
        self.ap = ap
        self.res = tuple(res)

    def __getitem__(self, k):
        return self.ap[k]


class Prog:
    ENGS = ("pe", "act", "dve", "pool", "sp")
    CH = 6000

    def __init__(self, nc):
        self.nc = nc
        self.ops = []

    @staticmethod
    def _res(lst):
        out = []
        for x in lst:
            if isinstance(x, Buf):
                out.extend(x.res)
            elif isinstance(x, str):
                out.append(x)
            else:
                out.extend(x)
        return tuple(out)

    def op(self, eng, fn, reads=(), writes=()):
        self.ops.append(dict(eng=eng, fn=fn, reads=self._res(reads), writes=self._res(writes), dma=False, sig=None))

    def dma(self, eng, fns, reads=(), writes=(), key=None):
        if not isinstance(fns, (list, tuple)):
            fns = [fns]
        key = None
        for x in list(writes) + list(reads):
            if isinstance(x, Buf):
                key = x.res[0] + ("s" if eng == "pool" else "h")
                break
        assert key is not None
        self.ops.append(dict(eng=eng, fn=list(fns), reads=self._res(reads), writes=self._res(writes), dma=True,
                             key=key, sig=None))

    def finalize(self, stack):
        nc = self.nc
        ops = self.ops
        lastw = {}
        readers = {}
        deps = [None] * len(ops)
        for n, o in enumerate(ops):
            d = {}
            for r in o["reads"]:
                for w in lastw.get(r, ()):
                    d.setdefault(w, "raw")
                if r[0] == "P" and not o["dma"]:
                    for tag, rd in readers.get(r, {}).items():
                        if tag != o["eng"] and rd not in d:
                            d[rd] = "rar"
            for w in o["writes"]:
                comm = w.startswith("D:")
                if not comm:
                    for pw in lastw.get(w, ()):
                        d.setdefault(pw, "waw")
                for rd in readers.get(w, {}).values():
                    if rd not in d:
                        d[rd] = "war"
            dd = []
            for j, kind in d.items():
                if j == n:
                    continue
                oj = ops[j]
                if (not oj["dma"]) and (not o["dma"]) and oj["eng"] == o["eng"]:
                    if o["eng"] == "pe":
                        continue
                dd.append(j)
            deps[n] = dd
            for r in o["reads"]:
                tag = ("dma", o["key"]) if o["dma"] else o["eng"]
                readers.setdefault(r, {})[tag] = n
            for w in o["writes"]:
                if w.startswith("D:"):
                    lastw.setdefault(w, []).append(n)
                else:
                    lastw[w] = [n]
                readers[w] = {}
        for n in range(len(ops)):
            for j in deps[n]:
                ops[j]["need"] = True
        cnt = {e: 0 for e in self.ENGS}
        dcnt = {}
        for o in ops:
            if o["dma"]:
                k = o["key"]
                dcnt[k] = dcnt.get(k, 0) + 16 * len(o["fn"])
                o["sig"] = dcnt[k]
            elif o.get("need"):
                o["sig"] = (cnt[o["eng"]] // self.CH, cnt[o["eng"]] % self.CH + 1)
                cnt[o["eng"]] += 1
        self.esem = {(e, c): stack.enter_context(nc.semaphore("s_%s%d" % (e, c)))
                     for e in self.ENGS for c in range(cnt[e] // self.CH + 1)}
        self.dsem = {k: stack.enter_context(nc.semaphore("d_%d" % i)) for i, k in enumerate(dcnt.keys())}
        known = {e: {} for e in self.ENGS}
        per_eng = {e: [] for e in self.ENGS}
        snap = {}
        for n, o in enumerate(ops):
            E = o["eng"]
            kn = known[E]
            waits = []
            for j in sorted(deps[n], reverse=True):
                oj = ops[j]
                if oj["dma"]:
                    sk, v = ("d", oj["key"]), (0, oj["sig"])
                else:
                    sk, v = ("e", oj["eng"]), oj["sig"]
                if kn.get(sk, (0, 0)) >= v:
                    continue
                kn[sk] = v
                waits.append((self.dsem[sk[1]] if sk[0] == "d" else self.esem[(sk[1], v[0])], v[1]))
                for k2, v2 in snap.get(j, {}).items():
                    if v2 > kn.get(k2, (0, 0)):
                        kn[k2] = v2
            if o["sig"] is not None:
                snap[n] = dict(kn)
            per_eng[E].append((waits, o))
        self.per_eng = per_eng
        self.counts = (cnt, dcnt)

    def emit(self, block):
        engmap = {"pe": block.tensor, "act": block.scalar, "dve": block.vector, "pool": block.gpsimd,
                  "sp": block.sync}
        for E in self.ENGS:
            items = self.per_eng[E]
            esem = self.esem
            dsem = self.dsem

            def body(eng, items=items, esem=esem, E=E):
                for waits, o in items:
                    for sem, v in waits:
                        eng.wait_ge(sem, v)
                    if o["dma"]:
                        for f in o["fn"]:
                            f(eng).then_inc(dsem[o["key"]], 16)
                    else:
                        ins = o["fn"](eng)
                        if o["sig"] is not None:
                            ins.then_inc(esem[(E, o["sig"][0])], 1)
            engmap[E](body)


def build_program(debug=False, upto=6):
    nc = bass.Bass("TRN2", target_bir_lowering=False)

    IN_PHASE = dict(xb=1, xq=2, mem=3, w_in=1, w_mem_kv=3, w_br_attn=5, w_br_sg=2, w_br_mem=3, w_out=5,
                    w_ffn_in=6, w_ffn_out=6, sg_norm_g=2, sg_norm_b=2, sg_w_s=2, sg_b_s=2, c_tril=2,
                    ln1_g=5, ln1_b=5, ln2_g=6, ln2_b=6)
    IN_SHAPES.clear()

    def din(name, shape, dt=F32):
        if upto < IN_PHASE.get(name, 0):
            shape = [1] * (len(shape) - 1) + [2]
        IN_SHAPES[name] = tuple(shape)
        return nc.dram_tensor(name, list(shape), dt, kind="ExternalInput").ap()

    def dscr(name, shape, dt):
        kind = "ExternalOutput" if debug else "Internal"
        return nc.dram_tensor(name, list(shape), dt, kind=kind).ap()

    xb_d = din("xb", [SEQ, D])
    xq_d = din("xq", [NQ, D])
    posb_d = din("posb", [128, NBLK], I32)
    posq_d = din("posq", [128, NSLOT], I32)
    mem_d = din("mem", [256, D])
    w_in_d = din("w_in", [D, 9 * D])
    w_mkv_d = din("w_mem_kv", [D, 2 * D])
    w_ba_d = din("w_br_attn", [D, D])
    w_bs_d = din("w_br_sg", [D, D])
    w_bm_d = din("w_br_mem", [D, D])
    w_out_d = din("w_out", [D, D])
    w_fi_d = din("w_ffn_in", [D, 2 * DFF])
    w_fo_d = din("w_ffn_out", [DFF, D])
    lq1_d = din("lambda_q1", [1, 64]); lk1_d = din("lambda_k1", [1, 64])
    lq2_d = din("lambda_q2", [1, 64]); lk2_d = din("lambda_k2", [1, 64])
    dag_d = din("da_subln_g", [1, 128])
    sgg_d = din("sg_norm_g", [1, D]); sgb_d = din("sg_norm_b", [1, D])
    sws_d = din("sg_w_s", [8, 128, 128]); sbs_d = din("sg_b_s", [1, 8 * 128])
    ln1g_d = din("ln1_g", [1, D]); ln1b_d = din("ln1_b", [1, D])
    ln2g_d = din("ln2_g", [1, D]); ln2b_d = din("ln2_b", [1, D])
    ident_d = din("c_ident", [128, 128])
    tril_d = din("c_tril", [128, 128])
    maskp_d = din("c_maskp", [128, 2 * 2 * 128])
    invf_d = din("c_invf", [1, 8])
    out_d = nc.dram_tensor("out", [NQ, D], F32, kind="ExternalOutput").ap()

    kT_d = dscr("kT_s", [8, 128, SEQ], BF16)
    v_d = dscr("v_s", [SEQ, D], BF16)
    qT_d = dscr("qT_s", [8, 128, NQ], BF16)
    m1_d = dscr("m1_s", [8, 128, NQ], BF16)
    m12_d = dscr("m12_s", [8, 128, NQ], BF16)
    x1_d = dscr("x1_s", [NQ, D], F32)
    oda_d = dscr("oda_s", [8, 128, NQ], BF16) if debug else None

    st = ExitStack()
    region = st.enter_context(nc.sbuf_tensor("region", [128, NG * 1024], BF16))
    psum = st.enter_context(nc.psum_tensor("psum", [128, 4096], F32))
    P = Prog(nc)

    def rv(off_b, shape, dt):
        esz = 4 if dt in (F32, I32) else 2
        n = int(np.prod(shape[1:]))
        nb = n * esz
        assert off_b % 4 == 0 and off_b + nb <= NG * 2048, (off_b, shape)
        a = region[0:shape[0], off_b // 2: (off_b + nb) // 2]
        if dt != BF16:
            a = a.bitcast(dt)
        if len(shape) == 3:
            a = a.rearrange("p (a b) -> p a b", a=shape[1])
        elif len(shape) == 4:
            a = a.rearrange("p (a b c) -> p a b c", a=shape[1], b=shape[2])
        res = ["G%d" % g for g in range(off_b // 2048, (off_b + nb - 1) // 2048 + 1)]
        return Buf(a, res)

    class Alloc:
        def __init__(self, g0, g1=NG):
            self.off = g0 * 2048
            self.end = g1 * 2048

        def take(self, shape, dt, align=2048):
            self.off = (self.off + align - 1) // align * align
            b = rv(self.off, shape, dt)
            esz = 4 if dt in (F32, I32) else 2
            self.off += int(np.prod(shape[1:])) * esz
            assert self.off <= self.end, ("region overflow", self.off, self.end)
            return b

    def pbank(b, nb=1):
        return ["P%d" % i for i in range(b, b + nb)]

    def ps_f32(b, nb=1):
        return Buf(psum[:, b * 512:(b + nb) * 512], pbank(b, nb))

    def ps_bf(b):
        return Buf(psum[:, b * 512:(b + 1) * 512].bitcast(BF16).rearrange("p (a b) -> p a b", a=8), pbank(b))

    def MM(O, o_ap, L, l_ap, R, r_ap, start, stop):
        P.op("pe", lambda e: e.matmul(o_ap, lhsT=l_ap, rhs=r_ap, start=start, stop=stop), reads=[L, R], writes=[O])

    def TR(O, o_ap, I, i_ap):
        P.op("pe", lambda e: e.transpose(out=o_ap, in_=i_ap, identity=identb.ap), reads=[I, identb], writes=[O])

    def ACT(O, o_ap, I, i_ap, func, scale=1.0, bias=0.0, extra_reads=()):
        P.op("act", lambda e: e.activation(out=o_ap, in_=i_ap, func=func, bias=bias, scale=scale),
             reads=[I] + list(extra_reads), writes=[O])

    def COPY(eng, O, o_ap, I, i_ap):
        if eng == "act":
            P.op("act", lambda e: e.copy(out=o_ap, in_=i_ap), reads=[I], writes=[O])
        else:
            P.op(eng, lambda e: e.tensor_copy(out=o_ap, in_=i_ap), reads=[I], writes=[O])

    def TT(eng, O, o_ap, A, a_ap, B, b_ap, op):
        P.op(eng, lambda e: e.tensor_tensor(out=o_ap, in0=a_ap, in1=b_ap, op=op), reads=[A, B], writes=[O])

    def TS(eng, O, o_ap, A, a_ap, s1, s2, op0, op1=None, extra_reads=()):
        if op1 is None:
            P.op(eng, lambda e: e.tensor_scalar(out=o_ap, in0=a_ap, scalar1=s1, scalar2=None, op0=op0),
                 reads=[A] + list(extra_reads), writes=[O])
        else:
            P.op(eng, lambda e: e.tensor_scalar(out=o_ap, in0=a_ap, scalar1=s1, scalar2=s2, op0=op0, op1=op1),
                 reads=[A] + list(extra_reads), writes=[O])

    def STT(eng, O, o_ap, A, a_ap, scalar, B, b_ap, op0, op1, extra_reads=()):
        P.op(eng, lambda e: e.scalar_tensor_tensor(out=o_ap, in0=a_ap, scalar=scalar, in1=b_ap, op0=op0, op1=op1),
             reads=[A, B] + list(extra_reads), writes=[O])

    def load_w(Wb, src, col0, ncols, nchunks=8, row0=0, key=None, piece=1024):
        fns = []
        for c in range(nchunks):
            for q0 in range(0, ncols, piece):
                q1 = min(ncols, q0 + piece)
                fns.append(lambda e, c=c, q0=q0, q1=q1: e.dma_start(
                    out=Wb.ap[:, c, q0:q1], in_=src[row0 + c * 128: row0 + (c + 1) * 128, col0 + q0: col0 + q1]))
        P.dma("pool", fns, writes=[Wb], key=key)

    def bcast_load(dst, src_row, key):
        P.dma("sp", lambda e: e.dma_start(out=dst.ap, in_=src_row.partition_broadcast(128)), writes=[dst], key=key)

    ca = Alloc(0, CONST_G)
    identb = ca.take([128, 128], BF16, align=4)
    maskp = ca.take([128, 2, 2, 128], BF16, align=4)
    gcol = ca.take([128, 1], F32, align=4)
    onesb = ca.take([128, 128], BF16, align=4)
    neglam = ca.take([128, 1], F32, align=4)
    epsb = ca.take([128, 1], F32, align=4)
    ca.off = 2 * 2048
    trig_b = ca.take([128, NBLK, 32], F32)
    trig_q = ca.take([128, NSLOT, 32], F32)
    DYN = CONST_G

    sa = Alloc(NG - 12)
    posb_i = sa.take([128, NBLK], I32, align=4)
    posq_i = sa.take([128, NSLOT], I32, align=4)
    invf = sa.take([128, 8], F32, align=4)
    lamv = sa.take([128, 4, 64], F32, align=4)
    lprod = sa.take([128, 2, 64], F32, align=4)
    ls = sa.take([128, 4], F32, align=4)
    posf = sa.take([128, NBLK], F32, align=4)
    ang = sa.take([128, NBLK, 8], F32)
    tr1 = sa.take([128, NBLK, 8], F32)
    tr2 = sa.take([128, NBLK, 8], F32)
    tri = sa.take([128, NBLK, 8], I32)

    P.dma("pool", [lambda e: e.dma_start(out=identb.ap, in_=ident_d[:, :]),
                   lambda e: e.dma_start(out=maskp.ap.rearrange("p a b c -> p (a b c)"), in_=maskp_d[:, :])],
          writes=[identb, maskp], key="cst_cast")
    P.dma("sp", [lambda e: e.dma_start(out=posb_i.ap, in_=posb_d[:, :]),
                 lambda e: e.dma_start(out=posq_i.ap, in_=posq_d[:, :]),
                 lambda e: e.dma_start(out=invf.ap, in_=invf_d[0:1, :].partition_broadcast(128)),
                 lambda e: e.dma_start(out=gcol.ap, in_=dag_d[0:1, :].rearrange("o d -> d o")),
                 lambda e: e.dma_start(out=lamv.ap[:, 0, :], in_=lq1_d[0:1, :].partition_broadcast(128)),
                 lambda e: e.dma_start(out=lamv.ap[:, 1, :], in_=lk1_d[0:1, :].partition_broadcast(128)),
                 lambda e: e.dma_start(out=lamv.ap[:, 2, :], in_=lq2_d[0:1, :].partition_broadcast(128)),
                 lambda e: e.dma_start(out=lamv.ap[:, 3, :], in_=lk2_d[0:1, :].partition_broadcast(128))],
          writes=[posb_i, posq_i, invf, gcol, lamv], key="cst_sp")
    P.op("pool", lambda e: e.memset(onesb.ap, 1.0), writes=[onesb])
    P.op("pool", lambda e: e.memset(epsb.ap, RMS_EPS), writes=[epsb])
    lv4 = lamv.ap.rearrange("p (a b) c -> p a b c", a=2)
    TT("dve", lprod, lprod.ap, lamv, lv4[:, :, 0, :], lamv, lv4[:, :, 1, :], ALU.mult)
    P.op("dve", lambda e: e.tensor_reduce(out=ls.ap[:, 0:2], in_=lprod.ap, axis=AX.X, op=ALU.add),
         reads=[lprod], writes=[ls])
    ACT(ls, ls.ap[:, 2:4], ls, ls.ap[:, 0:2], AF.Exp)
    TT("dve", neglam, neglam.ap, ls, ls.ap[:, 3:4], ls, ls.ap[:, 2:3], ALU.subtract)
    TS("dve", neglam, neglam.ap, neglam, neglam.ap, -LAMBDA_INIT, None, ALU.add)
    TS("dve", gcol, gcol.ap, gcol, gcol.ap, 1.0 - LAMBDA_INIT, None, ALU.mult)

    def trig_tables(pos_i, nb, table):
        pf = Buf(posf.ap[:, 0:nb], posf.res)
        A_ = Buf(ang.ap[:, 0:nb, :], ang.res)
        t1 = Buf(tr1.ap[:, 0:nb, :], tr1.res)
        t2 = Buf(tr2.ap[:, 0:nb, :], tr2.res)
        ti = Buf(tri.ap[:, 0:nb, :], tri.res)
        COPY("dve", pf, pf.ap, pos_i, pos_i.ap)
        TT("dve", A_, A_.ap, pf, pf.ap.unsqueeze(2).broadcast_to([128, nb, 8]),
           invf, invf.ap.unsqueeze(1).broadcast_to([128, nb, 8]), ALU.mult)

        def red_sin(dst_ap, shift):
            TS("dve", t1, t1.ap, A_, A_.ap, shift, None, ALU.add)
            TS("dve", t2, t2.ap, t1, t1.ap, 1.0 / TWO_PI, None, ALU.mult)
            COPY("dve", ti, ti.ap, t2, t2.ap)
            COPY("dve", t2, t2.ap, ti, ti.ap)
            STT("dve", t1, t1.ap, t2, t2.ap, -TWO_PI, t1, t1.ap, ALU.mult, ALU.add)
            TS("dve", t2, t2.ap, t1, t1.ap, math.pi, -TWO_PI, ALU.is_ge, ALU.mult)
            TT("dve", t1, t1.ap, t1, t1.ap, t2, t2.ap, ALU.add)
            TS("dve", t2, t2.ap, t1, t1.ap, -math.pi, TWO_PI, ALU.is_lt, ALU.mult)
            TT("dve", t1, t1.ap, t1, t1.ap, t2, t2.ap, ALU.add)
            TS("dve", t1, t1.ap, t1, t1.ap, 0.999999, None, ALU.mult)
            ACT(table, dst_ap, t1, t1.ap, AF.Sin)
        red_sin(table.ap[:, :, 24:32], 0.0)
        red_sin(table.ap[:, :, 0:8], math.pi / 2.0)
        COPY("dve", table, table.ap[:, :, 8:16], table, table.ap[:, :, 0:8])
        TS("dve", table, table.ap[:, :, 16:24], table, table.ap[:, :, 24:32], -1.0, None, ALU.mult)
    trig_tables(posb_i, NBLK, trig_b)
    trig_tables(posq_i, NSLOT, trig_q)

    def rope(Z, z_ap, OB, table, blk, RA, RB):
        kz = z_ap.rearrange("p (g d) -> p g d", g=16)
        ob3 = OB.ap.rearrange("p (g d) -> p g d", g=16)
        cs = table.ap[:, blk, 0:16].unsqueeze(1).broadcast_to([128, 16, 16])
        slo = table.ap[:, blk, 16:24].unsqueeze(1).broadcast_to([128, 16, 8])
        shi = table.ap[:, blk, 24:32].unsqueeze(1).broadcast_to([128, 16, 8])
        TT("dve", RA, RA.ap, Z, kz[:, :, 0:16], table, cs, ALU.mult)
        TT("dve", RB, RB.ap[:, :, 0:8], Z, kz[:, :, 8:16], table, slo, ALU.mult)
        TT("dve", RB, RB.ap[:, :, 8:16], Z, kz[:, :, 0:8], table, shi, ALU.mult)
        TT("dve", OB, ob3[:, :, 0:16], RA, RA.ap, RB, RB.ap, ALU.add)

    stat_ctr = [0]

    def layer_norm(R, r_ap, O, o_ap, g, b, STATS, eps=LN_EPS, eng2="pool"):
        k = stat_ctr[0] % 4
        stat_ctr[0] += 1
        sv_ = STATS.ap[:, k * 16:(k + 1) * 16]
        S = STATS
        P.op("dve", lambda e: e.bn_stats(out=sv_[:, 0:6], in_=r_ap[:, 0:512]), reads=[R], writes=[S])
        P.op("dve", lambda e: e.bn_stats(out=sv_[:, 6:12], in_=r_ap[:, 512:1024]), reads=[R], writes=[S])
        P.op("dve", lambda e: e.bn_aggr(out=sv_[:, 12:14], in_=sv_[:, 0:12]), reads=[S], writes=[S])
        ACT(S, sv_[:, 14:15], S, sv_[:, 13:14], AF.Sqrt, scale=1.0, bias=eps)
        P.op("dve", lambda e: e.reciprocal(out=sv_[:, 15:16], in_=sv_[:, 14:15]), reads=[S], writes=[S])
        TS("dve", R, r_ap, R, r_ap, sv_[:, 12:13], sv_[:, 15:16], ALU.subtract, ALU.mult, extra_reads=[S])
        TT(eng2, R, r_ap, R, r_ap, g, g.ap, ALU.mult)
        TT(eng2, O, o_ap, R, r_ap, b, b.ap, ALU.add)

    def load_xT(src_rows_ap, XB, key, tpb, XT, xt_ap, evac="act"):
        P.dma("pool", lambda e: e.dma_start(out=XB.ap, in_=src_rows_ap), writes=[XB], key=key)
        tp = ps_bf(tpb)
        for c in range(8):
            TR(tp, tp.ap[:, c, :], XB, XB.ap[:, c * 128:(c + 1) * 128])
        COPY(evac, XT, xt_ap, tp, tp.ap)

    def x_issue(src_rows_ap, XB, extra_reads=()):
        P.dma("pool", lambda e: e.dma_start(out=XB.ap, in_=src_rows_ap), reads=list(extra_reads), writes=[XB])

    def x_trans(XB, tpb, XT, xt_ap, evac="act"):
        tp = ps_bf(tpb)
        for c in range(8):
            TR(tp, tp.ap[:, c, :], XB, XB.ap[:, c * 128:(c + 1) * 128])
        COPY(evac, XT, xt_ap, tp, tp.ap)

    def x_tile(T, src_d, XBs, X, extra_reads=()):
        if T == 0:
            for blk in range(4):
                x_issue(src_d[blk * 128:(blk + 1) * 128, :], XBs[blk], extra_reads)
        for blk in range(4):
            x_trans(XBs[blk], 0, X, X.ap[:, :, blk * 128:(blk + 1) * 128])
            if T + 1 < 8:
                r1 = (T + 1) * 512 + blk * 128
                x_issue(src_d[r1:r1 + 128, :], XBs[blk], extra_reads)

    fm_ctr = [0]

    def fm_proj(W, col0, X, x_ap, banks):
        b = banks[fm_ctr[0] % len(banks)]
        fm_ctr[0] += 1
        ps = ps_f32(b)
        n = x_ap.shape[-1]
        for c in range(8):
            MM(ps, ps.ap[:, 0:n], W, W.ap[:, c, col0:col0 + 128], X, x_ap[:, c, :], c == 0, c == 7)
        return ps

    A = Alloc(DYN)
    Wk = A.take([128, 8, 1024], BF16)
    Wv = A.take([128, 8, 1024], BF16)
    if upto >= 1:
        load_w(Wk, w_in_d, 1 * D, D, key="W0")
        load_w(Wv, w_in_d, 2 * D, D, key="W1")
    a_xb = [A.take([128, 1024], BF16) for _ in range(3)]
    a_xT = [A.take([128, 8, 128], BF16) for _ in range(2)]
    a_kb = [A.take([128, 1024], BF16) for _ in range(2)]
    a_vb = [A.take([128, 1024], BF16) for _ in range(2)]
    a_kT = [A.take([128, 8, 512], BF16) for _ in range(2)]
    a_RA = A.take([128, 16, 16], F32, align=4)
    a_RB = A.take([128, 16, 16], F32, align=4)

    def a1_A(t):
        load_xT(xb_d[t * 128:(t + 1) * 128, :], a_xb[t % 3], "xb%d" % (t % 3), t % 2, a_xT[t % 2], a_xT[t % 2].ap)

    def a1_B(t):
        X = a_xT[t % 2]
        kps = ps_f32(2, 2)
        vps = ps_f32(4, 2)
        for (O, W) in ((kps, Wk), (vps, Wv)):
            for c in range(8):
                for n in range(2):
                    MM(O, O.ap[:, n * 512:(n + 1) * 512], X, X.ap[:, c, :], W,
                       W.ap[:, c, n * 512:(n + 1) * 512], c == 0, c == 7)
        kb = a_kb[t % 2]
        COPY("act", kb, kb.ap, kps, kps.ap)
        if DBG.get('b', 9) >= 2:
            rope(kps, kps.ap, kb, trig_b, t, a_RA, a_RB)
        vb = a_vb[t % 2]
        if DBG.get('b', 9) >= 3:
            COPY("dve", vb, vb.ap, vps, vps.ap)
        if DBG.get('b', 9) >= 4:
            P.dma("sp", lambda e: e.dma_start(out=v_d[t * 128:(t + 1) * 128, :], in_=vb.ap), reads=[vb],
                  writes=["D:v"], key="vb%d" % (t % 2))

    def a1_C(t):
        kb = a_kb[t % 2]
        tk = ps_bf(6 + t % 2)
        for h in range(8):
            TR(tk, tk.ap[:, h, :], kb, kb.ap[:, h * 128:(h + 1) * 128])
        KT = a_kT[(t // 4) % 2]
        COPY("dve", KT, KT.ap[:, :, (t % 4) * 128:(t % 4 + 1) * 128], tk, tk.ap)
        if t % 4 == 3:
            t0 = (t - 3) * 128
            P.dma("sp", lambda e: e.dma_start(out=kT_d[:, :, t0:t0 + 512].rearrange("h p t -> p h t"), in_=KT.ap),
                  reads=[KT], writes=["D:kT"], key="kTt%d" % ((t // 4) % 2))

    Bw = Alloc(DYN + 40)
    Wq = Bw.take([128, 8, 1024], BF16)
    Wsu = Bw.take([128, 8, 1024], BF16)
    Wsv = Bw.take([128, 8, 1024], BF16)
    Wg1 = Bw.take([128, 8, 1024], BF16)
    Wbs = Bw.take([128, 8, 1024], BF16)
    if upto >= 2:
        load_w(Wq, w_in_d, 0, D, key="W2")
        load_w(Wsv, w_in_d, 4 * D, D, key="W4")
        load_w(Wsu, w_in_d, 3 * D, D, key="W3")

    NB_A1 = NBLK if upto >= 1 else 0
    for t in range(NB_A1 + 2):
        if t < NB_A1:
            a1_A(t)
        if 1 <= t <= NB_A1 and DBG.get('a1', 3) >= 2:
            a1_B(t - 1)
        if t >= 2 and DBG.get('a1', 3) >= 3:
            a1_C(t - 2)
        if t == 8 and upto >= 2:
            load_w(Wg1, w_in_d, 6 * D + 1 * D, D, key="W5")
            load_w(Wbs, w_bs_d, 0, D, key="W6")

    if upto >= 2:
        Bt = Alloc(DYN, DYN + 40)
        b_xb = [Bt.take([128, 1024], BF16) for _ in range(4)]
        b_xT = [Bt.take([128, 8, 512], BF16) for _ in range(2)]
        b_uT = Bt.take([128, 8, 512], BF16)
        b_og = Bt.take([128, 8, 512], BF16)
        b_g1 = Bt.take([128, 8, 512], BF16)
        b_m1 = Bt.take([128, 8, 512], BF16)
        b_qT = Bt.take([128, 8, 512], BF16)
        b_qb = Bt.take([128, 1024], BF16)
        b_vn = [Bt.take([128, 1024], BF16) for _ in range(4)]
        Bt2 = Alloc(DYN + 80)
        b_gl = [Bt2.take([128, 1024], F32) for _ in range(2)]
        b_st = Bt2.take([128, 8, 128], F32)
        b_lng = Bt2.take([128, 1024], F32)
        b_lnb = Bt2.take([128, 1024], F32)
        b_bsB = Bt2.take([128, 8, 128], F32)
        b_WsT = Bt2.take([128, 8, 128], BF16)
        b_RA = Bt2.take([128, 16, 16], F32, align=4)
        b_RB = Bt2.take([128, 16, 16], F32, align=4)
        b_stats = Bt2.take([128, 64], F32, align=4)

        bcast_load(b_lng, sgg_d[0:1, :], "lng")
        bcast_load(b_lnb, sgb_d[0:1, :], "lnb")
        bcast_load(Buf(b_bsB.ap.rearrange("p a b -> p (a b)"), b_bsB.res), sbs_d[0:1, :], "bsB")
        wsf = Buf(b_gl[0].ap.rearrange("p (a b) -> p a b", a=8), b_gl[0].res)
        trf = Buf(b_gl[1].ap[:, 0:128], b_gl[1].res)
        wsb = Buf(b_qb.ap.rearrange("p (a b) -> p a b", a=8), b_qb.res)
        P.dma("sp", [lambda e: e.dma_start(out=wsf.ap, in_=sws_d.rearrange("g t s -> t g s")),
                     lambda e: e.dma_start(out=trf.ap, in_=tril_d[:, :])], writes=[wsf, trf], key="wsf")
        TT("dve", wsb, wsb.ap, wsf, wsf.ap, trf, trf.ap.unsqueeze(1).broadcast_to([128, 8, 128]), ALU.mult)
        tpw = ps_bf(0)
        for g in range(8):
            TR(tpw, tpw.ap[:, g, :], wsb, wsb.ap[:, g, :])
        COPY("act", b_WsT, b_WsT.ap, tpw, tpw.ap)

        def a2a_tile(T):
            X = b_xT[T % 2]
            x_tile(T, xq_d, b_xb, X)
            tok = ps_f32(1, 2)
            for blk in range(4):
                Xb = X.ap[:, :, blk * 128:(blk + 1) * 128]
                for c in range(8):
                    for n in range(2):
                        MM(tok, tok.ap[:, n * 512:(n + 1) * 512], X, Xb[:, c, :], Wq, Wq.ap[:, c, n * 512:(n + 1) * 512],
                           c == 0, c == 7)
                COPY("act", b_qb, b_qb.ap, tok, tok.ap)
                rope(tok, tok.ap, b_qb, trig_q, T * 4 + blk, b_RA, b_RB)
                for j in (2 * blk, 2 * blk + 1):
                    ps = fm_proj(Wsu, j * 128, X, X.ap, (6, 7))
                    ACT(b_uT, b_uT.ap[:, j, :], ps, ps.ap, AF.Gelu_apprx_tanh)
                tq = ps_bf(3)
                for h in range(8):
                    TR(tq, tq.ap[:, h, :], b_qb, b_qb.ap[:, h * 128:(h + 1) * 128])
                COPY("dve", b_qT, b_qT.ap[:, :, blk * 128:(blk + 1) * 128], tq, tq.ap)
            P.dma("sp", lambda e: e.dma_start(out=qT_d[:, :, T * 512:(T + 1) * 512].rearrange("h p t -> p h t"),
                                              in_=b_qT.ap), reads=[b_qT], writes=["D:qT"], key="bqT")
            for blk in range(4):
                Xb = X.ap[:, :, blk * 128:(blk + 1) * 128]
                for c in range(8):
                    for n in range(2):
                        MM(tok, tok.ap[:, n * 512:(n + 1) * 512], X, Xb[:, c, :], Wsv, Wsv.ap[:, c, n * 512:(n + 1) * 512],
                           c == 0, c == 7)
                gl = b_gl[blk % 2]
                ACT(gl, gl.ap, tok, tok.ap, AF.Gelu_apprx_tanh)
                layer_norm(gl, gl.ap, b_vn[blk], b_vn[blk].ap, b_lng, b_lnb, b_stats)
                for j in (2 * blk, 2 * blk + 1):
                    ps = fm_proj(Wg1, j * 128, X, X.ap, (6, 7))
                    ACT(b_g1, b_g1.ap[:, j, :], ps, ps.ap, AF.Sigmoid)
            for blk in range(4):
                sps = ps_f32(4, 2)
                s3 = sps.ap.rearrange("p (g t) -> p g t", g=8)
                for g in range(8):
                    MM(sps, s3[:, g, :], b_vn[blk], b_vn[blk].ap[:, g * 128:(g + 1) * 128], b_WsT, b_WsT.ap[:, g, :],
                       True, True)
                TT("dve", b_st, b_st.ap, sps, s3, b_bsB, b_bsB.ap, ALU.add)
                TT("pool", b_og, b_og.ap[:, :, blk * 128:(blk + 1) * 128], b_st, b_st.ap,
                   b_uT, b_uT.ap[:, :, blk * 128:(blk + 1) * 128], ALU.mult)
            for j in range(8):
                ps = fm_proj(Wbs, j * 128, b_og, b_og.ap, (6, 7))
                TT("dve", b_m1, b_m1.ap[:, j, :], ps, ps.ap, b_g1, b_g1.ap[:, j, :], ALU.mult)
            P.dma("sp", lambda e: e.dma_start(out=m1_d[:, :, T * 512:(T + 1) * 512].rearrange("j p t -> p j t"),
                                              in_=b_m1.ap), reads=[b_m1], writes=["D:m1"], key="bm1")

        for T in range(8 if upto >= 2 else 0):
            a2a_tile(T)

    if upto >= 3:
        Cw = Alloc(DYN)
        Wxq = Cw.take([128, 8, 1024], BF16)
        Wg2 = Cw.take([128, 8, 1024], BF16)
        Wbm = Cw.take([128, 8, 1024], BF16)
        Wmk = Cw.take([128, 8, 1024], BF16)
        Wmv = Cw.take([128, 8, 1024], BF16)
        c_xb = [Cw.take([128, 1024], BF16) for _ in range(4)]
        c_xT = [Cw.take([128, 8, 512], BF16) for _ in range(2)]
        c_memb = Cw.take([128, 2, 1024], BF16)
        c_memT = Cw.take([128, 8, 256], BF16)
        c_kmT = Cw.take([128, 8, 256], BF16)
        c_vm = Cw.take([128, 2, 1024], BF16)
        c_xqT = Cw.take([128, 8, 512], BF16)
        c_pT = Cw.take([128, 8, 512], BF16)
        c_rs = [Cw.take([128, 512], F32) for _ in range(2)]
        c_ox = Cw.take([128, 8, 512], BF16)
        c_g2 = Cw.take([128, 8, 512], BF16)
        c_m1 = Cw.take([128, 8, 512], BF16)
        c_m12 = Cw.take([128, 8, 512], BF16)
        c_tmp = [Cw.take([128, 512], F32) for _ in range(2)]
        load_w(Wmk, w_mkv_d, 0, D, key="W0")
        load_w(Wmv, w_mkv_d, D, D, key="W1")
        load_w(Wxq, w_in_d, 5 * D, D, key="W2")
        load_w(Wg2, w_in_d, 8 * D, D, key="W3")
        load_w(Wbm, w_bm_d, 0, D, key="W4")
        P.dma("pool", lambda e: e.dma_start(out=c_memb.ap, in_=mem_d.rearrange("(b p) d -> p b d", p=128)),
              writes=[c_memb], key="memb")
        for mb in range(2):
            tp = ps_bf(0)
            for c in range(8):
                TR(tp, tp.ap[:, c, :], c_memb, c_memb.ap[:, mb, c * 128:(c + 1) * 128])
            COPY("act", c_memT, c_memT.ap[:, :, mb * 128:(mb + 1) * 128], tp, tp.ap)
        for j in range(8):
            ps = fm_proj(Wmk, j * 128, c_memT, c_memT.ap, (1, 2, 3))
            COPY("dve", c_kmT, c_kmT.ap[:, j, :], ps, ps.ap[:, 0:256])
        for mb in range(2):
            for n in range(2):
                ps = ps_f32(4 + n)
                for c in range(8):
                    MM(ps, ps.ap, c_memT, c_memT.ap[:, c, mb * 128:(mb + 1) * 128], Wmv, Wmv.ap[:, c, n * 512:(n + 1) * 512],
                       c == 0, c == 7)
                COPY("act", c_vm, c_vm.ap[:, mb, n * 512:(n + 1) * 512], ps, ps.ap)

        xa_banks = (1, 2, 3, 4, 5, 6, 7)

        def a2b_tile(T):
            X = c_xT[T % 2]
            x_tile(T, xq_d, c_xb, X)
            P.dma("sp", lambda e: e.dma_start(out=c_m1.ap, in_=m1_d[:, :, T * 512:(T + 1) * 512].rearrange("j p t -> p j t")),
                  reads=["D:m1"], writes=[c_m1], key="cm1")
            for j in range(8):
                ps = fm_proj(Wxq, j * 128, X, X.ap, xa_banks)
                COPY("dve" if j % 2 else "act", c_xqT, c_xqT.ap[:, j, :], ps, ps.ap)
            for hh in range(4):
                for mb in range(2):
                    b = xa_banks[fm_ctr[0] % 7]
                    fm_ctr[0] += 1
                    ps = ps_f32(b)
                    for dc in range(2):
                        MM(ps, ps.ap, c_kmT, c_kmT.ap[:, hh * 2 + dc, mb * 128:(mb + 1) * 128], c_xqT,
                           c_xqT.ap[:, hh * 2 + dc, :], dc == 0, dc == 1)
                    ACT(c_pT, c_pT.ap[:, hh * 2 + mb, :], ps, ps.ap, AF.Exp, scale=1.0 / 16.0)
            for j in range(8):
                ps = fm_proj(Wg2, j * 128, X, X.ap, xa_banks)
                ACT(c_g2, c_g2.ap[:, j, :], ps, ps.ap, AF.Sigmoid)
            for hh in range(4):
                b = xa_banks[fm_ctr[0] % 7]
                fm_ctr[0] += 1
                pss = ps_f32(b)
                for mb in range(2):
                    MM(pss, pss.ap, onesb, onesb.ap, c_pT, c_pT.ap[:, hh * 2 + mb, :], mb == 0, mb == 1)
                rs = c_rs[hh % 2]
                P.op("dve", lambda e, rs=rs, pss=pss: e.reciprocal(out=rs.ap, in_=pss.ap), reads=[pss], writes=[rs])
                for dc in range(2):
                    b = xa_banks[fm_ctr[0] % 7]
                    fm_ctr[0] += 1
                    ps = ps_f32(b)
                    for mb in range(2):
                        MM(ps, ps.ap, c_vm, c_vm.ap[:, mb, hh * 256 + dc * 128: hh * 256 + (dc + 1) * 128], c_pT,
                           c_pT.ap[:, hh * 2 + mb, :], mb == 0, mb == 1)
                    TT("dve", c_ox, c_ox.ap[:, hh * 2 + dc, :], ps, ps.ap, rs, rs.ap, ALU.mult)
            for j in range(8):
                ps = fm_proj(Wbm, j * 128, c_ox, c_ox.ap, xa_banks)
                tm = c_tmp[j % 2]
                TT("dve", tm, tm.ap, ps, ps.ap, c_g2, c_g2.ap[:, j, :], ALU.mult)
                TT("pool", c_m12, c_m12.ap[:, j, :], tm, tm.ap, c_m1, c_m1.ap[:, j, :], ALU.add)
            P.dma("sp", lambda e: e.dma_start(out=m12_d[:, :, T * 512:(T + 1) * 512].rearrange("j p t -> p j t"),
                                              in_=c_m12.ap), reads=[c_m12], writes=["D:m12"], key="cm12")

        for T in range(8 if upto >= 3 else 0):
            a2b_tile(T)

    if upto >= 4:
        Ca = Alloc(DYN)
        odaT = Ca.take([128, 8, NQ], BF16)
        KT = [Ca.take([128, SEQ], BF16) for _ in range(2)]
        VV = [Ca.take([128, NBLK, 128], BF16) for _ in range(2)]
        QT = [Ca.take([128, NQ], BF16) for _ in range(2)]
        NEB = 4
        ET = [Ca.take([128, 2, 512], BF16) for _ in range(NEB)]
        ES = [Ca.take([128, 512], BF16) for _ in range(NEB)]
        On = [Ca.take([128, 512], F32) for _ in range(2)]
        c_rs = Ca.take([128, 512], F32)
        OF = [Ca.take([128, 512], F32) for _ in range(4)]
        SSQ = Ca.take([128, 4, 512], F32)
        c_sq = Ca.take([128, 512], BF16)

        def c_loads(h):
            kt, vv, qt = KT[h % 2], VV[h % 2], QT[h % 2]
            P.dma("sp", [lambda e, q=q: e.dma_start(out=qt.ap[:, q * 2048:(q + 1) * 2048], in_=qT_d[h, :, q * 2048:(q + 1) * 2048])
                         for q in range(2)], reads=["D:qT"], writes=[qt])
            P.dma("sp", [lambda e, q=q: e.dma_start(out=kt.ap[:, q * 2048:(q + 1) * 2048], in_=kT_d[h, :, q * 2048:(q + 1) * 2048])
                         for q in range(4)], reads=["D:kT"], writes=[kt])
            P.dma("sp", [lambda e, q=q: e.dma_start(
                out=vv.ap[:, q * 8:(q + 1) * 8, :],
                in_=v_d[q * 1024:(q + 1) * 1024, h * 128:(h + 1) * 128].rearrange("(j p) d -> p j d", p=128))
                for q in range(8)], reads=["D:v"], writes=[vv])

        def attn_items(h):
            items = []
            for T in range(8):
                for m in range(2):
                    npairs = 4 * T + 4
                    for pr in range(npairs):
                        items.append((h, T, m, pr, npairs))
            return items

        item_ctr = [0]
        grp_ctr = [0]
        deferred = []

        def item_qk_exp(it):
            h, T, m, pr, npairs = it
            kt, qt = KT[h % 2], QT[h % 2]
            k = item_ctr[0]
            item_ctr[0] += 1
            i0 = max(0, pr - 4 * T)
            c0 = i0 * 128
            S = ps_f32(2 * (k % 2), 2)
            S3 = S.ap.rearrange("p (j n) -> p j n", j=2)
            E = ET[k % NEB]
            Es = ES[k % NEB]
            for jj in range(2):
                j = 2 * pr + jj
                MM(S, S3[:, jj, c0:512], kt, kt.ap[m * 64:(m + 1) * 64, j * 128:(j + 1) * 128],
                   qt, qt.ap[m * 64:(m + 1) * 64, T * 512 + c0:(T + 1) * 512], True, True)
            ACT(E, E.ap[:, :, c0:512], S, S3[:, :, c0:512], AF.Exp, scale=0.125)
            if pr >= 4 * T:
                i = pr - 4 * T
                TT("dve", E, E.ap[:, :, i * 128:(i + 1) * 128], E, E.ap[:, :, i * 128:(i + 1) * 128],
                   maskp, maskp.ap[:, i % 2, :, :], ALU.mult)
            TT("dve", Es, Es.ap[:, c0:512], E, E.ap[:, 0, c0:512], E, E.ap[:, 1, c0:512], ALU.add)
            return (E, Es, k)

        def item_pv(it, E, Es, k):
            h, T, m, pr, npairs = it
            vv = VV[h % 2]
            if pr == 0:
                grp_ctr[0] += 1
            gp = grp_ctr[0] % 2
            aO = ps_f32(4 + gp)
            aS = ps_f32(6 + gp)
            c0 = max(0, pr - 4 * T) * 128
            for jj in range(2):
                j = 2 * pr + jj
                MM(aO, aO.ap[:, c0:512], vv, vv.ap[:, j, :], E, E.ap[:, jj, c0:512],
                   pr == 0 and jj == 0, pr == npairs - 1 and jj == 1)
            MM(aS, aS.ap[:, c0:512], onesb, onesb.ap, Es, Es.ap[:, c0:512], pr == 0, pr == npairs - 1)
            if pr == npairs - 1:
                P.op("dve", lambda e: e.reciprocal(out=c_rs.ap, in_=aS.ap), reads=[aS], writes=[c_rs])
                if m == 0:
                    TT("dve", On[0], On[0].ap, aO, aO.ap, c_rs, c_rs.ap, ALU.mult)
                else:
                    STT("dve", On[1], On[1].ap, aO, aO.ap, neglam.ap[:, 0:1], c_rs, c_rs.ap, ALU.mult, ALU.mult,
                        extra_reads=[neglam])
                    c_of = OF[T % 4]
                    TT("pool", c_of, c_of.ap, On[0], On[0].ap, On[1], On[1].ap, ALU.add)
                    TT("pool", c_sq, c_sq.ap, c_of, c_of.ap, c_of, c_of.ap, ALU.mult)
                    deferred.append([3, h, T, aS])

        def attn_finish(h, T, aS):
            MM(aS, aS.ap, onesb, onesb.ap, c_sq, c_sq.ap, True, True)
            COPY("dve", SSQ, SSQ.ap[:, T % 4, :], aS, aS.ap)
            if T % 4 == 3:
                flat = SSQ.ap.rearrange("p a b -> p (a b)")
                ACT(SSQ, flat, SSQ, flat, AF.Ln, scale=1.0 / 128.0, bias=epsb.ap[:, 0:1], extra_reads=[epsb])
                ACT(SSQ, flat, SSQ, flat, AF.Exp, scale=-0.5)
                for t in range(4):
                    TT_ = T - 3 + t
                    STT("dve", odaT, odaT.ap[:, h, TT_ * 512:(TT_ + 1) * 512], OF[t], OF[t].ap, gcol.ap[:, 0:1],
                        SSQ, SSQ.ap[:, t, :], ALU.mult, ALU.mult, extra_reads=[gcol])

        def tick_deferred(force=False):
            for d_ in list(deferred):
                d_[0] -= 1
                if d_[0] <= 0 or force:
                    deferred.remove(d_)
                    attn_finish(d_[1], d_[2], d_[3])

        NHEAD = 8
        c_loads(0)
        LAG = 2
        pend = []
        for h in range(NHEAD):
            while pend:
                item_pv(*pend.pop(0))
            if h + 1 < NHEAD:
                c_loads(h + 1)
            for it in attn_items(h):
                cur = item_qk_exp(it)
                pend.append((it,) + cur)
                if len(pend) > LAG:
                    item_pv(*pend.pop(0))
                tick_deferred()
        while pend:
            item_pv(*pend.pop(0))
        tick_deferred(force=True)
        if debug:
            P.dma("sp", lambda e: e.dma_start(out=oda_d.rearrange("h p t -> p h t"), in_=odaT.ap), reads=[odaT],
                  writes=["D:out"])

    if upto >= 5:
        Da = Alloc(DYN + 32)
        Wba = Da.take([128, 8, 1024], BF16)
        Wg0 = Da.take([128, 8, 1024], BF16)
        Wo = Da.take([128, 8, 1024], BF16)
        d_xb = [Da.take([128, 1024], BF16) for _ in range(4)]
        d_xT = Da.take([128, 8, 512], BF16)
        d_xf = [Da.take([128, 1024], F32) for _ in range(2)]
        d_g0 = Da.take([128, 8, 512], BF16)
        d_m12 = Da.take([128, 8, 512], BF16)
        d_mT = Da.take([128, 8, 512], BF16)
        d_r = [Da.take([128, 1024], F32) for _ in range(2)]
        d_lng = Da.take([128, 1024], F32)
        d_lnb = Da.take([128, 1024], F32)
        d_tmp = [Da.take([128, 512], F32) for _ in range(2)]
        d_stats = Da.take([128, 64], F32, align=4)
        load_w(Wg0, w_in_d, 6 * D, D, key="W0")
        load_w(Wba, w_ba_d, 0, D, key="W1")
        load_w(Wo, w_out_d, 0, D, key="W2")
        bcast_load(d_lng, ln1g_d[0:1, :], "lng")
        bcast_load(d_lnb, ln1b_d[0:1, :], "lnb")

        def d1_tile(T):
            X = d_xT
            x_tile(T, xq_d, d_xb, X)
            P.dma("sp", lambda e: e.dma_start(out=d_m12.ap, in_=m12_d[:, :, T * 512:(T + 1) * 512].rearrange("j p t -> p j t")),
                  reads=["D:m12"], writes=[d_m12], key="dm12")
            for j in range(8):
                ps = fm_proj(Wg0, j * 128, X, X.ap, (1, 2, 3))
                ACT(d_g0, d_g0.ap[:, j, :], ps, ps.ap, AF.Sigmoid)
            for j in range(8):
                ps = fm_proj(Wba, j * 128, odaT, odaT.ap[:, :, T * 512:(T + 1) * 512], (1, 2, 3))
                tm = d_tmp[j % 2]
                TT("dve", tm, tm.ap, ps, ps.ap, d_g0, d_g0.ap[:, j, :], ALU.mult)
                TT("dve", d_mT, d_mT.ap[:, j, :], tm, tm.ap, d_m12, d_m12.ap[:, j, :], ALU.add)
            for blk in range(4):
                r0 = T * 512 + blk * 128
                xf = d_xf[blk % 2]
                P.dma("sp", lambda e, xf=xf, r0=r0: e.dma_start(out=xf.ap, in_=xq_d[r0:r0 + 128, :]), writes=[xf],
                      key="dxf%d" % (blk % 2))
                yps = ps_f32(4 + 2 * (blk % 2), 2)
                for c in range(8):
                    for n in range(2):
                        MM(yps, yps.ap[:, n * 512:(n + 1) * 512], d_mT, d_mT.ap[:, c, blk * 128:(blk + 1) * 128],
                           Wo, Wo.ap[:, c, n * 512:(n + 1) * 512], c == 0, c == 7)
                r = d_r[blk % 2]
                STT("dve", r, r.ap, xf, xf.ap, ALPHA, yps, yps.ap, ALU.mult, ALU.add)
                layer_norm(r, r.ap, r, r.ap, d_lng, d_lnb, d_stats, eng2="dve")
                P.dma("sp", lambda e, r=r, r0=r0: e.dma_start(out=x1_d[r0:r0 + 128, :], in_=r.ap), reads=[r],
                      writes=["D:x1"], key="dr%d" % (blk % 2))

        for T in range(8 if upto >= 5 else 0):
            d1_tile(T)

    if upto >= 6:
        Ea = Alloc(DYN)
        Wfi = Ea.take([128, 8, 2 * DFF], BF16)
        Wfo = Ea.take([128, NF, 1024], BF16)
        e_xb = [Ea.take([128, 1024], BF16) for _ in range(4)]
        e_xT = Ea.take([128, 8, 512], BF16)
        e_hT = Ea.take([128, NF, 512], BF16)
        e_sa = [Ea.take([128, 512], F32) for _ in range(2)]
        e_r = [Ea.take([128, 1024], F32) for _ in range(2)]
        e_stats = Ea.take([128, 64], F32, align=4)
        e_lng = Buf(rv(2 * 2048, [128, 1024], F32).ap, rv(2 * 2048, [128, 1024], F32).res)
        e_lnb = Buf(rv(4 * 2048, [128, 1024], F32).ap, rv(4 * 2048, [128, 1024], F32).res)
        load_w(Wfi, w_fi_d, 0, 2 * DFF, key="W3", piece=1408)
        load_w(Wfo, w_fo_d, 0, D, nchunks=NF, key="W4")
        bcast_load(e_lng, ln2g_d[0:1, :], "lng")
        bcast_load(e_lnb, ln2b_d[0:1, :], "lnb")

        def d2_tile(T):
            X = e_xT
            x_tile(T, x1_d, e_xb, X, extra_reads=["D:x1"])
            for f in range(NF):
                pa = ps_f32(1 + 2 * (f % 2))
                pb = ps_f32(2 + 2 * (f % 2))
                for c in range(8):
                    MM(pa, pa.ap, Wfi, Wfi.ap[:, c, f * 128:(f + 1) * 128], X, X.ap[:, c, :], c == 0, c == 7)
                for c in range(8):
                    MM(pb, pb.ap, Wfi, Wfi.ap[:, c, DFF + f * 128:DFF + (f + 1) * 128], X, X.ap[:, c, :], c == 0, c == 7)
                sa_ = e_sa[f % 2]
                ACT(sa_, sa_.ap, pa, pa.ap, AF.Silu)
                TT("dve", e_hT, e_hT.ap[:, f, :], sa_, sa_.ap, pb, pb.ap, ALU.mult)
            for blk in range(4):
                r0 = T * 512 + blk * 128
                r = e_r[blk % 2]
                P.dma("sp", lambda e, r=r, r0=r0: e.dma_start(out=r.ap, in_=x1_d[r0:r0 + 128, :]), reads=["D:x1"],
                      writes=[r])
                yps = ps_f32(5, 2)
                for f in range(NF):
                    for n in range(2):
                        MM(yps, yps.ap[:, n * 512:(n + 1) * 512], e_hT, e_hT.ap[:, f, blk * 128:(blk + 1) * 128],
                           Wfo, Wfo.ap[:, f, n * 512:(n + 1) * 512], f == 0, f == NF - 1)
                STT("dve", r, r.ap, r, r.ap, ALPHA, yps, yps.ap, ALU.mult, ALU.add)
                layer_norm(r, r.ap, r, r.ap, e_lng, e_lnb, e_stats, eng2="dve")
                P.dma("sp", lambda e, r=r, r0=r0: e.dma_start(out=out_d[r0:r0 + 128, :], in_=r.ap), reads=[r],
                      writes=["D:out"], key="er%d" % (blk % 2))

        for T in range(8 if upto >= 6 else 0):
            d2_tile(T)

    P.op("sp", lambda e: e.nop(), reads=["D:out"])
    P.finalize(st)
    with nc.Block() as block:
        P.emit(block)
    st.close()
    return nc, P, st


def host_inputs(inputs):
    x = np.asarray(inputs["x"], dtype=np.float32)
    mem = np.asarray(inputs["mem"], dtype=np.float32)
    pos = np.asarray(inputs["positions"], dtype=np.int32)
    ident = np.eye(128, dtype=np.float32)
    tril = np.tril(np.ones((128, 128), dtype=np.float32))
    tri_kq = np.ascontiguousarray(tril.T)
    ones = np.ones((128, 128), np.float32)
    zeros = np.zeros((128, 128), np.float32)
    invf = (500000.0 ** (-np.arange(8, dtype=np.float32) * 2.0 / 16.0)).astype(np.float32).reshape(1, 8)
    shared = {}
    for k in ("w_in", "w_mem_kv", "w_br_attn", "w_br_sg", "w_br_mem", "w_out", "w_ffn_in", "w_ffn_out"):
        shared[k] = np.ascontiguousarray(np.asarray(inputs[k], dtype=np.float32)[0])
    for k in ("lambda_q1", "lambda_k1", "lambda_q2", "lambda_k2", "da_subln_g", "sg_norm_g", "sg_norm_b",
              "ln1_g", "ln1_b", "ln2_g", "ln2_b"):
        shared[k] = np.ascontiguousarray(np.asarray(inputs[k], dtype=np.float32)[0].reshape(1, -1))
    shared["sg_w_s"] = np.ascontiguousarray(np.asarray(inputs["sg_w_s"], dtype=np.float32)[0])
    shared["sg_b_s"] = np.ascontiguousarray(np.asarray(inputs["sg_b_s"], dtype=np.float32)[0].reshape(1, -1))
    shared["c_ident"] = ident
    shared["c_tril"] = tril
    shared["c_invf"] = invf
    in_maps = []
    for c in range(8):
        b, half = c // 2, c % 2
        blocks = q_blocks(half)
        xb = np.ascontiguousarray(x[b])
        xq = np.ascontiguousarray(x[b].reshape(NBLK, 128, D)[blocks].reshape(NQ, D))
        pb = pos[b].reshape(NBLK, 128)
        posb = np.ascontiguousarray(pb.T)
        posq = np.ascontiguousarray(pb[blocks].T)
        if half == 0:
            mp = np.stack([np.stack([tri_kq, zeros], 1), np.stack([ones, tri_kq], 1)], 1)
        else:
            mp = np.stack([np.stack([ones, tri_kq], 1), np.stack([tri_kq, zeros], 1)], 1)
        m = dict(shared)
        m.update(xb=xb, xq=xq, posb=posb, posq=posq, mem=np.ascontiguousarray(mem[b]),
                 c_maskp=np.ascontiguousarray(mp.reshape(128, 512)))
        in_maps.append(m)
    return in_maps


_CACHE = {}


def kernel(**inputs):
    in_maps = host_inputs(inputs)
    if "nc" not in _CACHE:
        _CACHE["nc"] = build_program(debug=False)[0]
    nc = _CACHE["nc"]
    res = run_bass_kernel_spmd(nc, in_maps, core_ids=list(range(8)))
    out = np.zeros((4, SEQ, D), dtype=np.float32)
    for c in range(8):
        b, half = c // 2, c % 2
        blocks = q_blocks(half)
        o = np.asarray(res.results[c]["out"]).reshape(NSLOT, 128, D)
        out[b].reshape(NBLK, 128, D)[blocks] = o
    return out
```
